# Optimizing a Trainium2 kernel written in Bass

```python
import math
import jax, jax.numpy as jnp
from jax import lax
import numpy as np

D_MODEL = 1024
BATCH = 8
SEQ = 2048
DEPTH = 2
DEC_BATCH = 128
DEC_SEQ = 1
PAST_LEN = 16384
PAGE_SIZE = 128

W_A = D_MODEL
N_BLOCKS_A = 8
BLK_A = W_A // N_BLOCKS_A
CONV_A = 4
LRU_C = 8.0
W_B = D_MODEL // 2
CONV_B = 31
W_C = D_MODEL // 2
GS_C = 16
G_C = W_C // GS_C
P_C = 64
N_BRANCH = 3
IN_W = W_A + 2 * W_B + W_C + N_BRANCH * D_MODEL
D_FF = int(math.ceil(8 * D_MODEL / 3 / 256) * 256)
EPS = 1e-6

kernel_name = 'hybrid_rglru_conformer_s5_step'


def rms_norm(x, g):
    xf = x.astype(jnp.float32)
    y = xf * lax.rsqrt(jnp.mean(xf * xf, axis=-1, keepdims=True) + EPS)
    return (y * g.astype(jnp.float32)).astype(x.dtype)


def layer_norm(x, g, b):
    xf = x.astype(jnp.float32)
    mu = jnp.mean(xf, axis=-1, keepdims=True)
    xc = xf - mu
    y = xc * lax.rsqrt(jnp.mean(xc * xc, axis=-1, keepdims=True) + EPS)
    return (y * g.astype(jnp.float32) + b.astype(jnp.float32)).astype(x.dtype)


def causal_dwconv(u, buf, w, b):
    k = w.shape[0]
    full = jnp.concatenate([buf.astype(u.dtype), u], axis=1)
    y = lax.conv_general_dilated(full, w[:, None, :].astype(u.dtype), window_strides=(1,), padding='VALID',
                                 dimension_numbers=('NWC', 'WIO', 'NWC'), feature_group_count=u.shape[-1])
    return y + b.astype(u.dtype), full[:, -(k - 1):]


def _lin_combine(e1, e2):
    a1, b1 = e1
    a2, b2 = e2
    return a1 * a2, a2 * b1 + b2


def _cplx_combine(e1, e2):
    a1r, a1i, b1r, b1i = e1
    a2r, a2i, b2r, b2i = e2
    return (a2r * a1r - a2i * a1i, a2r * a1i + a2i * a1r,
            a2r * b1r - a2i * b1i + b2r, a2r * b1i + a2i * b1r + b2i)


def rg_lru(u, h0, w_rg, b_rg, w_ig, b_ig, lam_a):
    n, t, w = u.shape
    ub = u.reshape(n, t, N_BLOCKS_A, BLK_A)
    r = jax.nn.sigmoid(jnp.einsum('nthi,hij->nthj', ub, w_rg).reshape(n, t, w).astype(jnp.float32) + b_rg.astype(jnp.float32))
    ig = jax.nn.sigmoid(jnp.einsum('nthi,hij->nthj', ub, w_ig).reshape(n, t, w).astype(jnp.float32) + b_ig.astype(jnp.float32))
    log_a = -LRU_C * r * jax.nn.softplus(-lam_a.astype(jnp.float32))
    a = jnp.exp(log_a)
    bx = jnp.sqrt(-jnp.expm1(2.0 * log_a)) * ig * u.astype(jnp.float32)
    bx = bx.at[:, 0].add(a[:, 0] * h0.astype(jnp.float32))
    _, h = lax.associative_scan(_lin_combine, (a, bx), axis=1)
    return h.astype(u.dtype), h[:, -1]


def s5_ssm(u, s0_re, s0_im, lam_re, lam_im, log_dt, b_re, b_im, c_re, c_im, d_skip):
    n, t, w = u.shape
    uf = u.astype(jnp.float32)
    ug = uf.reshape(n, t, G_C, GS_C)
    dt = jnp.exp(log_dt.astype(jnp.float32))[:, None]
    lr, li = lam_re.astype(jnp.float32), lam_im.astype(jnp.float32)
    mag = jnp.exp(lr * dt)
    ar, ai = mag * jnp.cos(li * dt), mag * jnp.sin(li * dt)
    den = lr * lr + li * li
    qr = ((ar - 1.0) * lr + ai * li) / den
    qi = (ai * lr - (ar - 1.0) * li) / den
    br, bi = b_re.astype(jnp.float32), b_im.astype(jnp.float32)
    bbr = qr[..., None] * br - qi[..., None] * bi
    bbi = qr[..., None] * bi + qi[..., None] * br
    xr = jnp.einsum('ntgc,gpc->ntgp', ug, bbr)
    xi = jnp.einsum('ntgc,gpc->ntgp', ug, bbi)
    s0r, s0i = s0_re.astype(jnp.float32), s0_im.astype(jnp.float32)
    xr = xr.at[:, 0].add(ar * s0r - ai * s0i)
    xi = xi.at[:, 0].add(ar * s0i + ai * s0r)
    arb = jnp.broadcast_to(ar, xr.shape)
    aib = jnp.broadcast_to(ai, xi.shape)
    _, _, sr, si = lax.associative_scan(_cplx_combine, (arb, aib, xr, xi), axis=1)
    y = (jnp.einsum('ntgp,gcp->ntgc', sr, c_re.astype(jnp.float32))
         - jnp.einsum('ntgp,gcp->ntgc', si, c_im.astype(jnp.float32))).reshape(n, t, w)
    y = y + d_skip.astype(jnp.float32) * uf
    return y.astype(u.dtype), sr[:, -1], si[:, -1]


def layer(x, conv_a, h_a, conv_b, s_re, s_im,
          g_mix, w_in, w_conv_a, b_conv_a, w_rg, b_rg, w_ig, b_ig, lam_a,
          w_dw_b, b_dw_b, ln_g_b, ln_b_b,
          lam_re, lam_im, log_dt, b_ssm_re, b_ssm_im, c_ssm_re, c_ssm_im, d_ssm, w_glu_c, b_glu_c,
          b_gate, w_pa, w_pb, w_pc, w_out, g_ffn, w_ffn_in, w_ffn_out):
    h = rms_norm(x, g_mix)
    proj = jnp.einsum('ntd,de->nte', h, w_in)
    o1, o2, o3 = W_A, W_A + 2 * W_B, W_A + 2 * W_B + W_C
    u_a, z_b, u_c, gate_in = proj[..., :o1], proj[..., o1:o2], proj[..., o2:o3], proj[..., o3:]
    c_a, new_conv_a = causal_dwconv(u_a, conv_a, w_conv_a, b_conv_a)
    y_a, new_h = rg_lru(c_a, h_a, w_rg, b_rg, w_ig, b_ig, lam_a)
    glu_b = z_b[..., :W_B] * jax.nn.sigmoid(z_b[..., W_B:])
    c_b, new_conv_b = causal_dwconv(glu_b, conv_b, w_dw_b, b_dw_b)
    y_b = jax.nn.silu(layer_norm(c_b, ln_g_b, ln_b_b))
    y_c, new_s_re, new_s_im = s5_ssm(u_c, s_re, s_im, lam_re, lam_im, log_dt, b_ssm_re, b_ssm_im, c_ssm_re, c_ssm_im, d_ssm)
    y_c = jax.nn.gelu(y_c)
    y_c = y_c * jax.nn.sigmoid(jnp.einsum('ntc,ce->nte', y_c, w_glu_c) + b_glu_c)
    gates = jax.nn.sigmoid(gate_in + b_gate)
    g_a, g_b, g_c = gates[..., :D_MODEL], gates[..., D_MODEL:2 * D_MODEL], gates[..., 2 * D_MODEL:]
    merged = (g_a * jnp.einsum('ntc,cd->ntd', y_a, w_pa)
              + g_b * jnp.einsum('ntc,cd->ntd', y_b, w_pb)
              + g_c * jnp.einsum('ntc,cd->ntd', y_c, w_pc))
    x = x + jnp.einsum('ntd,de->nte', merged, w_out)
    h2 = rms_norm(x, g_ffn)
    gu = jnp.einsum('ntd,df->ntf', h2, w_ffn_in)
    x = x + jnp.einsum('ntf,fd->ntd', jax.nn.silu(gu[..., :D_FF]) * gu[..., D_FF:], w_ffn_out)
    return x, (new_conv_a, new_h.astype(x.dtype), new_conv_b, new_s_re.astype(x.dtype), new_s_im.astype(x.dtype))


def trunk(x, states, layer_params, g_final):
    outs = ([], [], [], [], [])
    for l in range(DEPTH):
        st = [s[l] for s in states]
        x, new = layer(x, *st, *[p[l] for p in layer_params])
        for o, v in zip(outs, new):
            o.append(v)
    stacked = [jnp.stack(o, axis=0) for o in outs]
    return rms_norm(x, g_final), stacked


def setup_inputs(seed: int = 0) -> dict:
    key = jax.random.key(seed)
    ks = iter(jax.random.split(key, 64))
    f32 = jnp.float32
    L = DEPTH

    def nrm(shape, s):
        return jax.random.normal(next(ks), shape, f32) * s

    u0 = jax.random.uniform(next(ks), (L, W_A), f32, minval=0.9, maxval=0.999)
    a_base = u0 ** (1.0 / LRU_C)
    lam_a = jnp.log(a_base) - jnp.log1p(-a_base)
    lam_im = jnp.broadcast_to(jnp.pi * jnp.arange(P_C, dtype=f32), (L, G_C, P_C)) + nrm((L, G_C, P_C), 0.01)
    log_dt = jax.random.uniform(next(ks), (L, G_C), f32, minval=math.log(0.001), maxval=math.log(0.1))
    return {
        'x_prompt': nrm((BATCH, SEQ, D_MODEL), 1.0),
        'x_sample': nrm((DEC_BATCH, DEC_SEQ, D_MODEL), 1.0),
        'state_lru_conv': nrm((L, DEC_BATCH, CONV_A - 1, W_A), 1.0),
        'state_lru_h': nrm((L, DEC_BATCH, W_A), 0.5),
        'state_cfm_conv': nrm((L, DEC_BATCH, CONV_B - 1, W_B), 1.0),
        'state_ssm_re': nrm((L, DEC_BATCH, G_C, P_C), 0.1),
        'state_ssm_im': nrm((L, DEC_BATCH, G_C, P_C), 0.1),
        'g_mix': 1.0 + nrm((L, D_MODEL), 0.02),
        'w_in': nrm((L, D_MODEL, IN_W), D_MODEL ** -0.5),
        'w_conv_a': nrm((L, CONV_A, W_A), CONV_A ** -0.5),
        'b_conv_a': nrm((L, W_A), 0.01),
        'w_rg': nrm((L, N_BLOCKS_A, BLK_A, BLK_A), BLK_A ** -0.5),
        'b_rg': nrm((L, W_A), 0.01),
        'w_ig': nrm((L, N_BLOCKS_A, BLK_A, BLK_A), BLK_A ** -0.5),
        'b_ig': nrm((L, W_A), 0.01),
        'lam_a': lam_a,
        'w_dw_b': nrm((L, CONV_B, W_B), CONV_B ** -0.5),
        'b_dw_b': nrm((L, W_B), 0.01),
        'ln_g_b': 1.0 + nrm((L, W_B), 0.02),
        'ln_b_b': nrm((L, W_B), 0.01),
        'lam_re': -0.5 + nrm((L, G_C, P_C), 0.01),
        'lam_im': lam_im,
        'log_dt': log_dt,
        'b_ssm_re': nrm((L, G_C, P_C, GS_C), (2.0 * GS_C) ** -0.5),
        'b_ssm_im': nrm((L, G_C, P_C, GS_C), (2.0 * GS_C) ** -0.5),
        'c_ssm_re': nrm((L, G_C, GS_C, P_C), P_C ** -0.5),
        'c_ssm_im': nrm((L, G_C, GS_C, P_C), P_C ** -0.5),
        'd_ssm': nrm((L, W_C), 1.0),
        'w_glu_c': nrm((L, W_C, W_C), W_C ** -0.5),
        'b_glu_c': nrm((L, W_C), 0.01),
        'b_gate': nrm((L, N_BRANCH * D_MODEL), 0.01),
        'w_pa': nrm((L, W_A, D_MODEL), W_A ** -0.5),
        'w_pb': nrm((L, W_B, D_MODEL), W_B ** -0.5),
        'w_pc': nrm((L, W_C, D_MODEL), W_C ** -0.5),
        'w_out': nrm((L, D_MODEL, D_MODEL), D_MODEL ** -0.5),
        'g_ffn': 1.0 + nrm((L, D_MODEL), 0.02),
        'w_ffn_in': nrm((L, D_MODEL, 2 * D_FF), D_MODEL ** -0.5),
        'w_ffn_out': nrm((L, D_FF, D_MODEL), D_FF ** -0.5),
        'g_final': 1.0 + nrm((D_MODEL,), 0.02),
    }


def reference(x_prompt, x_sample, state_lru_conv, state_lru_h, state_cfm_conv, state_ssm_re, state_ssm_im,
              g_mix, w_in, w_conv_a, b_conv_a, w_rg, b_rg, w_ig, b_ig, lam_a,
              w_dw_b, b_dw_b, ln_g_b, ln_b_b,
              lam_re, lam_im, log_dt, b_ssm_re, b_ssm_im, c_ssm_re, c_ssm_im, d_ssm, w_glu_c, b_glu_c,
              b_gate, w_pa, w_pb, w_pc, w_out, g_ffn, w_ffn_in, w_ffn_out, g_final):
    layer_params = (g_mix, w_in, w_conv_a, b_conv_a, w_rg, b_rg, w_ig, b_ig, lam_a,
                    w_dw_b, b_dw_b, ln_g_b, ln_b_b,
                    lam_re, lam_im, log_dt, b_ssm_re, b_ssm_im, c_ssm_re, c_ssm_im, d_ssm, w_glu_c, b_glu_c,
                    b_gate, w_pa, w_pb, w_pc, w_out, g_ffn, w_ffn_in, w_ffn_out)
    dt = x_prompt.dtype
    nb = x_prompt.shape[0]
    prompt_states = (jnp.zeros((DEPTH, nb, CONV_A - 1, W_A), dt),
                     jnp.zeros((DEPTH, nb, W_A), dt),
                     jnp.zeros((DEPTH, nb, CONV_B - 1, W_B), dt),
                     jnp.zeros((DEPTH, nb, G_C, P_C), dt),
                     jnp.zeros((DEPTH, nb, G_C, P_C), dt))
    y_prompt, p_new = trunk(x_prompt, prompt_states, layer_params, g_final)
    sample_states = (state_lru_conv, state_lru_h, state_cfm_conv, state_ssm_re, state_ssm_im)
    y_sample, s_new = trunk(x_sample, sample_states, layer_params, g_final)
    p_lru_conv, p_lru_h, p_cfm_conv, p_ssm_re, p_ssm_im = p_new
    s_lru_conv, s_lru_h, s_cfm_conv, s_ssm_re, s_ssm_im = s_new
    return (y_prompt, y_sample, p_lru_conv, p_lru_h, p_cfm_conv, p_ssm_re, p_ssm_im,
            s_lru_conv, s_lru_h, s_cfm_conv, s_ssm_re, s_ssm_im)
```

```python
import contextlib
import numpy as np
import concourse.bass as bass
import concourse.mybir as mybir
from concourse.bass_utils import run_bass_kernel_spmd

F32 = mybir.dt.float32
BF16 = mybir.dt.bfloat16
AF = mybir.ActivationFunctionType
ALU = mybir.AluOpType


class Chan:
    def __init__(self, sem, name):
        self.sem = sem
        self.cnt = 0
        self.name = name


class Sched:
    def __init__(self, nc, es):
        self.nc = nc
        self.es = es
        self.names = ['pe', 'act', 'dve', 'pool', 'sp']
        self.echan = {n: self.chan('e_' + n) for n in self.names}
        self.prog = {n: [] for n in self.names}
        self.seen = {n: {} for n in self.names}
        self.lastw = {}
        self.readers = {}
        self.nins = 0
        self.snap = {}

    def chan(self, name):
        sem = self.es.enter_context(self.nc.semaphore(name))
        return Chan(sem, name)

    def sbuf(self, name, shape, dtype):
        return self.es.enter_context(self.nc.sbuf_tensor('sb_' + name, list(shape), dtype))

    def psum(self, name, shape, dtype):
        return self.es.enter_context(self.nc.psum_tensor('ps_' + name, list(shape), dtype))

    def _deps(self, ename, reads, writes):
        own = self.echan[ename]
        need = {}

        def add(c, v):
            if need.get(c, 0) < v:
                need[c] = v

        for k in reads:
            lw = self.lastw.get(k)
            if lw is not None:
                if lw[0] is own and ename == 'pe':
                    continue
                add(*lw)
        for k in writes:
            lw = self.lastw.get(k)
            if lw is not None:
                if not (lw[0] is own and ename == 'pe'):
                    add(*lw)
            for c, v in self.readers.get(k, {}).items():
                if c is own:
                    continue
                add(c, v)
        out = []
        seen = self.seen[ename]
        items = sorted(need.items(), key=lambda cv: -cv[1])
        for c, v in items:
            if seen.get(c, 0) < v:
                seen[c] = v
                out.append((c.sem, v))
                sn = self.snap.get((c, v))
                if sn is not None and c is not own:
                    for c2, v2 in sn.items():
                        if c2 is not own and seen.get(c2, 0) < v2:
                            seen[c2] = v2
        return out

    def _record(self, reads, writes, c, v):
        for k in writes:
            self.lastw[k] = (c, v)
            self.readers[k] = {}
        for k in reads:
            self.readers.setdefault(k, {})[c] = v

    def barrier_keys(self, src_keys, dst_keys):
        for d in dst_keys:
            rd = self.readers.setdefault(d, {})
            for k in src_keys:
                lw = self.lastw.get(k)
                if lw is not None and rd.get(lw[0], 0) < lw[1]:
                    rd[lw[0]] = lw[1]
                for c, v in self.readers.get(k, {}).items():
                    if rd.get(c, 0) < v:
                        rd[c] = v

    def op(self, ename, fn, reads=(), writes=()):
        waits = self._deps(ename, reads, writes)
        c = self.echan[ename]
        c.cnt += 1
        sem = c.sem

        embed = ename in ('act', 'dve', 'pool') and len(waits) > 0

        def run(e):
            ws = waits[1:] if embed else waits
            for s, v in ws:
                e.wait_ge(s, v)
            ins = fn(e)
            if embed:
                ins._wait_ge(waits[0][0], waits[0][1])
            ins.then_inc(sem, 1)

        self.prog[ename].append(run)
        self._record(reads, writes, c, c.cnt)
        sn = dict(self.seen[ename])
        sn.pop(c, None)
        self.snap[(c, c.cnt)] = sn
        self.nins += 1

    def dma(self, ename, out, in_, chan, reads=(), writes=(), **kw):
        waits = self._deps(ename, reads, writes)
        chan.cnt += 16
        sem = chan.sem

        def run(e):
            for s, v in waits:
                e.wait_ge(s, v)
            e.dma_start(out=out, in_=in_, **kw).then_inc(sem, 16)

        self.prog[ename].append(run)
        self._record(reads, writes, chan, chan.cnt)
        sn = dict(self.seen[ename])
        sn.pop(self.echan[ename], None)
        self.snap[(chan, chan.cnt)] = sn
        self.nins += 1

    def finish(self, chans):
        fw = [(c.sem, c.cnt) for c in chans if c.cnt > 0]

        def run(e):
            for s, v in fw:
                e.wait_ge(s, v)

        self.prog['sp'].append(run)
        prog = self.prog
        with self.nc.Block() as block:
            @block.sync
            def _(e):
                for r in prog['sp']:
                    r(e)

            @block.tensor
            def _(e):
                for r in prog['pe']:
                    r(e)

            @block.scalar
            def _(e):
                for r in prog['act']:
                    r(e)

            @block.vector
            def _(e):
                for r in prog['dve']:
                    r(e)

            @block.gpsimd
            def _(e):
                for r in prog['pool']:
                    r(e)
        self.es.close()


D = 1024
SEQ = 2048
DEPTH = 2
NS = 16
W_B = 512
W_C = 512
IN_W = 5632
D_FF = 2816
O1, O2, O3 = 1024, 2048, 2560
EPS = 1e-6
NPASS = 4
PT = SEQ // NPASS
LCH = 8
NCH = PT // LCH
PI = float(np.pi)

PARAMS = ['g_mix', 'w_in', 'w_conv_a', 'b_conv_a', 'w_rg', 'b_rg', 'w_ig', 'b_ig', 'lam_a',
          'w_dw_b', 'b_dw_b', 'ln_g_b', 'ln_b_b', 'lam_re', 'lam_im', 'log_dt', 'b_ssm_re', 'b_ssm_im',
          'c_ssm_re', 'c_ssm_im', 'd_ssm', 'w_glu_c', 'b_glu_c', 'b_gate', 'w_pa', 'w_pb', 'w_pc',
          'w_out', 'g_ffn', 'w_ffn_in', 'w_ffn_out', 'g_final']


def build_program(shapes):
    nc = bass.Bass("TRN2", target_bir_lowering=False)
    es = contextlib.ExitStack()
    S = Sched(nc, es)
    I = {}
    for k, shp in shapes.items():
        I[k] = nc.dram_tensor(k, list(shp), F32, kind="ExternalInput").ap()
    O = {}

    def outp(name, shp):
        O[name] = nc.dram_tensor(name, list(shp), F32, kind="ExternalOutput").ap()

    outp('y_p', [SEQ, D]); outp('y_s', [NS, D])
    outp('p_lru_conv', [DEPTH, 3, D]); outp('p_lru_h', [DEPTH, D]); outp('p_cfm_conv', [DEPTH, 30, W_B])
    outp('p_ssm_re', [DEPTH, 32, 64]); outp('p_ssm_im', [DEPTH, 32, 64])
    outp('s_lru_conv', [DEPTH, NS, 3, D]); outp('s_lru_h', [DEPTH, NS, D]); outp('s_cfm_conv', [DEPTH, NS, 30, W_B])
    outp('s_ssm_re', [DEPTH, NS, 32, 64]); outp('s_ssm_im', [DEPTH, NS, 32, 64])

    NCM = PT + NS
    CBK = 1024
    NSTG = 4
    sb = S.sbuf
    xres = sb('xres', [128, 8, NCM], F32)
    hbf = sb('hbf', [128, 8, NCM], BF16)
    scr = sb('scr', [128, 24, NCM + 4], BF16)
    UA = scr[:, 16:24, :]
    GLU = sb('GLU', [128, 4, 30 + NCM], BF16)
    UC = sb('UC', [128, 4, NCM], BF16)
    YG = sb('YG', [128, 4, NCM], BF16)
    NTMP = 8
    tmpA = sb('tmpA', [128, NTMP, NCM], F32)
    TMP = [tmpA[:, i, :] for i in range(NTMP)]
    TB = [sb('tb%d' % i, [128, NCM], BF16) for i in range(3)]
    RSLOT = 5120
    NRING = 4
    ring = [sb('ring%d' % i, [128, RSLOT], BF16) for i in range(NRING)]
    ring_ch = [S.chan('ring%d' % i) for i in range(NRING)]
    identf = sb('identf', [128, 128], F32)
    identb = sb('identb', [128, 128], BF16)
    onesb = sb('onesb', [128, 128], BF16)
    sel4 = sb('sel4', [64, NS], BF16)
    sel8 = sb('sel8', [128, NS], BF16)
    gmix = sb('gmix', [128, DEPTH, 8], F32); gffn = sb('gffn', [128, DEPTH, 8], F32); gfin = sb('gfin', [128, 8], F32)
    wca = sb('wca', [128, DEPTH, 4, 8], F32); bca = sb('bca', [128, DEPTH, 8], F32)
    brg = sb('brg', [128, DEPTH, 8], F32); big = sb('big', [128, DEPTH, 8], F32); lama = sb('lama', [128, DEPTH, 8], F32)
    wdb = sb('wdb', [128, DEPTH, 31, 4], F32); bdb = sb('bdb', [128, DEPTH, 4], F32)
    lng = sb('lng', [128, DEPTH, 4], F32); lnb = sb('lnb', [128, DEPTH, 4], F32)
    dss = sb('dss', [128, DEPTH, 4], F32); bglu = sb('bglu', [128, DEPTH, 4], F32); bgate = sb('bgate', [128, DEPTH, 24], F32)
    wrg = sb('wrg', [128, DEPTH, 8, 128], BF16); wig = sb('wig', [128, DEPTH, 8, 128], BF16)
    HST = sb('HST', [128, DEPTH, 8], F32)
    SP2 = sb('SP2', [128, NCH + 1, 32], F32); SC = sb('SC', [128, DEPTH, 2, 16], F32)
    CO2 = sb('CO2', [128, DEPTH, 2, 32], F32)
    QT = sb('QT', [128, 32], F32); QU = sb('QU', [128, 32], F32); QW = sb('QW', [128, 32], F32)
    _sf = scr[:, :, :].rearrange("p a b -> p (a b)")
    stB = _sf[:, 4096:6144].rearrange("p (a b) -> p a b", a=4); wrepB = _sf[:, 6144:8192].rearrange("p (a b) -> p a b", a=4)
    stA = _sf[0:64, 8192:9216]; wrepA = _sf[0:64, 9216:10240]
    prodA = sb('prodA', [64, DEPTH, D], BF16); prodB = sb('prodB', [128, DEPTH, 4, W_B], BF16)
    H0 = sb('H0', [128, DEPTH, 8, NS], F32)
    S0 = sb('S0', [128, DEPTH, 2, 16, NS], F32)
    LR = sb('LR', [128, 16], F32); LI = sb('LI', [128, 16], F32); DT = sb('DT', [128, 16], F32)
    AR = sb('AR', [128, 16], F32); AI = sb('AI', [128, 16], F32); NAI = sb('NAI', [128, 16], F32)
    AR8 = sb('AR8', [128, 16], F32); AI8 = sb('AI8', [128, 16], F32)
    QR = sb('QR', [128, 16], F32); QI = sb('QI', [128, 16], F32)
    P1 = sb('P1', [128, 16], F32); P2 = sb('P2', [128, 16], F32); P3 = sb('P3', [128, 16], F32); P4 = sb('P4', [128, 16], F32); P5 = sb('P5', [128, 16], F32)
    _xf = xres[:, :, :].rearrange("p a b -> p (a b)")
    BRq, BIq, CRq, CIq, BBR, BBI, BT1 = [_xf[:, i_ * 256:(i_ + 1) * 256].rearrange("p (j c) -> p j c", c=16) for i_ in range(7)]
    CQpad = _xf[:, 1792:3840].bitcast(BF16).rearrange("p (j r m) -> p j r m", r=2, m=128)
    BTf = sb('BTf', [128, 2048], F32)
    BT = BTf[:, :].bitcast(BF16).rearrange("p (a b c) -> p a b c", a=4, b=2)
    stgf = [tmpA[:, :, :].rearrange("p a b -> p (a b)")[:, i * CBK:(i + 1) * CBK] for i in range(NSTG)]
    stgb = [scr[:, :, :].rearrange("p a b -> p (a b)")[:, i * CBK:(i + 1) * CBK] for i in range(NSTG)]
    Z2 = BTf[:, 0:32 * NCH].rearrange("p (c k) -> p c k", k=32)
    ZRK = ['ZR']; ZIK = ['ZI']
    S5S = sb('S5S', [128, 4, 2, NCM], BF16)
    BQpad = S5S[:, :, :, :].rearrange("p a b c -> p (a b c)")[:, 0:4096].rearrange("p (r j m) -> p r j m", r=2, j=16)
    BQK = [('S5S', jm) for jm in range(4)]
    dstg = S5S[:, :, :, :].rearrange("p a b c -> p (a b c)")[:, 0:4096]
    XR = TMP[5]; XI = TMP[6]
    XRK = [('t', 5)] + [('XR', jj) for jj in range(LCH)]; XIK = [('t', 6)] + [('XI', jj) for jj in range(LCH)]
    XSs = sb('XSs', [128, DEPTH, 2, 16, NS], F32)
    PWR = sb('PWR', [128, 9, 16], F32); PWI = sb('PWI', [128, 9, 16], F32); NPW = sb('NPW', [128, 9, 16], F32)
    WT = [sb('WT%d' % i, [128, 128], BF16) for i in range(2)]
    WT2 = [sb('WT2%d' % i, [128, 128], BF16) for i in range(2)]
    WTw = [sb('WTw%d' % i, [128, 256], BF16) for i in range(2)]
    WT4 = [sb('WT4%d' % i, [128, 128], BF16) for i in range(4)]
    COEF = sb('COEF', [128, DEPTH, 5, 16], F32)
    CBs = sb('CBs', [128, 4, NS], BF16); CB2s = sb('CB2s', [128, 4, NS], BF16)
    sca = sb('sca', [128, 8], F32)
    tokbuf = sb('tokbuf', [128, D], F32)
    keepA = sb('keepA', [128, DEPTH, 8, 3], F32); keepB = sb('keepB', [128, DEPTH, 4, 30], F32)
    keepAs = sb('keepAs', [128, DEPTH, 8, NS], F32); keepBs = sb('keepBs', [128, DEPTH, 4, NS], F32); keepH = sb('keepH', [128, DEPTH, 8, NS], F32)
    UAh = sb('UAh', [128, DEPTH, 8, 3], BF16); GLUh = sb('GLUh', [128, DEPTH, 4, 30], BF16)
    Q1 = sb('Q1', [128, 16], F32); Q2 = sb('Q2', [128, 16], F32); Q3 = sb('Q3', [128, 16], F32); Q4 = sb('Q4', [128, 16], F32)
    tokout = tokbuf
    NB = 5
    pst1 = S.psum('pst1', [128, 512], F32); pst2 = S.psum('pst2', [128, 512], F32)
    banks = [S.psum('pb%d' % i, [128, 512], F32) for i in range(NB)]
    pbf = S.psum('pbf', [128, 1024], BF16)
    bank_i = [0]

    def nb():
        i = bank_i[0] % NB
        bank_i[0] += 1
        return banks[i], ('ps', i)

    c_const = S.chan('c_const')
    c_out = S.chan('c_out')
    c_in = S.chan('c_in')

    cl = []

    def cload(dst, src, key):
        cl.append((dst, src, key))

    cload(identf[:], I['identf'], 'identf'); cload(sel4[:], None, None) if False else None
    for l in range(DEPTH):
        cload(gmix[:, l, :], I['g_mix'][l].rearrange("(k p) -> p k", p=128), 'par')
        cload(gffn[:, l, :], I['g_ffn'][l].rearrange("(k p) -> p k", p=128), 'par')
        for k in range(4):
            cload(wca[:, l, k, :], I['w_conv_a'][l, k].rearrange("(k p) -> p k", p=128), 'par')
        cload(bca[:, l, :], I['b_conv_a'][l].rearrange("(k p) -> p k", p=128), 'par')
        cload(brg[:, l, :], I['b_rg'][l].rearrange("(k p) -> p k", p=128), 'par')
        cload(big[:, l, :], I['b_ig'][l].rearrange("(k p) -> p k", p=128), 'par')
        cload(lama[:, l, :], I['lam_a'][l].rearrange("(k p) -> p k", p=128), 'par')
        cload(wdb[:, l, :, :], I['w_dw_b'][l].rearrange("t (k p) -> p t k", p=128), 'par')
        cload(bdb[:, l, :], I['b_dw_b'][l].rearrange("(k p) -> p k", p=128), 'par')
        cload(lng[:, l, :], I['ln_g_b'][l].rearrange("(k p) -> p k", p=128), 'par')
        cload(lnb[:, l, :], I['ln_b_b'][l].rearrange("(k p) -> p k", p=128), 'par')
        cload(dss[:, l, :], I['d_ssm'][l].rearrange("(k p) -> p k", p=128), 'par')
        cload(bglu[:, l, :], I['b_glu_c'][l].rearrange("(k p) -> p k", p=128), 'par')
        cload(bgate[:, l, :], I['b_gate'][l].rearrange("(k p) -> p k", p=128), 'par')
    cload(gfin[:], I['g_final'].rearrange("(k p) -> p k", p=128), 'par')
    cl = [c for c in cl if c is not None]
    S.op('pool', lambda e: e.memset(stA[:], 0.0), writes=['stg'])
    S.op('pool', lambda e: e.memset(wrepA[:], 0.0), writes=['stg'])
    S.op('pool', lambda e: e.memset(stB[:], 0.0), writes=['stg'])
    S.op('pool', lambda e: e.memset(wrepB[:], 0.0), writes=['stg'])
    c_csw = S.chan('c_csw')
    CONST = ['const', 'const2']

    def emit_const_loads():
        for dst, src, key in cl:
            if key == 'sw':
                S.dma('pool', dst, src, c_csw, allow_slow_non_contiguous=True)
            else:
                S.dma('sp', dst, src, c_const, allow_slow_non_contiguous=True)
        S.dma('pool', identb[:], I['identf'], c_csw)
        S.dma('pool', onesb[:], I['onesf'], c_csw)
        S.dma('pool', sel4[:], I['sel4f'], c_csw)
        S.dma('pool', sel8[:], I['sel8f'], c_csw)
        for l in range(DEPTH):
            S.dma('pool', wrg[:, l, :, :], I['w_rg'][l].rearrange("h i j -> i h j"), c_csw)
            S.dma('pool', wig[:, l, :, :], I['w_ig'][l].rearrange("h i j -> i h j"), c_csw)
        S.lastw['const'] = (c_const, c_const.cnt)
        S.readers['const'] = {}
        S.lastw['const2'] = (c_csw, c_csw.cnt)
        S.readers['const2'] = {}

    NSL = 58
    wsc = nc.dram_tensor("wsc", [DEPTH, NSL, 128, RSLOT], BF16, kind="Internal").ap()
    c_wt = S.chan('c_wt')
    c_ws = [S.chan('c_ws%d' % i) for i in range(NSTG)]
    c_stg = [S.chan('c_stg%d' % i) for i in range(NSTG)]

    def layer_slabs():
        sl = []
        for s_ in range(5):
            sl.append([(0, 8, 512, 'w_in', s_ * 512)])
        for i_ in range(8):
            sl.append(('WZ', i_))
        sl.append('diagA')
        for c in range(4):
            sl.append(('diagB', c))
        for i_ in range(8):
            sl.append(('CA', i_))
        sl.append('BT')
        sl.append('CQ')
        sl.append([(0, 4, 512, 'w_glu_c', 0)])
        for d in range(8):
            sl.append([(0, 8, 128, 'w_in', O3 + d * 128), (1024, 8, 128, 'w_in', O3 + 1024 + d * 128),
                       (2048, 8, 128, 'w_in', O3 + 2048 + d * 128), (3072, 8, 128, 'w_pa', d * 128),
                       (4096, 4, 128, 'w_pb', d * 128), (4608, 4, 128, 'w_pc', d * 128)])
        for s_ in range(2):
            sl.append([(0, 8, 512, 'w_out', s_ * 512)])
        for s_ in range(11):
            sl.append([(0, 8, 512, 'w_ffn_in', s_ * 512)])
        for d in range(8):
            sl.append([(0, 22, 128, 'w_ffn_out', d * 128)])
        return sl

    LSL = layer_slabs()
    SIDX = {(sp_ if not isinstance(sp_, list) else None): i_ for i_, sp_ in enumerate(LSL)}
    assert len(LSL) == NSL
    SLAB_N = []
    for sp_ in LSL:
        if isinstance(sp_, list):
            SLAB_N.append(max(off + KC * W for (off, KC, W, _, _) in sp_))
        elif sp_ == 'diagB' or (isinstance(sp_, tuple) and sp_[0] == 'diagB'):
            SLAB_N.append(31 * 128)
        elif isinstance(sp_, tuple) and sp_[0] == 'CA':
            SLAB_N.append(5120)
        else:
            SLAB_N.append(4096)
    slabs = [(l, si) for p in range(NPASS) for l in range(DEPTH) for si in range(NSL)]
    rs = {'issued': 0, 'next': 0}

    def issue_slab():
        n = rs['issued']
        if n >= len(slabs):
            return
        slot = n % NRING
        l, si = slabs[n]
        ne = SLAB_N[si]
        S.dma('sp', ring[slot][:, 0:ne], wsc[l, si, :, 0:ne], ring_ch[slot], reads=[('wscL', l, i_) for i_ in range(NSTG + 1)], writes=[('ring', slot)])
        rs['issued'] += 1

    def get_slab(held=0):
        n = rs['next']
        while rs['issued'] < min(n + NRING - held, len(slabs)):
            issue_slab()
        rs['next'] += 1
        slot = n % NRING
        return ring[slot], ('ring', slot)

    def mm(out_ap, pairs, reads, wkey):
        def f(e):
            last = None
            n = len(pairs)
            for i, (a, b) in enumerate(pairs):
                last = e.matmul(out_ap, a, b, start=(i == 0), stop=(i == n - 1))
            return last
        S.op('pe', f, reads=reads, writes=[wkey])

    def act(out, in_, func, reads, writes, **kw):
        S.op('act', lambda e: e.activation(out, in_, func, **kw), reads=reads, writes=writes)

    def dve_tt(out, a, b, op, reads, writes, eng='dve'):
        S.op(eng, lambda e: e.tensor_tensor(out, a, b, op), reads=reads, writes=writes)

    def dve_ts(out, a, s1, s2, op0, op1, reads, writes, eng='dve'):
        if op1 is None:
            S.op(eng, lambda e: e.tensor_scalar(out, a, s1, None, op0), reads=reads, writes=writes)
        else:
            S.op(eng, lambda e: e.tensor_scalar(out, a, s1, s2, op0, op1), reads=reads, writes=writes)

    def dve_stt(out, a, sc_, b, op0, op1, reads, writes):
        S.op('dve', lambda e: e.scalar_tensor_tensor(out, a, sc_, b, op0, op1), reads=reads, writes=writes)

    def dve_cp(out, a, reads, writes, eng='dve'):
        S.op(eng, lambda e: e.tensor_copy(out, a), reads=reads, writes=writes)

    MUL, ADD, SUB = ALU.mult, ALU.add, ALU.subtract

    def rmsnorm(ctiles, gsc, okey):
        for (c0, n) in ctiles:
            ps, pk = nb()
            for k in range(8):
                tb = TB[k % 2]
                act(tb[:, :n], xres[:, k, c0:c0 + n], AF.Square, reads=[('x', k)], writes=[('tb', k % 2)])
                def f(e, k=k, tb=tb, ps=ps, n=n):
                    return e.matmul(ps[:, :n], onesb[:], tb[:, :n], start=(k == 0), stop=(k == 7))
                S.op('pe', f, reads=[('tb', k % 2)] + CONST, writes=[pk] if k == 0 else [])
                S.lastw[pk] = (S.echan['pe'], S.echan['pe'].cnt)
            r = TMP[7]
            act(r[:, :n], ps[:, :n], AF.Sqrt, reads=[pk], writes=['t7'], scale=1.0 / D, bias=epsb[:, 0:1])
            S.op('dve', lambda e, r=r, n=n: e.reciprocal(r[:, :n], r[:, :n]), reads=['t7'], writes=['t7'])
            for k in range(8):
                dve_stt(hbf[:, k, c0:c0 + n], xres[:, k, c0:c0 + n], gsc[:, k:k + 1], r[:, :n], MUL, MUL,
                        reads=[('x', k), 't7'] + CONST, writes=[(okey, k)])

    epsb = sb('epsb', [128, 1], F32)
    oneb = sb('oneb', [128, 1], F32)
    S.op('pool', lambda e: e.memset(epsb[:], EPS), writes=['epsb'])
    S.op('pool', lambda e: e.memset(oneb[:], 1.0), writes=['epsb'])
    CONST.append('epsb')
    S.op('pool', lambda e: e.memset(HST[:], 0.0), writes=['HST'])
    S.op('pool', lambda e: e.memset(SC[:], 0.0), writes=['SC'])
    S.op('pool', lambda e: e.memset(UAh[:], 0.0), writes=['UAh'])
    UAK = [('scr', 16 + k) for k in range(8)]
    S.op('pool', lambda e: e.memset(GLUh[:], 0.0), writes=['GLUh'])

    c_st = S.chan('c_st')
    for l in range(DEPTH):
        first = True
        def sd(dst, src):
            nonlocal first
            S.dma('pool', dst, src, c_st, writes=['stg'] if first else [], allow_slow_non_contiguous=True)
            first = False
        for b in range(NS):
            sd(stA[4 * b:4 * b + 3, :], I['st_lru_conv'][l, b])
            sd(wrepA[4 * b:4 * b + 3, :], I['w_conv_a'][l, 0:3])
            sd(stB[8 * b:8 * b + 7, :, :], I['st_cfm_conv'][l, b, 0:28].rearrange("(kb kk) c -> kb kk c", kk=4))
            sd(stB[8 * b + 7:8 * b + 8, 0:2, :], I['st_cfm_conv'][l, b, 28:30].rearrange("(kb kk) c -> kb kk c", kk=2))
            sd(wrepB[8 * b:8 * b + 7, :, :], I['w_dw_b'][l, 0:28].rearrange("(kb kk) c -> kb kk c", kk=4))
            sd(wrepB[8 * b + 7:8 * b + 8, 0:2, :], I['w_dw_b'][l, 28:30].rearrange("(kb kk) c -> kb kk c", kk=2))
        S.lastw['stg'] = (c_st, c_st.cnt)
        dve_tt(prodA[:, l, :], stA[:], wrepA[:], MUL, ['stg'], ['prodA'])
        for kk in range(4):
            dve_tt(prodB[:, l, kk, :], stB[:, kk, :], wrepB[:, kk, :], MUL, ['stg'], ['prodB'])

    c_s5 = S.chan('c_s5')
    c_tok = S.chan('c_tok')
    c_out2 = S.chan('c_out2')
    NSC = dict(allow_slow_non_contiguous=True)

    def s5_prep(l):
        S.dma('sp', LR[:], I['lam_re'][l].rearrange("(j two) p -> (two p) j", two=2), c_s5, writes=['s5in'], **NSC)
        S.dma('sp', LI[:], I['lam_im'][l].rearrange("(j two) p -> (two p) j", two=2), c_s5, **NSC)
        for h in range(2):
            S.dma('sp', DT[64 * h:64 * h + 64, :], I['log_dt'][l].rearrange("(j two) -> two j", two=2)[h].partition_broadcast(64), c_s5, **NSC)
        S.dma('sp', BRq[:], I['b_ssm_re'][l].rearrange("(j two) p c -> (two p) j c", two=2), c_s5, **NSC)
        S.dma('sp', BIq[:], I['b_ssm_im'][l].rearrange("(j two) p c -> (two p) j c", two=2), c_s5, **NSC)
        for h in range(2):
            for j in range(16):
                S.dma('sp', CRq[64 * h:64 * h + 64, j, :], I['c_ssm_re'][l, 2 * j + h].rearrange("c p -> p c"), c_s5, **NSC)
                S.dma('sp', CIq[64 * h:64 * h + 64, j, :], I['c_ssm_im'][l, 2 * j + h].rearrange("c p -> p c"), c_s5, **NSC)
        S.lastw['s5in'] = (c_s5, c_s5.cnt)
        R = ['s5in']
        W = ['s5p']
        RW = ['s5in', 's5p']
        act(DT[:], DT[:], AF.Exp, reads=R, writes=W)
        dve_tt(P1[:], LR[:], DT[:], MUL, RW, W)
        dve_tt(P2[:], LI[:], DT[:], MUL, RW, W)
        act(P3[:], P1[:], AF.Exp, reads=RW, writes=W)
        MAGIC = 12582912.0
        for shift, dst in ((0.0, AI), (0.25, AR)):
            dve_ts(P4[:], P2[:], 1.0 / (2 * PI), shift, MUL, ADD, RW, W)
            dve_ts(P5[:], P4[:], MAGIC, None, ADD, None, RW, W)
            dve_ts(P5[:], P5[:], -MAGIC, None, ADD, None, RW, W)
            dve_tt(P4[:], P4[:], P5[:], SUB, RW, W)
            act(P4[:], P4[:], AF.Sin, reads=RW, writes=W, scale=6.283185)
            dve_tt(dst[:], P3[:], P4[:], MUL, RW, W)
        dve_ts(NAI[:], AI[:], -1.0, None, MUL, None, RW, W)
        dve_ts(P1[:], AR[:], -1.0, None, ADD, None, RW, W)
        dve_tt(P2[:], LR[:], LR[:], MUL, RW, W)
        dve_tt(P3[:], LI[:], LI[:], MUL, RW, W)
        dve_tt(P2[:], P2[:], P3[:], ADD, RW, W)
        S.op('dve', lambda e: e.reciprocal(P2[:], P2[:]), reads=RW, writes=W)
        dve_tt(P3[:], P1[:], LR[:], MUL, RW, W)
        dve_tt(P4[:], AI[:], LI[:], MUL, RW, W)
        dve_tt(P3[:], P3[:], P4[:], ADD, RW, W)
        dve_tt(QR[:], P3[:], P2[:], MUL, RW, W)
        dve_tt(P3[:], AI[:], LR[:], MUL, RW, W)
        dve_tt(P4[:], P1[:], LI[:], MUL, RW, W)
        dve_tt(P3[:], P3[:], P4[:], SUB, RW, W)
        dve_tt(QI[:], P3[:], P2[:], MUL, RW, W)
        dve_cp(AR8[:], AR[:], RW, W)
        dve_cp(AI8[:], AI[:], RW, W)
        for _ in range(3):
            dve_tt(P1[:], AR8[:], AR8[:], MUL, RW, W)
            dve_tt(P2[:], AI8[:], AI8[:], MUL, RW, W)
            dve_tt(P3[:], AR8[:], AI8[:], MUL, RW, W)
            dve_tt(AR8[:], P1[:], P2[:], SUB, RW, W)
            dve_ts(AI8[:], P3[:], 2.0, None, MUL, None, RW, W)
        QRb = QR[:].unsqueeze(2).broadcast_to([128, 16, 16])
        QIb = QI[:].unsqueeze(2).broadcast_to([128, 16, 16])
        dve_tt(BBR[:], BRq[:], QRb, MUL, RW, W)
        dve_tt(BT1[:], BIq[:], QIb, MUL, RW, W)
        dve_tt(BBR[:], BBR[:], BT1[:], SUB, RW, W)
        dve_tt(BBI[:], BIq[:], QRb, MUL, RW, W)
        dve_tt(BT1[:], BRq[:], QIb, MUL, RW, W)
        dve_tt(BBI[:], BBI[:], BT1[:], ADD, RW, W)
        S.op('pool', lambda e: e.memset(BQpad, 0.0), reads=[], writes=BQK)
        S.op('pool', lambda e: e.memset(CQpad[:, :, :, :], 0.0), reads=[], writes=['CQ'])
        for h in range(2):
            for jm in range(4):
                co = 32 * jm + 16 * h
                ps_ = slice(64 * h, 64 * h + 64)
                dve_cp(BQpad[ps_, 0, jm::4, co:co + 16], BBR[ps_, jm::4, :], RW + BQK, BQK)
                dve_cp(BQpad[ps_, 1, jm::4, co:co + 16], BBI[ps_, jm::4, :], RW + BQK, BQK)
                dve_cp(CQpad[ps_, jm::4, 0, co:co + 16], CRq[ps_, jm::4, :], RW + ['CQ'], ['CQ'])
                dve_ts(CQpad[ps_, jm::4, 1, co:co + 16], CIq[ps_, jm::4, :], -1.0, None, MUL, None, RW + ['CQ'], ['CQ'])
        for c8 in range(4):
            for ri in range(2):
                def f(e, c8=c8, ri=ri):
                    last = None
                    for jm in range(4):
                        last = e.transpose(pbf[:, jm * 128:(jm + 1) * 128], BQpad[:, ri, 4 * c8 + jm, :], identb[:])
                    return last
                S.op('pe', f, reads=BQK + CONST, writes=['pbf'])
                act(BT[:, c8, ri, :], pbf[:, 0:512], AF.Copy, reads=['pbf'], writes=['BT'])
        S.dma('act', wsc[l, SIDX['BT'], :, 0:4096], BT[:, :, :, :].rearrange("p a b c -> p (a b c)"), c_wt, reads=['BT'])
        S.dma('act', wsc[l, SIDX['CQ'], :, 0:4096], CQpad[:, :, :, :].rearrange("p a b c -> p (a b c)"), c_wt, reads=['CQ'])
        for i_, t_ in enumerate([AR, AI, NAI, AR8, AI8]):
            dve_cp(COEF[:, l, i_, :], t_[:], RW, ['COEF'])
        dve_cp(CO2[:, l, 0, 0:16], AR8[:], RW, ['COEF'])
        dve_cp(CO2[:, l, 0, 16:32], AR8[:], RW, ['COEF'])
        dve_ts(CO2[:, l, 1, 0:16], AI8[:], -1.0, None, MUL, None, RW, ['COEF'])
        dve_cp(CO2[:, l, 1, 16:32], AI8[:], RW, ['COEF'])

    MATS = {'w_in': (1024, IN_W), 'w_glu_c': (512, 512), 'w_pa': (1024, 1024), 'w_pb': (512, 1024), 'w_pc': (512, 1024),
            'w_out': (1024, 1024), 'w_ffn_in': (1024, 2 * D_FF), 'w_ffn_out': (D_FF, 1024)}
    cvt = {'i': 0}

    STGK = [('stgf', i) for i in range(NSTG)] + [('stgb', i) for i in range(NSTG)]

    def convert_gen(l):
        index = {m: [] for m in MATS}
        for si, sp_ in enumerate(LSL):
            if isinstance(sp_, list):
                for (off, KC, W, m, col0) in sp_:
                    index[m].append((si, off, KC, W, col0))
        blocks = []
        for m, (K, N) in MATS.items():
            for k in range(K // 128):
                for a in range(0, N, CBK):
                    blocks.append((m, k, a, min(N, a + CBK)))
        nblk_ = len(blocks)
        base = cvt['i']
        cvt['i'] += nblk_

        def load(bi):
            m, k, a, b = blocks[bi]
            i = (base + bi) % NSTG
            S.dma('sp', stgf[i][:, 0:b - a], I[m][l][k * 128:(k + 1) * 128, a:b], c_stg[i], writes=[('stgf', i)])

        PF = NSTG - 1
        for bi in range(min(PF, nblk_)):
            load(bi)
        for bi in range(nblk_):
            if bi + PF < nblk_:
                load(bi + PF)
            m, k, a, b = blocks[bi]
            i = (base + bi) % NSTG
            dve_cp(stgb[i][:, 0:b - a], stgf[i][:, 0:b - a], [('stgf', i)], [('stgb', i)])
            for (si, off, KC, W, col0) in index[m]:
                if k >= KC:
                    continue
                lo, hi = max(a, col0), min(b, col0 + W)
                if lo >= hi:
                    continue
                d0 = off + k * W + (lo - col0)
                S.dma('sp', wsc[l, si, :, d0:d0 + (hi - lo)], stgb[i][:, lo - a:hi - a], c_ws[i], reads=[('stgb', i)])
            yield

    cgen = {'g': None}

    def pump(n):
        g = cgen['g']
        if g is None:
            return
        for _ in range(n):
            try:
                next(g)
            except StopIteration:
                cgen['g'] = None
                return

    def convert(l):
        cgen['g'] = convert_gen(l)
        pump(1 << 30)

    def tables(l, last_of=None):
        for e_ in range(8):
            for k in range(4):
                o_ = (e_ * 4 + k) * 128
                S.op('dve', lambda e, e_=e_, k=k, o_=o_: e.tensor_scalar(dstg[:, o_:o_ + 128], identb[:], wca[:, l, k, e_:e_ + 1], None, MUL),
                     reads=CONST, writes=BQK if ((e_ == 0 and k == 0) or (e_ == 7 and k == 3)) else [])
        S.dma('act', wsc[l, SIDX['diagA'], :, 0:4096], dstg, c_wt, reads=BQK)
        for c in range(4):
            for k in range(31):
                S.op('dve', lambda e, c=c, k=k: e.tensor_scalar(dstg[:, k * 128:(k + 1) * 128], identb[:], wdb[:, l, k, c:c + 1], None, MUL),
                     reads=CONST, writes=BQK if k in (0, 30) else [])
            S.dma('act', wsc[l, SIDX[('diagB', c)], :, 0:31 * 128], dstg[:, 0:31 * 128], c_wt, reads=BQK)
            pump(3)
        s5_prep(l)
        RWp = ['s5p', 'PW']
        S.op('dve', lambda e: e.memset(PWR[:, 0, :], 1.0), reads=[], writes=['PW'])
        S.op('dve', lambda e: e.memset(PWI[:, 0, :], 0.0), reads=['PW'], writes=['PW'])
        for m in range(1, 9):
            dve_tt(P1[:], PWR[:, m - 1, :], AR[:], MUL, RWp, ['s5p'])
            dve_tt(P2[:], PWI[:, m - 1, :], AI[:], MUL, RWp, ['s5p'])
            dve_tt(PWR[:, m, :], P1[:], P2[:], SUB, RWp, ['PW'])
            dve_tt(P1[:], PWR[:, m - 1, :], AI[:], MUL, RWp, ['s5p'])
            dve_tt(P2[:], PWI[:, m - 1, :], AR[:], MUL, RWp, ['s5p'])
            dve_tt(PWI[:, m, :], P1[:], P2[:], ADD, RWp, ['PW'])
        dve_ts(NPW[:, :, :], PWI[:, :, :], -1.0, None, MUL, None, RWp, ['PW'])
        BTflat = BT[:, :, :, :].rearrange("p a b c -> p (a b c)")
        bi_ = 0
        for c8 in range(4):
            for hf in range(2):
                for jm2 in range(2):
                    j = 4 * c8 + 2 * hf + jm2
                    pbs = [nb(), nb()]
                    for m in range(8):
                        pw = 7 - m
                        ww = WTw[bi_ % 2]
                        kw = ('WTw', bi_ % 2)
                        S.op('act', lambda e, ww=ww, j=j, pw=pw: e.activation(ww[:, :].rearrange("p (a b) -> p a b", a=2), BQpad[:, :, j, :], AF.Copy, scale=PWR[:, pw, j:j + 1]),
                             reads=BQK + ['PW'], writes=[kw])
                        for ri in range(2):
                            w2 = WT4[(2 * bi_ + ri) % 4]
                            k2 = ('WT4', (2 * bi_ + ri) % 4)
                            if ri == 0:
                                b_, sb_ = BQpad[:, 1, j, :], NPW[:, pw, j:j + 1]
                            else:
                                b_, sb_ = BQpad[:, 0, j, :], PWI[:, pw, j:j + 1]
                            S.op('dve', lambda e, w2=w2, b_=b_, sb_=sb_, ww=ww, ri=ri: e.scalar_tensor_tensor(w2[:], b_, sb_, ww[:, ri * 128:(ri + 1) * 128], MUL, ADD),
                                 reads=BQK + ['PW', kw], writes=[k2])
                            pb_, pkb = pbs[ri]
                            S.op('pe', lambda e, w2=w2, m=m, pb_=pb_: e.transpose(pb_[:, :].bitcast(BF16)[:, m * 128:(m + 1) * 128], w2[:], identb[:]),
                                 reads=[k2] + CONST, writes=[pkb] if m == 0 else [])
                            S.lastw[pkb] = (S.echan['pe'], S.echan['pe'].cnt)
                        bi_ += 1
                    for ri in range(2):
                        pb_, pkb = pbs[ri]
                        o_ = (jm2 * 2 + ri) * 1024
                        act(BTflat[:, o_:o_ + 1024], pb_[:, :].bitcast(BF16), AF.Copy, reads=[pkb], writes=['BT'])
                    pump(4)
                S.dma('act', wsc[l, SIDX[('WZ', c8 * 2 + hf)], :, 0:4096], BTflat, c_wt, reads=['BT'])
        Kstg = hbf[:, :, :].rearrange("p a b -> p (a b)")[:, 0:4096]
        for c8 in range(4):
            kb = [nb(), nb()]
            def kgroup(m, rhs_fn, c8=c8, kb=kb):
                ps_, pk_ = kb[m // 4]
                def f(e):
                    last = None
                    i_ = 0
                    for jm in range(4):
                        for ri in range(2):
                            last = e.matmul(ps_[:, (m % 4) * 128:(m % 4 + 1) * 128], BQpad[:, ri, 4 * c8 + jm, :], rhs_fn(jm, ri), start=(i_ == 0), stop=(i_ == 7))
                            i_ += 1
                    return last
                S.op('pe', f, reads=BQK + ['CQ', 'BT'], writes=[pk_] if m % 4 == 0 else [])
                S.lastw[pk_] = (S.echan['pe'], S.echan['pe'].cnt)
            kgroup(0, lambda jm, ri, c8=c8: CQpad[:, 4 * c8 + jm, ri, :])
            for ph in range(2):
                for jm in range(4):
                    j = 4 * c8 + jm
                    for pwi in range(4):
                        pw = 4 * ph + pwi + 1
                        ww = WTw[bi_ % 2]
                        k1 = ('WTw', bi_ % 2)
                        bi_ += 1
                        S.op('act', lambda e, ww=ww, j=j, pw=pw: e.activation(ww[:], CQpad[:, j, :, :].rearrange("p a b -> p (a b)"), AF.Copy, scale=PWR[:, pw, j:j + 1]),
                             reads=['CQ', 'PW'], writes=[k1])
                        for ri in range(2):
                            o_ = ((jm * 2 + ri) * 4 + pwi) * 128
                            w1 = ww[:, ri * 128:(ri + 1) * 128]
                            if ri == 0:
                                b_, sb_ = CQpad[:, j, 1, :], PWI[:, pw, j:j + 1]
                            else:
                                b_, sb_ = CQpad[:, j, 0, :], NPW[:, pw, j:j + 1]
                            S.op('dve', lambda e, w1=w1, b_=b_, sb_=sb_, o_=o_: e.scalar_tensor_tensor(BTflat[:, o_:o_ + 128], b_, sb_, w1, MUL, ADD),
                                 reads=['CQ', 'PW', k1], writes=['BT'] if ((jm == 0 and pwi == 0 and ri == 0) or (jm == 3 and pwi == 3 and ri == 1)) else [])
                for pwi in range(4):
                    m = 4 * ph + pwi + 1
                    if m <= 7:
                        kgroup(m, lambda jm, ri, pwi=pwi: BTflat[:, ((jm * 2 + ri) * 4 + pwi) * 128:((jm * 2 + ri) * 4 + pwi + 1) * 128])
                S.dma('act', wsc[l, SIDX[('CA', c8 * 2 + ph)], :, 0:4096], BTflat, c_wt, reads=['BT'])
                pump(8)
            for hb in range(2):
                ps_, pk_ = kb[hb]
                act(Kstg[:, (c8 * 8 + hb * 4) * 128:(c8 * 8 + hb * 4 + 4) * 128], ps_[:, :], AF.Copy, reads=[pk_], writes=['Kstg'])
            for ph in range(2):
                S.dma('act', wsc[l, SIDX[('CA', c8 * 2 + ph)], :, 4096:5120], Kstg[:, c8 * 1024:(c8 + 1) * 1024], c_wt, reads=['Kstg'])
        S.barrier_keys(['Kstg'], HK)
        pump(1 << 30)
        for i_, ch_ in enumerate([c_wt] + c_ws):
            S.lastw[('wscL', l, i_)] = (ch_, ch_.cnt)
            S.readers[('wscL', l, i_)] = {}

    tokB = [prodB[:, :, :, :].rearrange("p a b c -> p (a b c)")[:, i_ * 2048:(i_ + 1) * 2048].bitcast(F32) for i_ in range(2)]
    c_inB = [S.chan('c_inB%d' % i_) for i_ in range(2)]

    def x_dma(p, tt):
        t0 = p * PT + tt * 128
        S.dma('sp', tokB[tt % 2], I['xp'][t0:t0 + 128, :], c_inB[tt % 2], writes=[('tokB', tt % 2)])

    def x_tr(p, tt):
        tb_ = tokB[tt % 2]
        for half in range(2):
            ps, pk = nb()
            def f(e, ps=ps, half=half, tb_=tb_):
                last = None
                for kk in range(4):
                    k = half * 4 + kk
                    last = e.transpose(ps[:, kk * 128:(kk + 1) * 128], tb_[:, k * 128:(k + 1) * 128], identf[:])
                return last
            S.op('pe', f, reads=[('tokB', tt % 2)] + CONST, writes=[pk])
            act(xres[:, half * 4:half * 4 + 4, tt * 128:(tt + 1) * 128], ps[:, :].rearrange("p (k t) -> p k t", t=128), AF.Copy,
                reads=[pk], writes=[('xt', half * 4 + kk, tt) for kk in range(4)])
            if tt == 3:
                for kk in range(4):
                    S.lastw[('x', half * 4 + kk)] = S.lastw[('xt', half * 4 + kk, 3)]

    def x_load(p):
        for tt in range(4):
            t0 = p * PT + tt * 128
            S.dma('sp', tokbuf[:], I['xp'][t0:t0 + 128, :], c_in, writes=['tokbuf'])
            for half in range(2):
                ps, pk = nb()
                def f(e, ps=ps, half=half):
                    last = None
                    for kk in range(4):
                        k = half * 4 + kk
                        last = e.transpose(ps[:, kk * 128:(kk + 1) * 128], tokbuf[:, k * 128:(k + 1) * 128], identf[:])
                    return last
                S.op('pe', f, reads=['tokbuf'] + CONST, writes=[pk])
                act(xres[:, half * 4:half * 4 + 4, tt * 128:(tt + 1) * 128], ps[:, :].rearrange("p (k t) -> p k t", t=128), AF.Copy,
                    reads=[pk], writes=[('x', half * 4 + kk) for kk in range(4)])
        if p == 0:
            S.dma('sp', tokbuf[0:NS, :], I['xs'], c_in, writes=['tokbuf'])
            for half in range(2):
                ps, pk = nb()
                def f(e, ps=ps, half=half):
                    last = None
                    for kk in range(4):
                        k = half * 4 + kk
                        last = e.transpose(ps[:, kk * NS:(kk + 1) * NS], tokbuf[0:NS, k * 128:(k + 1) * 128], identf[0:NS, 0:NS])
                    return last
                S.op('pe', f, reads=['tokbuf'] + CONST, writes=[pk])
                act(xres[:, half * 4:half * 4 + 4, PT:PT + NS], ps[:, 0:4 * NS].rearrange("p (k t) -> p k t", t=NS), AF.Copy,
                    reads=[pk], writes=[('x', half * 4 + kk) for kk in range(4)])

    HK = [('h', k) for k in range(8)]

    def layer(p, l):
        last = (p == NPASS - 1)
        ctiles = [(0, PT)] + ([(PT, NS)] if p == 0 else [])
        NC = PT + (NS if p == 0 else 0)
        rmsnorm(ctiles, gmix[:, l, :], 'h')
        dve_cp(UA[:, :, 0:3], UAh[:, l, :, :], ['UAh'], UAK, eng='pool')
        dve_cp(GLU[:, :, 0:30], GLUh[:, l, :, :], ['GLUh'], ['GLU'], eng='pool')
        for s_ in range(5):
            slab, rk = get_slab()
            for c4 in range(4):
                e_ = s_ * 4 + c4
                for (c0, n) in ctiles:
                    ps, pk = nb()
                    mm(ps[:, :n], [(slab[:, k * 512 + c4 * 128:k * 512 + c4 * 128 + 128], hbf[:, k, c0:c0 + n]) for k in range(8)],
                       reads=[rk] + HK, wkey=pk)
                    if e_ < 8:
                        act(UA[:, e_, 3 + c0:3 + c0 + n], ps[:, :n], AF.Copy, reads=[pk], writes=[('scr', 16 + e_)])
                        if c0 == PT:
                            act(keepAs[:, l, e_, :], ps[:, :n], AF.Copy, reads=[pk], writes=['keepAs'])
                        elif last:
                            act(keepA[:, l, e_, :], ps[:, PT - 3:PT], AF.Copy, reads=[pk], writes=['keepA'])
                    elif e_ < 12:
                        act(TMP[e_ - 8][:, c0:c0 + n], ps[:, :n], AF.Copy, reads=[pk], writes=[('t', e_ - 8)])
                    elif e_ < 16:
                        c = e_ - 12
                        act(TMP[4][:, c0:c0 + n], ps[:, :n], AF.Sigmoid, reads=[pk], writes=[('t', 4)])
                        dve_tt(GLU[:, c, 30 + c0:30 + c0 + n], TMP[c][:, c0:c0 + n], TMP[4][:, c0:c0 + n], MUL,
                               [('t', c), ('t', 4)], ['GLU'])
                        if c0 == PT:
                            dve_tt(keepBs[:, l, c, :], TMP[c][:, c0:c0 + n], TMP[4][:, c0:c0 + n], MUL, [('t', c), ('t', 4)], ['keepBs'])
                        elif last:
                            dve_tt(keepB[:, l, c, :], TMP[c][:, PT - 30:PT], TMP[4][:, PT - 30:PT], MUL, [('t', c), ('t', 4)], ['keepB'])
                    else:
                        act(UC[:, e_ - 16, c0:c0 + n], ps[:, :n], AF.Copy, reads=[pk], writes=['UC'])
        dve_cp(UAh[:, l, :, :], UA[:, :, PT:PT + 3], UAK, ['UAh'], eng='pool')
        dve_cp(GLUh[:, l, :, :], GLU[:, :, PT:PT + 30], ['GLU'], ['GLUh'], eng='pool')
        BTs = lambda c8, ri, jm: slabT[:, (c8 * 2 + ri) * 512 + jm * 128:(c8 * 2 + ri) * 512 + (jm + 1) * 128]
        CQs = lambda j, ri: slabQ[:, (j * 2 + ri) * 128:(j * 2 + ri + 1) * 128]
        cAR = lambda j: COEF[:, l, 0, j:j + 1]
        cAI = lambda j: COEF[:, l, 1, j:j + 1]
        cNAI = lambda j: COEF[:, l, 2, j:j + 1]
        AR8l = COEF[:, l, 3, :]
        AI8l = COEF[:, l, 4, :]
        SP_ = ['SPr', 'SPi']
        dve_cp(SP2[:, 0, :], SC[:, l, :, :].rearrange("p a b -> p (a b)"), ['SC'], ['SPr', 'SPi', ('S2', 0)])

        def xcompute(j):
            c8, jm = j // 4, j % 4
            psr, pkr = nb()
            mm(psr[:, :PT], [(BTs(c8, 0, jm), UC[:, c8, 0:PT])], reads=[rkT, 'UC'], wkey=pkr)
            psi, pki = nb()
            mm(psi[:, :PT], [(BTs(c8, 1, jm), UC[:, c8, 0:PT])], reads=[rkT, 'UC'], wkey=pki)
            act(XR[:, :PT], psr[:, :PT], AF.Copy, reads=[pkr], writes=XRK)
            act(XI[:, :PT], psi[:, :PT], AF.Copy, reads=[pki], writes=XIK)

        def horner(j):
            C_ = ['COEF']
            for jj in range(1, LCH):
                cr, ci = XR[:, jj:PT:LCH], XI[:, jj:PT:LCH]
                pr, pi_ = XR[:, jj - 1:PT:LCH], XI[:, jj - 1:PT:LCH]
                kr, ki, kpr, kpi = ('XR', jj), ('XI', jj), ('XR', jj - 1), ('XI', jj - 1)
                dve_stt(cr, pr, cAR(j), cr, MUL, ADD, C_ + [kpr, kr], [kr])
                dve_stt(ci, pi_, cAR(j), ci, MUL, ADD, C_ + [kpi, ki], [ki])
                dve_stt(cr, pi_, cNAI(j), cr, MUL, ADD, C_ + [kpi, kr], [kr])
                dve_stt(ci, pr, cAI(j), ci, MUL, ADD, C_ + [kpr, ki], [ki])

        for c8 in range(4):
            for hf in range(2):
                slabW, rkW = get_slab()
                for jm2 in range(2):
                    j = 4 * c8 + 2 * hf + jm2
                    for ri in range(2):
                        ps, pk = nb()
                        bb = (jm2 * 2 + ri) * 8
                        mm(ps[:, :NCH], [(slabW[:, (bb + m) * 128:(bb + m + 1) * 128], UC[:, c8, m:PT:LCH]) for m in range(LCH)],
                           reads=[rkW, 'UC'], wkey=pk)
                        act(Z2[:, :, ri * 16 + j], ps[:, :NCH], AF.Copy, reads=[pk], writes=(ZRK if ri == 0 else ZIK))
        def chunk_gen():
            CA2, CB2 = CO2[:, l, 0, :], CO2[:, l, 1, :]
            for c in range(NCH):
                cur = SP2[:, c, :]
                kc = ('S2', c)
                fin = (c == NCH - 1)
                dve_tt(QT[:], cur, CA2, MUL, ['COEF', kc], ['QT'])
                dve_tt(QU[:, 0:16], SP2[:, c, 16:32], CB2[:, 0:16], MUL, ['COEF', kc], ['QU0'])
                dve_tt(QU[:, 16:32], SP2[:, c, 0:16], CB2[:, 16:32], MUL, ['COEF', kc], ['QU1'])
                dve_tt(QW[:], QT[:], Z2[:, c, :], ADD, ['QT'] + ZRK + ZIK, ['QW'])
                dve_tt(SP2[:, c + 1, :], QW[:], QU[:], ADD, ['QW', 'QU0', 'QU1'], [('S2', c + 1)] + (['SPr', 'SPi'] if fin else []))
                yield
        cg = chunk_gen()

        def cpump(n):
            for _ in range(n):
                try:
                    next(cg)
                except StopIteration:
                    return
        act(sca[:], lama[:, l, :], AF.Exp, reads=CONST, writes=['sca'], scale=-1.0)
        act(sca[:], sca[:], AF.Ln, reads=['sca'], writes=['sca'], bias=oneb[:, 0:1])
        dve_ts(sca[:], sca[:], -8.0, None, MUL, None, ['sca'], ['sca'])
        slabA, rkA = get_slab()
        for e_ in range(8):
            dA = lambda k, e_=e_: slabA[:, (e_ * 4 + k) * 128:(e_ * 4 + k + 1) * 128]
            si_ = e_ % 2
            T0, T1, T2, T3 = TMP[4 * si_], TMP[4 * si_ + 1], TMP[4 * si_ + 2], TMP[4 * si_ + 3]
            k0, k1_, k2_, k3_ = ('t', 4 * si_), ('t', 4 * si_ + 1), ('t', 4 * si_ + 2), ('t', 4 * si_ + 3)
            TBe = TB[2] if si_ == 0 else TB[0]
            kb_ = ('tb', 2) if si_ == 0 else ('tb', 0)
            for (c0, n) in ctiles:
                ps, pk = nb()
                if c0 == 0:
                    mm(ps[:, :n], [(dA(k), UA[:, e_, k:k + n]) for k in range(4)], reads=[rkA, ('scr', 16 + e_)], wkey=pk)
                else:
                    mm(ps[:, :n], [(dA(3), UA[:, e_, 3 + c0:3 + c0 + n]),
                                   (prodA[:, l, e_ * 128:(e_ + 1) * 128], sel4[:, :])], reads=[rkA, ('scr', 16 + e_), 'prodA'] + CONST, wkey=pk)
                act(TBe[:, c0:c0 + n], ps[:, :n], AF.Identity, reads=[pk] + CONST, writes=[kb_], bias=bca[:, l, e_:e_ + 1])
                act(T0[:, c0:c0 + n], ps[:, :n], AF.Identity, reads=[pk] + CONST, writes=[k0], bias=bca[:, l, e_:e_ + 1])
                ps2, pk2 = nb()
                mm(ps2[:, :n], [(wrg[:, l, e_, :], TBe[:, c0:c0 + n])], reads=[kb_] + CONST, wkey=pk2)
                ps3, pk3 = nb()
                mm(ps3[:, :n], [(wig[:, l, e_, :], TBe[:, c0:c0 + n])], reads=[kb_] + CONST, wkey=pk3)
                act(T1[:, c0:c0 + n], ps2[:, :n], AF.Sigmoid, reads=[pk2] + CONST, writes=[k1_], bias=brg[:, l, e_:e_ + 1])
                act(T2[:, c0:c0 + n], ps3[:, :n], AF.Sigmoid, reads=[pk3] + CONST, writes=[k2_], bias=big[:, l, e_:e_ + 1])
            act(T1[:, :NC], T1[:, :NC], AF.Exp, reads=[k1_, 'sca'], writes=[k1_], scale=sca[:, e_:e_ + 1])
            dve_tt(T2[:, :NC], T2[:, :NC], T0[:, :NC], MUL, [k2_, k0], [k2_])
            dve_tt(T3[:, :NC], T1[:, :NC], T1[:, :NC], MUL, [k1_], [k3_])
            act(T3[:, :NC], T3[:, :NC], AF.Sqrt, reads=[k3_] + CONST, writes=[k3_], scale=-1.0, bias=oneb[:, 0:1])
            dve_tt(T2[:, :NC], T2[:, :NC], T3[:, :NC], MUL, [k2_, k3_], [k2_])
            S.op('dve', lambda e, e_=e_, T0=T0, T1=T1, T2=T2: e.tensor_tensor_scan(T0[:, 0:PT], T1[:, 0:PT], T2[:, 0:PT], HST[:, l, e_:e_ + 1], MUL, ADD),
                 reads=[k1_, k2_, 'HST', k0], writes=[k0])
            dve_cp(HST[:, l, e_:e_ + 1], T0[:, PT - 1:PT], [k0], ['HST'])
            if p == 0:
                dve_tt(T0[:, PT:NC], T1[:, PT:NC], H0[:, l, e_, :], MUL, [k1_, k0] + CONST, [k0])
                dve_tt(T0[:, PT:NC], T0[:, PT:NC], T2[:, PT:NC], ADD, [k0, k2_], [k0])
                dve_cp(keepH[:, l, e_, :], T0[:, PT:NC], [k0], ['keepH'])
            dve_cp(scr[:, e_, :NC], T0[:, :NC], [k0], [('scr', e_)], eng='pool')
            cpump(NCH // 8)
        for c in range(4):
            slabB, rkB = get_slab()
            dB = lambda k: slabB[:, k * 128:(k + 1) * 128]
            for (c0, n) in ctiles:
                ps, pk = nb()
                if c0 == 0:
                    mm(ps[:, :n], [(dB(k), GLU[:, c, k:k + n]) for k in range(31)], reads=[rkB, 'GLU'], wkey=pk)
                else:
                    mm(ps[:, :n], [(dB(30), GLU[:, c, 30 + c0:30 + c0 + n])] +
                       [(prodB[:, l, kk, c * 128:(c + 1) * 128], sel8[:, :]) for kk in range(4)], reads=[rkB, 'GLU', 'prodB'] + CONST, wkey=pk)
                act(TMP[c][:, c0:c0 + n], ps[:, :n], AF.Identity, reads=[pk] + CONST, writes=[('t', c)], bias=bdb[:, l, c:c + 1])
                if c0 == 0:
                    act(TB[0][:, :n], ps[:, :n], AF.Identity, reads=[pk] + CONST, writes=[('tb', 0)], bias=bdb[:, l, c:c + 1])
                    act(TB[1][:, :n], ps[:, :n], AF.Square, reads=[pk] + CONST, writes=[('tb', 1)], bias=bdb[:, l, c:c + 1])
                    S.op('pe', lambda e, c=c, n=n: e.matmul(pst1[:, :n], onesb[:], TB[0][:, :n], start=(c == 0), stop=(c == 3)),
                         reads=[('tb', 0)] + CONST, writes=['pst1'] if c == 0 else [])
                    S.lastw['pst1'] = (S.echan['pe'], S.echan['pe'].cnt)
                    S.op('pe', lambda e, c=c, n=n: e.matmul(pst2[:, :n], onesb[:], TB[1][:, :n], start=(c == 0), stop=(c == 3)),
                         reads=[('tb', 1)] + CONST, writes=['pst2'] if c == 0 else [])
                    S.lastw['pst2'] = (S.echan['pe'], S.echan['pe'].cnt)
                else:
                    act(CBs[:, c, :], ps[:, :n], AF.Identity, reads=[pk] + CONST, writes=['CBs'], bias=bdb[:, l, c:c + 1])
                    act(CB2s[:, c, :], ps[:, :n], AF.Square, reads=[pk] + CONST, writes=['CBs'], bias=bdb[:, l, c:c + 1])
        for (c0, n) in ctiles:
            if c0 == 0:
                s1, k1, s2, k2 = pst1, 'pst1', pst2, 'pst2'
            else:
                s1, k1 = nb()
                mm(s1[:, :n], [(onesb[:], CBs[:, c, :]) for c in range(4)], reads=['CBs'] + CONST, wkey=k1)
                s2, k2 = nb()
                mm(s2[:, :n], [(onesb[:], CB2s[:, c, :]) for c in range(4)], reads=['CBs'] + CONST, wkey=k2)
            act(TMP[4][:, :n], s1[:, :n], AF.Copy, reads=[k1], writes=[('t', 4)], scale=1.0 / W_B)
            dve_tt(TMP[5][:, :n], TMP[4][:, :n], TMP[4][:, :n], MUL, [('t', 4)], [('t', 5)])
            dve_stt(TMP[5][:, :n], s2[:, :n], 1.0 / W_B, TMP[5][:, :n], MUL, SUB, [k2, ('t', 5)], [('t', 5)])
            act(TMP[5][:, :n], TMP[5][:, :n], AF.Sqrt, reads=[('t', 5)] + CONST, writes=[('t', 5)], bias=epsb[:, 0:1])
            S.op('dve', lambda e, n=n: e.reciprocal(TMP[5][:, :n], TMP[5][:, :n]), reads=[('t', 5)], writes=[('t', 5)])
            for c in range(4):
                dve_tt(TMP[6][:, :n], TMP[c][:, c0:c0 + n], TMP[4][:, :n], SUB, [('t', c), ('t', 4)], [('t', 6)])
                dve_tt(TMP[6][:, :n], TMP[6][:, :n], TMP[5][:, :n], MUL, [('t', 6), ('t', 5)], [('t', 6)])
                act(scr[:, 8 + c, c0:c0 + n], TMP[6][:, :n], AF.Silu, reads=[('t', 6)] + CONST, writes=[('scr', 8 + c)],
                    scale=lng[:, l, c:c + 1], bias=lnb[:, l, c:c + 1])
        cpump(NCH)
        dve_cp(SC[:, l, :, :].rearrange("p a b -> p (a b)"), SP2[:, NCH, :], SP_, ['SC'])
        SPb = S5S[:, :, :, :].rearrange("p a b c -> p (a b c)")[:, 0:2 * 16 * NCH].rearrange("p (j r c) -> p j r c", r=2, c=NCH)
        XSb = S5S[:, :, :, :].rearrange("p a b c -> p (a b c)")[:, 2048:2048 + 32 * NS].rearrange("p (j r c) -> p j r c", r=2, c=NS)
        act(SPb[:, :, 0, :], SP2[:, 0:NCH, 0:16].rearrange("p c j -> p j c"), AF.Copy, reads=SP_, writes=BQK)
        act(SPb[:, :, 1, :], SP2[:, 0:NCH, 16:32].rearrange("p c j -> p j c"), AF.Copy, reads=SP_, writes=BQK)
        for c8 in range(4):
            ps, pk = nb()
            for ph in range(2):
                slabC, rkC = get_slab()
                for pwi in range(4):
                    jp = 4 * ph + pwi
                    pairs = [(slabC[:, ((jm * 2 + ri) * 4 + pwi) * 128:((jm * 2 + ri) * 4 + pwi + 1) * 128], SPb[:, 4 * c8 + jm, ri, :])
                             for jm in range(4) for ri in range(2)]
                    pairs += [(slabC[:, 4096 + m * 128:4096 + (m + 1) * 128], UC[:, c8, (jp - m):PT:LCH]) for m in range(jp + 1)]
                    def f(e, pairs=pairs, jp=jp, ps=ps):
                        last = None
                        for i_, (a_, b_) in enumerate(pairs):
                            last = e.matmul(ps[:, jp:PT:LCH], a_, b_, start=(i_ == 0), stop=(i_ == len(pairs) - 1))
                        return last
                    S.op('pe', f, reads=[rkC, 'UC'] + BQK, writes=[pk] if jp == 0 else [])
                    S.lastw[pk] = (S.echan['pe'], S.echan['pe'].cnt)
            dve_stt(TMP[0][:, :PT], UC[:, c8, 0:PT], dss[:, l, c8:c8 + 1], ps[:, :PT], MUL, ADD, ['UC', pk] + CONST, [('t', 0)])
            act(YG[:, c8, 0:PT], TMP[0][:, :PT], AF.Gelu, reads=[('t', 0)], writes=['YG'])
        slabT, rkT = get_slab()
        slabQ, rkQ = get_slab(held=1)
        if p == 0:
            for c8 in range(4):
                for jm in range(4):
                    j = 4 * c8 + jm
                    psr, pkr = nb()
                    mm(psr[:, :NS], [(BTs(c8, 0, jm), UC[:, c8, PT:PT + NS])], reads=[rkT, 'UC'], wkey=pkr)
                    psi, pki = nb()
                    mm(psi[:, :NS], [(BTs(c8, 1, jm), UC[:, c8, PT:PT + NS])], reads=[rkT, 'UC'], wkey=pki)
                    xs_r, xs_i = XSs[:, l, 0, j, :], XSs[:, l, 1, j, :]
                    RS = CONST + ['COEF', 'XSs']
                    dve_stt(xs_r, S0[:, l, 0, j, :], cAR(j), psr[:, :NS], MUL, ADD, RS + [pkr], ['XSs'])
                    dve_stt(xs_r, S0[:, l, 1, j, :], cNAI(j), xs_r, MUL, ADD, RS, ['XSs'])
                    dve_stt(xs_i, S0[:, l, 1, j, :], cAR(j), psi[:, :NS], MUL, ADD, RS + [pki], ['XSs'])
                    dve_stt(xs_i, S0[:, l, 0, j, :], cAI(j), xs_i, MUL, ADD, RS, ['XSs'])
                    act(XSb[:, j, 0, :], xs_r, AF.Copy, reads=['XSs'], writes=['XSb'])
                    act(XSb[:, j, 1, :], xs_i, AF.Copy, reads=['XSs'], writes=['XSb'])
                ps, pk = nb()
                mm(ps[:, :NS], [(CQs(4 * c8 + jm, ri), XSb[:, 4 * c8 + jm, ri, :]) for jm in range(4) for ri in range(2)],
                   reads=[rkQ, 'XSb'], wkey=pk)
                dve_stt(TMP[0][:, :NS], UC[:, c8, PT:PT + NS], dss[:, l, c8:c8 + 1], ps[:, :NS], MUL, ADD, ['UC', pk] + CONST, [('t', 0)])
                act(YG[:, c8, PT:PT + NS], TMP[0][:, :NS], AF.Gelu, reads=[('t', 0)], writes=['YG'])
        slab, rk = get_slab()
        for c in range(4):
            for (c0, n) in ctiles:
                ps, pk = nb()
                mm(ps[:, :n], [(slab[:, k * 512 + c * 128:k * 512 + c * 128 + 128], YG[:, k, c0:c0 + n]) for k in range(4)], reads=[rk, 'YG'], wkey=pk)
                act(TMP[1][:, :n], ps[:, :n], AF.Sigmoid, reads=[pk] + CONST, writes=[('t', 1)], bias=bglu[:, l, c:c + 1])
                dve_tt(scr[:, 12 + c, c0:c0 + n], YG[:, c, c0:c0 + n], TMP[1][:, :n], MUL, ['YG', ('t', 1)], [('scr', 12 + c)])
        for d in range(8):
            slab, rk = get_slab()
            for (c0, n) in ctiles:
                specs = [(0, 3072, 8, 0, 0), (1024, 4096, 4, 8, 8), (2048, 4608, 4, 12, 16)]
                for bi_, (goff, poff, KC, sbase, gb) in enumerate(specs):
                    psg, pkg = nb()
                    mm(psg[:, :n], [(slab[:, goff + k * 128:goff + (k + 1) * 128], hbf[:, k, c0:c0 + n]) for k in range(8)], reads=[rk] + HK, wkey=pkg)
                    gt = TMP[0] if bi_ == 0 else TMP[2]
                    gk = ('t', 0) if bi_ == 0 else ('t', 2)
                    act(gt[:, :n], psg[:, :n], AF.Sigmoid, reads=[pkg] + CONST, writes=[gk], bias=bgate[:, l, gb + d:gb + d + 1])
                    psp, pkp = nb()
                    mm(psp[:, :n], [(slab[:, poff + k * 128:poff + (k + 1) * 128], scr[:, sbase + k, c0:c0 + n]) for k in range(KC)],
                       reads=[rk] + [('scr', sbase + k) for k in range(KC)], wkey=pkp)
                    if bi_ == 0:
                        dve_tt(TMP[1][:, :n], TMP[0][:, :n], psp[:, :n], MUL, [gk, pkp], [('t', 1)])
                    else:
                        dve_tt(TMP[3][:, :n], TMP[2][:, :n], psp[:, :n], MUL, [gk, pkp], [('t', 3)])
                        if bi_ == 1:
                            dve_tt(TMP[1][:, :n], TMP[1][:, :n], TMP[3][:, :n], ADD, [('t', 1), ('t', 3)], [('t', 1)])
                        else:
                            dve_tt(scr[:, 16 + d, c0:c0 + n], TMP[1][:, :n], TMP[3][:, :n], ADD, [('t', 1), ('t', 3)], [('scr', 16 + d)])
        for s_ in range(2):
            slab, rk = get_slab()
            for c4 in range(4):
                d = s_ * 4 + c4
                for (c0, n) in ctiles:
                    ps, pk = nb()
                    mm(ps[:, :n], [(slab[:, k * 512 + c4 * 128:k * 512 + c4 * 128 + 128], scr[:, 16 + k, c0:c0 + n]) for k in range(8)],
                       reads=[rk] + [('scr', 16 + k) for k in range(8)], wkey=pk)
                    dve_tt(xres[:, d, c0:c0 + n], xres[:, d, c0:c0 + n], ps[:, :n], ADD, [('x', d), pk], [('x', d)])
        rmsnorm(ctiles, gffn[:, l, :], 'h')
        for s_ in range(11):
            slab, rk = get_slab()
            for c4 in range(4):
                ci_ = s_ * 4 + c4
                for (c0, n) in ctiles:
                    ps, pk = nb()
                    mm(ps[:, :n], [(slab[:, k * 512 + c4 * 128:k * 512 + c4 * 128 + 128], hbf[:, k, c0:c0 + n]) for k in range(8)],
                       reads=[rk] + HK, wkey=pk)
                    if ci_ < 22:
                        act(scr[:, ci_, c0:c0 + n], ps[:, :n], AF.Silu, reads=[pk], writes=[('scr', ci_)])
                    else:
                        f_ = ci_ - 22
                        dve_tt(scr[:, f_, c0:c0 + n], scr[:, f_, c0:c0 + n], ps[:, :n], MUL, [('scr', f_), pk], [('scr', f_)])
        for d in range(8):
            slab, rk = get_slab()
            for (c0, n) in ctiles:
                ps, pk = nb()
                mm(ps[:, :n], [(slab[:, k * 128:(k + 1) * 128], scr[:, k, c0:c0 + n]) for k in range(22)],
                   reads=[rk] + [('scr', k) for k in range(22)], wkey=pk)
                dve_tt(xres[:, d, c0:c0 + n], xres[:, d, c0:c0 + n], ps[:, :n], ADD, [('x', d), pk], [('x', d)])
        def tr_out(src_fn, nblk, nrows, dst_ap):
            for g0 in range(0, nblk, 4):
                ps, pk = nb()
                gn = min(4, nblk - g0)
                def f(e, ps=ps, g0=g0, gn=gn):
                    last = None
                    for i in range(gn):
                        last = e.transpose(ps[0:nrows, i * 128:(i + 1) * 128], src_fn(g0 + i), identf[:])
                    return last
                S.op('pe', f, reads=KEEPK + CONST, writes=[pk])
                act(tokbuf[0:nrows, g0 * 128:(g0 + gn) * 128], ps[0:nrows, 0:gn * 128], AF.Copy, reads=[pk], writes=['tokbuf'])
            S.dma('sp', dst_ap, tokbuf[0:nrows, 0:nblk * 128], c_tok, reads=['tokbuf'])

        KEEPK = ['keepAs', 'keepH', 'keepBs', 'XSs', 'keepA', 'keepB', 'HST', 'SC']
        if p == 0:
            S.dma('sp', O['s_lru_conv'][l, :, 0:2, :], I['st_lru_conv'][l, :, 1:3, :], c_out)
            S.dma('sp', O['s_cfm_conv'][l, :, 0:29, :], I['st_cfm_conv'][l, :, 1:30, :], c_out)
            tr_out(lambda i: keepAs[:, l, i, :], 8, NS, O['s_lru_conv'][l, :, 2, :])
            tr_out(lambda i: keepH[:, l, i, :], 8, NS, O['s_lru_h'][l, :, :])
            tr_out(lambda i: keepBs[:, l, i, :], 4, NS, O['s_cfm_conv'][l, :, 29, :])
            for ri, nm in enumerate(['s_ssm_re', 's_ssm_im']):
                dstv = O[nm][l].rearrange("b g p -> b (g p)")
                for hf in range(2):
                    tr_out(lambda i, ri=ri, hf=hf: XSs[:, l, ri, hf * 8 + i, :], 8, NS, dstv[:, hf * 1024:(hf + 1) * 1024])
        if last:
            tr_out(lambda i: keepA[:, l, i, :], 8, 3, O['p_lru_conv'][l, :, :])
            tr_out(lambda i: keepB[:, l, i, :], 4, 30, O['p_cfm_conv'][l, :, :])
            tr_out(lambda i: HST[:, l, :], 1, 8, O['p_lru_h'][l].rearrange("(e p) -> e p", p=128))
            tr_out(lambda i: SC[:, l, 0, :], 1, 16, O['p_ssm_re'][l].rearrange("(j two) p -> j (two p)", two=2))
            tr_out(lambda i: SC[:, l, 1, :], 1, 16, O['p_ssm_im'][l].rearrange("(j two) p -> j (two p)", two=2))

    tok2 = S0[:, :, :, :, :].rearrange("p a b c d -> p (a b c d)")
    c_tok2 = S.chan('c_tok2')

    def final_out(p, hook=None):
        ctiles = [(0, PT)] + ([(PT, NS)] if p == 0 else [])
        for (c0, n) in ctiles:
            ps, pk = nb()
            for k in range(8):
                tb = TB[k % 2]
                act(tb[:, :n], xres[:, k, c0:c0 + n], AF.Square, reads=[('x', k)], writes=[('tb', k % 2)])
                S.op('pe', lambda e, k=k, tb=tb, ps=ps, n=n: e.matmul(ps[:, :n], onesb[:], tb[:, :n], start=(k == 0), stop=(k == 7)),
                     reads=[('tb', k % 2)] + CONST, writes=[pk] if k == 0 else [])
                S.lastw[pk] = (S.echan['pe'], S.echan['pe'].cnt)
            r = TMP[7]
            act(r[:, :n], ps[:, :n], AF.Sqrt, reads=[pk] + CONST, writes=['t7'], scale=1.0 / D, bias=epsb[:, 0:1])
            S.op('dve', lambda e, r=r, n=n: e.reciprocal(r[:, :n], r[:, :n]), reads=['t7'], writes=['t7'])
            for k in range(8):
                dve_stt(xres[:, k, c0:c0 + n], xres[:, k, c0:c0 + n], gfin[:, k:k + 1], r[:, :n], MUL, MUL,
                        reads=[('x', k), 't7'] + CONST, writes=[('x', k)])
        XK = [('x', k) for k in range(8)]
        for tt in range(4):
            for half in range(2):
                ps, pk = nb()
                def f(e, ps=ps, half=half, tt=tt):
                    last = None
                    for kk in range(4):
                        k = half * 4 + kk
                        last = e.transpose(ps[:, kk * 128:(kk + 1) * 128], xres[:, k, tt * 128:(tt + 1) * 128], identf[:])
                    return last
                S.op('pe', f, reads=XK + [('xt', half * 4 + kk, tt) for kk in range(4)] + CONST, writes=[pk])
                if p >= 1 and tt % 2 == 1:
                    act(tok2[:, half * 512:(half + 1) * 512], ps[:, :], AF.Copy, reads=[pk], writes=['tok2'])
                else:
                    act(tokout[:, half * 512:(half + 1) * 512], ps[:, :], AF.Copy, reads=[pk], writes=['tokbuf'])
            t0 = p * PT + tt * 128
            if p >= 1 and tt % 2 == 1:
                S.dma('sp', O['y_p'][t0:t0 + 128, :], tok2, c_tok2, reads=['tok2'])
            else:
                S.dma('sp', O['y_p'][t0:t0 + 128, :], tokout[:], c_tok, reads=['tokbuf'])
            if hook is not None:
                hook(tt)
        if p == 0:
            for half in range(2):
                ps, pk = nb()
                def f(e, ps=ps, half=half):
                    last = None
                    for kk in range(4):
                        k = half * 4 + kk
                        last = e.transpose(ps[0:NS, kk * 128:(kk + 1) * 128], xres[:, k, PT:PT + NS], identf[:])
                    return last
                S.op('pe', f, reads=XK + CONST, writes=[pk])
                act(tokout[0:NS, half * 512:(half + 1) * 512], ps[0:NS, :], AF.Copy, reads=[pk], writes=['tokbuf'])
            S.dma('sp', O['y_s'][:, :], tokout[0:NS, :], c_tok, reads=['tokbuf'])

    cgen['g'] = convert_gen(0)
    pump(16)
    emit_const_loads()

    def tr_in(dst_fn, src_ap, ncols):
        S.dma('sp', tokbuf[0:NS, 0:ncols], src_ap, c_in, writes=['tokbuf'])
        nblk = ncols // 128
        for g0 in range(0, nblk, 4):
            ps, pk = nb()
            gn = min(4, nblk - g0)
            def f(e, ps=ps, g0=g0, gn=gn):
                last = None
                for i in range(gn):
                    last = e.transpose(ps[:, i * NS:(i + 1) * NS], tokbuf[0:NS, (g0 + i) * 128:(g0 + i + 1) * 128], identf[0:NS, 0:NS])
                return last
            S.op('pe', f, reads=['tokbuf'] + CONST, writes=[pk])
            for i in range(gn):
                act(dst_fn(g0 + i), ps[:, i * NS:(i + 1) * NS], AF.Copy, reads=[pk], writes=['H0S0'])

    for l in range(DEPTH):
        tr_in(lambda i, l=l: H0[:, l, i, :], I['st_lru_h'][l], 1024)
        for ri, nm in enumerate(['st_ssm_re', 'st_ssm_im']):
            srcv = I[nm][l].rearrange("b g p -> b (g p)")
            for hf in range(2):
                tr_in(lambda i, l=l, ri=ri, hf=hf: S0[:, l, ri, hf * 8 + i, :], srcv[:, hf * 1024:(hf + 1) * 1024], 1024)
    CONST.append('H0S0')
    tables(0)
    cgen['g'] = convert_gen(1)
    tables(1)
    S.barrier_keys(['BT'], ['ZR', 'ZI'])
    S.barrier_keys(STGK + ['stg'], [('t', i_) for i_ in range(8)] + [('scr', k_) for k_ in range(24)])
    S.barrier_keys(['s5in', 's5p', 'CQ', 'BT', 'pbf'] + BQK, [('x', k_) for k_ in range(8)])
    for p in range(NPASS):
        if p == 0:
            x_load(0)
        for l in range(DEPTH):
            layer(p, l)
        nxt = p + 1 < NPASS
        if nxt:
            if p == 0:
                S.barrier_keys(['prodB'], [('tokB', 0), ('tokB', 1)])
            x_dma(p + 1, 0)
            x_dma(p + 1, 1)
        if p == 1:
            S.barrier_keys(['H0S0'], ['tok2'])
        if nxt:
            def hook(tt, p=p):
                x_tr(p + 1, tt)
                if tt + 2 < 4:
                    x_dma(p + 1, tt + 2)
            final_out(p, hook)
        else:
            final_out(p)
    S.finish([c_out, c_tok, c_tok2])
    return nc


_CACHE = {}


def kernel(**inputs):
    n = 8
    consts = {
        'identf': np.eye(128, dtype=np.float32),
        'onesf': np.ones((128, 128), dtype=np.float32),
        'sel4f': np.repeat(np.eye(NS, dtype=np.float32), 4, axis=0),
        'sel8f': np.repeat(np.eye(NS, dtype=np.float32), 8, axis=0),
    }
    in_maps = []
    for i in range(n):
        m = {}
        m['xp'] = np.ascontiguousarray(inputs['x_prompt'][i])
        m['xs'] = np.ascontiguousarray(inputs['x_sample'][NS * i:NS * (i + 1), 0, :])
        m['st_lru_conv'] = np.ascontiguousarray(inputs['state_lru_conv'][:, NS * i:NS * (i + 1)])
        m['st_lru_h'] = np.ascontiguousarray(inputs['state_lru_h'][:, NS * i:NS * (i + 1)])
        m['st_cfm_conv'] = np.ascontiguousarray(inputs['state_cfm_conv'][:, NS * i:NS * (i + 1)])
        m['st_ssm_re'] = np.ascontiguousarray(inputs['state_ssm_re'][:, NS * i:NS * (i + 1)])
        m['st_ssm_im'] = np.ascontiguousarray(inputs['state_ssm_im'][:, NS * i:NS * (i + 1)])
        for k in PARAMS:
            m[k] = np.ascontiguousarray(np.asarray(inputs[k], dtype=np.float32))
        m.update(consts)
        in_maps.append(m)
    shapes = {k: v.shape for k, v in in_maps[0].items()}
    if 'nc' not in _CACHE:
        _CACHE['nc'] = build_program(shapes)
    nc = _CACHE['nc']
    res = run_bass_kernel_spmd(nc, in_maps, core_ids=list(range(n)))
    R = res.results
    cat = lambda name, ax: np.concatenate([np.asarray(r[name]) for r in R], axis=ax)
    y_prompt = np.stack([np.asarray(r['y_p']) for r in R], axis=0)
    y_sample = cat('y_s', 0)[:, None, :]
    outs = [y_prompt, y_sample]
    for nm in ['p_lru_conv', 'p_lru_h', 'p_cfm_conv', 'p_ssm_re', 'p_ssm_im']:
        outs.append(np.stack([np.asarray(r[nm]) for r in R], axis=1))
    for nm in ['s_lru_conv', 's_lru_h', 's_cfm_conv', 's_ssm_re', 's_ssm_im']:
        outs.append(cat(nm, 1))
    return tuple(np.ascontiguousarray(o.astype(np.float32)) for o in outs)
```

```python
import contextlib
import numpy as np
import concourse.bass as bass
import concourse.mybir as mybir
from concourse.bass_utils import run_bass_kernel_spmd

F32 = mybir.dt.float32
BF16 = mybir.dt.bfloat16
AF = mybir.ActivationFunctionType
ALU = mybir.AluOpType


class Chan:
    def __init__(self, sem, name):
        self.sem = sem
        self.cnt = 0
        self.name = name


class Sched:
    def __init__(self, nc, es):
        self.nc = nc
        self.es = es
        self.names = ['pe', 'act', 'dve', 'pool', 'sp']
        self.echan = {n: self.chan('e_' + n) for n in self.names}
        self.prog = {n: [] for n in self.names}
        self.seen = {n: {} for n in self.names}
        self.lastw = {}
        self.readers = {}
        self.nins = 0
        self.snap = {}

    def chan(self, name):
        sem = self.es.enter_context(self.nc.semaphore(name))
        return Chan(sem, name)

    def sbuf(self, name, shape, dtype):
        return self.es.enter_context(self.nc.sbuf_tensor('sb_' + name, list(shape), dtype))

    def psum(self, name, shape, dtype):
        return self.es.enter_context(self.nc.psum_tensor('ps_' + name, list(shape), dtype))

    def _deps(self, ename, reads, writes):
        own = self.echan[ename]
        need = {}

        def add(c, v):
            if need.get(c, 0) < v:
                need[c] = v

        for k in reads:
            lw = self.lastw.get(k)
            if lw is not None:
                if lw[0] is own and ename == 'pe':
                    continue
                add(*lw)
        for k in writes:
            lw = self.lastw.get(k)
            if lw is not None:
                if not (lw[0] is own and ename == 'pe'):
                    add(*lw)
            for c, v in self.readers.get(k, {}).items():
                if c is own:
                    continue
                add(c, v)
        out = []
        seen = self.seen[ename]
        items = sorted(need.items(), key=lambda cv: -cv[1])
        for c, v in items:
            if seen.get(c, 0) < v:
                seen[c] = v
                out.append((c.sem, v))
                sn = self.snap.get((c, v))
                if sn is not None and c is not own:
                    for c2, v2 in sn.items():
                        if c2 is not own and seen.get(c2, 0) < v2:
                            seen[c2] = v2
        return out

    def _record(self, reads, writes, c, v):
        for k in writes:
            self.lastw[k] = (c, v)
            self.readers[k] = {}
        for k in reads:
            self.readers.setdefault(k, {})[c] = v

    def barrier_keys(self, src_keys, dst_keys):
        for d in dst_keys:
            rd = self.readers.setdefault(d, {})
            for k in src_keys:
                lw = self.lastw.get(k)
                if lw is not None and rd.get(lw[0], 0) < lw[1]:
                    rd[lw[0]] = lw[1]
                for c, v in self.readers.get(k, {}).items():
                    if rd.get(c, 0) < v:
                        rd[c] = v

    def op(self, ename, fn, reads=(), writes=()):
        waits = self._deps(ename, reads, writes)
        c = self.echan[ename]
        c.cnt += 1
        sem = c.sem

        embed = ename in ('act', 'dve', 'pool') and len(waits) > 0

        def run(e):
            ws = waits[1:] if embed else waits
            for s, v in ws:
                e.wait_ge(s, v)
            ins = fn(e)
            if embed:
                ins._wait_ge(waits[0][0], waits[0][1])
            ins.then_inc(sem, 1)

        self.prog[ename].append(run)
        self._record(reads, writes, c, c.cnt)
        sn = dict(self.seen[ename])
        sn.pop(c, None)
        self.snap[(c, c.cnt)] = sn
        self.nins += 1

    def dma(self, ename, out, in_, chan, reads=(), writes=(), **kw):
        waits = self._deps(ename, reads, writes)
        chan.cnt += 16
        sem = chan.sem

        def run(e):
            for s, v in waits:
                e.wait_ge(s, v)
            e.dma_start(out=out, in_=in_, **kw).then_inc(sem, 16)

        self.prog[ename].append(run)
        self._record(reads, writes, chan, chan.cnt)
        sn = dict(self.seen[ename])
        sn.pop(self.echan[ename], None)
        self.snap[(chan, chan.cnt)] = sn
        self.nins += 1

    def finish(self, chans):
        fw = [(c.sem, c.cnt) for c in chans if c.cnt > 0]

        def run(e):
            for s, v in fw:
                e.wait_ge(s, v)

        self.prog['sp'].append(run)
        prog = self.prog
        with self.nc.Block() as block:
            @block.sync
            def _(e):
                for r in prog['sp']:
                    r(e)

            @block.tensor
            def _(e):
                for r in prog['pe']:
                    r(e)

            @block.scalar
            def _(e):
                for r in prog['act']:
                    r(e)

            @block.vector
            def _(e):
                for r in prog['dve']:
                    r(e)

            @block.gpsimd
            def _(e):
                for r in prog['pool']:
                    r(e)
        self.es.close()


D = 1024
SEQ = 2048
DEPTH = 2
NS = 16
W_B = 512
W_C = 512
IN_W = 5632
D_FF = 2816
O1, O2, O3 = 1024, 2048, 2560
EPS = 1e-6
NPASS = 4
PT = SEQ // NPASS
LCH = 8
NCH = PT // LCH
PI = float(np.pi)

PARAMS = ['g_mix', 'w_in', 'w_conv_a', 'b_conv_a', 'w_rg', 'b_rg', 'w_ig', 'b_ig', 'lam_a',
          'w_dw_b', 'b_dw_b', 'ln_g_b', 'ln_b_b', 'lam_re', 'lam_im', 'log_dt', 'b_ssm_re', 'b_ssm_im',
          'c_ssm_re', 'c_ssm_im', 'd_ssm', 'w_glu_c', 'b_glu_c', 'b_gate', 'w_pa', 'w_pb', 'w_pc',
          'w_out', 'g_ffn', 'w_ffn_in', 'w_ffn_out', 'g_final']


def build_program(shapes):
    nc = bass.Bass("TRN2", target_bir_lowering=False)
    es = contextlib.ExitStack()
    S = Sched(nc, es)
    I = {}
    for k, shp in shapes.items():
        I[k] = nc.dram_tensor(k, list(shp), F32, kind="ExternalInput").ap()
    O = {}

    def outp(name, shp):
        O[name] = nc.dram_tensor(name, list(shp), F32, kind="ExternalOutput").ap()

    outp('y_p', [SEQ, D]); outp('y_s', [NS, D])
    outp('p_lru_conv', [DEPTH, 3, D]); outp('p_lru_h', [DEPTH, D]); outp('p_cfm_conv', [DEPTH, 30, W_B])
    outp('p_ssm_re', [DEPTH, 32, 64]); outp('p_ssm_im', [DEPTH, 32, 64])
    outp('s_lru_conv', [DEPTH, NS, 3, D]); outp('s_lru_h', [DEPTH, NS, D]); outp('s_cfm_conv', [DEPTH, NS, 30, W_B])
    outp('s_ssm_re', [DEPTH, NS, 32, 64]); outp('s_ssm_im', [DEPTH, NS, 32, 64])

    NCM = PT + NS
    CBK = 1024
    NSTG = 4
    sb = S.sbuf
    xres = sb('xres', [128, 8, NCM], F32)
    hbf = sb('hbf', [128, 8, NCM], BF16)
    scr = sb('scr', [128, 24, NCM + 4], BF16)
    UA = scr[:, 16:24, :]
    GLU = sb('GLU', [128, 4, 30 + NCM], BF16)
    UC = sb('UC', [128, 4, NCM], BF16)
    YG = sb('YG', [128, 4, NCM], BF16)
    NTMP = 8
    tmpA = sb('tmpA', [128, NTMP, NCM], F32)
    TMP = [tmpA[:, i, :] for i in range(NTMP)]
    TB = [sb('tb%d' % i, [128, NCM], BF16) for i in range(3)]
    RSLOT = 5120
    NRING = 4
    ring = [sb('ring%d' % i, [128, RSLOT], BF16) for i in range(NRING)]
    ring_ch = [S.chan('ring%d' % i) for i in range(NRING)]
    identf = sb('identf', [128, 128], F32)
    identb = sb('identb', [128, 128], BF16)
    onesb = sb('onesb', [128, 128], BF16)
    sel4 = sb('sel4', [64, NS], BF16)
    sel8 = sb('sel8', [128, NS], BF16)
    gmix = sb('gmix', [128, DEPTH, 8], F32); gffn = sb('gffn', [128, DEPTH, 8], F32); gfin = sb('gfin', [128, 8], F32)
    wca = sb('wca', [128, DEPTH, 4, 8], F32); bca = sb('bca', [128, DEPTH, 8], F32)
    brg = sb('brg', [128, DEPTH, 8], F32); big = sb('big', [128, DEPTH, 8], F32); lama = sb('lama', [128, DEPTH, 8], F32)
    wdb = sb('wdb', [128, DEPTH, 31, 4], F32); bdb = sb('bdb', [128, DEPTH, 4], F32)
    lng = sb('lng', [128, DEPTH, 4], F32); lnb = sb('lnb', [128, DEPTH, 4], F32)
    dss = sb('dss', [128, DEPTH, 4], F32); bglu = sb('bglu', [128, DEPTH, 4], F32); bgate = sb('bgate', [128, DEPTH, 24], F32)
    wrg = sb('wrg', [128, DEPTH, 8, 128], BF16); wig = sb('wig', [128, DEPTH, 8, 128], BF16)
    HST = sb('HST', [128, DEPTH, 8], F32)
    SP2 = sb('SP2', [128, NCH + 1, 32], F32); SC = sb('SC', [128, DEPTH, 2, 16], F32)
    CO2 = sb('CO2', [128, DEPTH, 2, 32], F32)
    QT = sb('QT', [128, 32], F32); QU = sb('QU', [128, 32], F32); QW = sb('QW', [128, 32], F32)
    _sf = scr[:, :, :].rearrange("p a b -> p (a b)")
    stB = _sf[:, 4096:6144].rearrange("p (a b) -> p a b", a=4); wrepB = _sf[:, 6144:8192].rearrange("p (a b) -> p a b", a=4)
    stA = _sf[0:64, 8192:9216]; wrepA = _sf[0:64, 9216:10240]
    prodA = sb('prodA', [64, DEPTH, D], BF16); prodB = sb('prodB', [128, DEPTH, 4, W_B], BF16)
    H0 = sb('H0', [128, DEPTH, 8, NS], F32)
    S0 = sb('S0', [128, DEPTH, 2, 16, NS], F32)
    LR = sb('LR', [128, 16], F32); LI = sb('LI', [128, 16], F32); DT = sb('DT', [128, 16], F32)
    AR = sb('AR', [128, 16], F32); AI = sb('AI', [128, 16], F32); NAI = sb('NAI', [128, 16], F32)
    AR8 = sb('AR8', [128, 16], F32); AI8 = sb('AI8', [128, 16], F32)
    QR = sb('QR', [128, 16], F32); QI = sb('QI', [128, 16], F32)
    P1 = sb('P1', [128, 16], F32); P2 = sb('P2', [128, 16], F32); P3 = sb('P3', [128, 16], F32); P4 = sb('P4', [128, 16], F32); P5 = sb('P5', [128, 16], F32)
    _xf = xres[:, :, :].rearrange("p a b -> p (a b)")
    BRq, BIq, CRq, CIq, BBR, BBI, BT1 = [_xf[:, i_ * 256:(i_ + 1) * 256].rearrange("p (j c) -> p j c", c=16) for i_ in range(7)]
    CQpad = _xf[:, 1792:3840].bitcast(BF16).rearrange("p (j r m) -> p j r m", r=2, m=128)
    BTf = sb('BTf', [128, 2048], F32)
    BT = BTf[:, :].bitcast(BF16).rearrange("p (a b c) -> p a b c", a=4, b=2)
    stgf = [tmpA[:, :, :].rearrange("p a b -> p (a b)")[:, i * CBK:(i + 1) * CBK] for i in range(NSTG)]
    stgb = [scr[:, :, :].rearrange("p a b -> p (a b)")[:, i * CBK:(i + 1) * CBK] for i in range(NSTG)]
    Z2 = BTf[:, 0:32 * NCH].rearrange("p (c k) -> p c k", k=32)
    ZRK = ['ZR']; ZIK = ['ZI']
    S5S = sb('S5S', [128, 4, 2, NCM], BF16)
    BQpad = S5S[:, :, :, :].rearrange("p a b c -> p (a b c)")[:, 0:4096].rearrange("p (r j m) -> p r j m", r=2, j=16)
    BQK = [('S5S', jm) for jm in range(4)]
    dstg = S5S[:, :, :, :].rearrange("p a b c -> p (a b c)")[:, 0:4096]
    XR = TMP[5]; XI = TMP[6]
    XRK = [('t', 5)] + [('XR', jj) for jj in range(LCH)]; XIK = [('t', 6)] + [('XI', jj) for jj in range(LCH)]
    XSs = sb('XSs', [128, DEPTH, 2, 16, NS], F32)
    PWR = sb('PWR', [128, 9, 16], F32); PWI = sb('PWI', [128, 9, 16], F32); NPW = sb('NPW', [128, 9, 16], F32)
    WT = [sb('WT%d' % i, [128, 128], BF16) for i in range(2)]
    WT2 = [sb('WT2%d' % i, [128, 128], BF16) for i in range(2)]
    WTw = [sb('WTw%d' % i, [128, 256], BF16) for i in range(2)]
    WT4 = [sb('WT4%d' % i, [128, 128], BF16) for i in range(4)]
    COEF = sb('COEF', [128, DEPTH, 5, 16], F32)
    CBs = sb('CBs', [128, 4, NS], BF16); CB2s = sb('CB2s', [128, 4, NS], BF16)
    sca = sb('sca', [128, 8], F32)
    tokbuf = sb('tokbuf', [128, D], F32)
    keepA = sb('keepA', [128, DEPTH, 8, 3], F32); keepB = sb('keepB', [128, DEPTH, 4, 30], F32)
    keepAs = sb('keepAs', [128, DEPTH, 8, NS], F32); keepBs = sb('keepBs', [128, DEPTH, 4, NS], F32); keepH = sb('keepH', [128, DEPTH, 8, NS], F32)
    UAh = sb('UAh', [128, DEPTH, 8, 3], BF16); GLUh = sb('GLUh', [128, DEPTH, 4, 30], BF16)
    Q1 = sb('Q1', [128, 16], F32); Q2 = sb('Q2', [128, 16], F32); Q3 = sb('Q3', [128, 16], F32); Q4 = sb('Q4', [128, 16], F32)
    tokout = tokbuf
    NB = 5
    pst1 = S.psum('pst1', [128, 512], F32); pst2 = S.psum('pst2', [128, 512], F32)
    banks = [S.psum('pb%d' % i, [128, 512], F32) for i in range(NB)]
    pbf = S.psum('pbf', [128, 1024], BF16)
    bank_i = [0]

    def nb():
        i = bank_i[0] % NB
        bank_i[0] += 1
        return banks[i], ('ps', i)

    c_const = S.chan('c_const')
    c_out = S.chan('c_out')
    c_in = S.chan('c_in')

    cl = []

    def cload(dst, src, key):
        cl.append((dst, src, key))

    cload(identf[:], I['identf'], 'identf'); cload(sel4[:], None, None) if False else None
    for l in range(DEPTH):
        cload(gmix[:, l, :], I['g_mix'][l].rearrange("(k p) -> p k", p=128), 'par')
        cload(gffn[:, l, :], I['g_ffn'][l].rearrange("(k p) -> p k", p=128), 'par')
        for k in range(4):
            cload(wca[:, l, k, :], I['w_conv_a'][l, k].rearrange("(k p) -> p k", p=128), 'par')
        cload(bca[:, l, :], I['b_conv_a'][l].rearrange("(k p) -> p k", p=128), 'par')
        cload(brg[:, l, :], I['b_rg'][l].rearrange("(k p) -> p k", p=128), 'par')
        cload(big[:, l, :], I['b_ig'][l].rearrange("(k p) -> p k", p=128), 'par')
        cload(lama[:, l, :], I['lam_a'][l].rearrange("(k p) -> p k", p=128), 'par')
        cload(wdb[:, l, :, :], I['w_dw_b'][l].rearrange("t (k p) -> p t k", p=128), 'par')
        cload(bdb[:, l, :], I['b_dw_b'][l].rearrange("(k p) -> p k", p=128), 'par')
        cload(lng[:, l, :], I['ln_g_b'][l].rearrange("(k p) -> p k", p=128), 'par')
        cload(lnb[:, l, :], I['ln_b_b'][l].rearrange("(k p) -> p k", p=128), 'par')
        cload(dss[:, l, :], I['d_ssm'][l].rearrange("(k p) -> p k", p=128), 'par')
        cload(bglu[:, l, :], I['b_glu_c'][l].rearrange("(k p) -> p k", p=128), 'par')
        cload(bgate[:, l, :], I['b_gate'][l].rearrange("(k p) -> p k", p=128), 'par')
    cload(gfin[:], I['g_final'].rearrange("(k p) -> p k", p=128), 'par')
    cl = [c for c in cl if c is not None]
    S.op('pool', lambda e: e.memset(stA[:], 0.0), writes=['stg'])
    S.op('pool', lambda e: e.memset(wrepA[:], 0.0), writes=['stg'])
    S.op('pool', lambda e: e.memset(stB[:], 0.0), writes=['stg'])
    S.op('pool', lambda e: e.memset(wrepB[:], 0.0), writes=['stg'])
    c_csw = S.chan('c_csw')
    CONST = ['const', 'const2']

    def emit_const_loads():
        for dst, src, key in cl:
            if key == 'sw':
                S.dma('pool', dst, src, c_csw, allow_slow_non_contiguous=True)
            else:
                S.dma('sp', dst, src, c_const, allow_slow_non_contiguous=True)
        S.dma('pool', identb[:], I['identf'], c_csw)
        S.dma('pool', onesb[:], I['onesf'], c_csw)
        S.dma('pool', sel4[:], I['sel4f'], c_csw)
        S.dma('pool', sel8[:], I['sel8f'], c_csw)
        for l in range(DEPTH):
            S.dma('pool', wrg[:, l, :, :], I['w_rg'][l].rearrange("h i j -> i h j"), c_csw)
            S.dma('pool', wig[:, l, :, :], I['w_ig'][l].rearrange("h i j -> i h j"), c_csw)
        S.lastw['const'] = (c_const, c_const.cnt)
        S.readers['const'] = {}
        S.lastw['const2'] = (c_csw, c_csw.cnt)
        S.readers['const2'] = {}

    NSL = 58
    wsc = nc.dram_tensor("wsc", [DEPTH, NSL, 128, RSLOT], BF16, kind="Internal").ap()
    c_wtD = S.chan('c_wtD'); c_wtB = S.chan('c_wtB'); c_wtQ = S.chan('c_wtQ'); c_wtK = S.chan('c_wtK')
    WTCH = [c_wtD, c_wtB, c_wtQ, c_wtK]
    c_ws = [S.chan('c_ws%d' % i) for i in range(NSTG)]
    c_stg = [S.chan('c_stg%d' % i) for i in range(NSTG)]

    def layer_slabs():
        sl = []
        for s_ in range(5):
            sl.append([(0, 8, 512, 'w_in', s_ * 512)])
        for i_ in range(8):
            sl.append(('WZ', i_))
        sl.append('diagA')
        for c in range(4):
            sl.append(('diagB', c))
        for i_ in range(8):
            sl.append(('CA', i_))
        sl.append('BT')
        sl.append('CQ')
        sl.append([(0, 4, 512, 'w_glu_c', 0)])
        for d in range(8):
            sl.append([(0, 8, 128, 'w_in', O3 + d * 128), (1024, 8, 128, 'w_in', O3 + 1024 + d * 128),
                       (2048, 8, 128, 'w_in', O3 + 2048 + d * 128), (3072, 8, 128, 'w_pa', d * 128),
                       (4096, 4, 128, 'w_pb', d * 128), (4608, 4, 128, 'w_pc', d * 128)])
        for s_ in range(2):
            sl.append([(0, 8, 512, 'w_out', s_ * 512)])
        for s_ in range(11):
            sl.append([(0, 8, 512, 'w_ffn_in', s_ * 512)])
        for d in range(8):
            sl.append([(0, 22, 128, 'w_ffn_out', d * 128)])
        return sl

    LSL = layer_slabs()
    SIDX = {(sp_ if not isinstance(sp_, list) else None): i_ for i_, sp_ in enumerate(LSL)}
    assert len(LSL) == NSL
    SLAB_N = []
    for sp_ in LSL:
        if isinstance(sp_, list):
            SLAB_N.append(max(off + KC * W for (off, KC, W, _, _) in sp_))
        elif sp_ == 'diagB' or (isinstance(sp_, tuple) and sp_[0] == 'diagB'):
            SLAB_N.append(31 * 128)
        elif isinstance(sp_, tuple) and sp_[0] == 'CA':
            SLAB_N.append(5120)
        else:
            SLAB_N.append(4096)
    slabs = [(l, si) for p in range(NPASS) for l in range(DEPTH) for si in range(NSL)]
    rs = {'issued': 0, 'next': 0}

    def issue_slab():
        n = rs['issued']
        if n >= len(slabs):
            return
        slot = n % NRING
        l, si = slabs[n]
        ne = SLAB_N[si]
        S.dma('sp', ring[slot][:, 0:ne], wsc[l, si, :, 0:ne], ring_ch[slot], reads=[('wscL', l, i_) for i_ in range(NSTG + 4)], writes=[('ring', slot)])
        rs['issued'] += 1

    def get_slab(held=0):
        n = rs['next']
        while rs['issued'] < min(n + NRING - held, len(slabs)):
            issue_slab()
        rs['next'] += 1
        slot = n % NRING
        return ring[slot], ('ring', slot)

    def mm(out_ap, pairs, reads, wkey):
        def f(e):
            last = None
            n = len(pairs)
            for i, (a, b) in enumerate(pairs):
                last = e.matmul(out_ap, a, b, start=(i == 0), stop=(i == n - 1))
            return last
        S.op('pe', f, reads=reads, writes=[wkey])

    def act(out, in_, func, reads, writes, **kw):
        S.op('act', lambda e: e.activation(out, in_, func, **kw), reads=reads, writes=writes)

    def dve_tt(out, a, b, op, reads, writes, eng='dve'):
        S.op(eng, lambda e: e.tensor_tensor(out, a, b, op), reads=reads, writes=writes)

    def dve_ts(out, a, s1, s2, op0, op1, reads, writes, eng='dve'):
        if op1 is None:
            S.op(eng, lambda e: e.tensor_scalar(out, a, s1, None, op0), reads=reads, writes=writes)
        else:
            S.op(eng, lambda e: e.tensor_scalar(out, a, s1, s2, op0, op1), reads=reads, writes=writes)

    def dve_stt(out, a, sc_, b, op0, op1, reads, writes):
        S.op('dve', lambda e: e.scalar_tensor_tensor(out, a, sc_, b, op0, op1), reads=reads, writes=writes)

    def dve_cp(out, a, reads, writes, eng='dve'):
        S.op(eng, lambda e: e.tensor_copy(out, a), reads=reads, writes=writes)

    MUL, ADD, SUB = ALU.mult, ALU.add, ALU.subtract

    def rmsnorm(ctiles, gsc, okey):
        for (c0, n) in ctiles:
            ps, pk = nb()
            for k in range(8):
                tb = TB[k % 2]
                act(tb[:, :n], xres[:, k, c0:c0 + n], AF.Square, reads=[('x', k)], writes=[('tb', k % 2)])
                def f(e, k=k, tb=tb, ps=ps, n=n):
                    return e.matmul(ps[:, :n], onesb[:], tb[:, :n], start=(k == 0), stop=(k == 7))
                S.op('pe', f, reads=[('tb', k % 2)] + CONST, writes=[pk] if k == 0 else [])
                S.lastw[pk] = (S.echan['pe'], S.echan['pe'].cnt)
            r = TMP[7]
            act(r[:, :n], ps[:, :n], AF.Sqrt, reads=[pk], writes=['t7'], scale=1.0 / D, bias=epsb[:, 0:1])
            S.op('dve', lambda e, r=r, n=n: e.reciprocal(r[:, :n], r[:, :n]), reads=['t7'], writes=['t7'])
            for k in range(8):
                dve_stt(hbf[:, k, c0:c0 + n], xres[:, k, c0:c0 + n], gsc[:, k:k + 1], r[:, :n], MUL, MUL,
                        reads=[('x', k), 't7'] + CONST, writes=[(okey, k)])

    epsb = sb('epsb', [128, 1], F32)
    oneb = sb('oneb', [128, 1], F32)
    S.op('pool', lambda e: e.memset(epsb[:], EPS), writes=['epsb'])
    S.op('pool', lambda e: e.memset(oneb[:], 1.0), writes=['epsb'])
    CONST.append('epsb')
    S.op('pool', lambda e: e.memset(HST[:], 0.0), writes=['HST'])
    S.op('pool', lambda e: e.memset(SC[:], 0.0), writes=['SC'])
    S.op('pool', lambda e: e.memset(UAh[:], 0.0), writes=['UAh'])
    UAK = [('scr', 16 + k) for k in range(8)]
    S.op('pool', lambda e: e.memset(GLUh[:], 0.0), writes=['GLUh'])

    c_st = S.chan('c_st')
    for l in range(DEPTH):
        first = True
        def sd(dst, src):
            nonlocal first
            S.dma('pool', dst, src, c_st, writes=['stg'] if first else [], allow_slow_non_contiguous=True)
            first = False
        for b in range(NS):
            sd(stA[4 * b:4 * b + 3, :], I['st_lru_conv'][l, b])
            sd(wrepA[4 * b:4 * b + 3, :], I['w_conv_a'][l, 0:3])
            sd(stB[8 * b:8 * b + 7, :, :], I['st_cfm_conv'][l, b, 0:28].rearrange("(kb kk) c -> kb kk c", kk=4))
            sd(stB[8 * b + 7:8 * b + 8, 0:2, :], I['st_cfm_conv'][l, b, 28:30].rearrange("(kb kk) c -> kb kk c", kk=2))
            sd(wrepB[8 * b:8 * b + 7, :, :], I['w_dw_b'][l, 0:28].rearrange("(kb kk) c -> kb kk c", kk=4))
            sd(wrepB[8 * b + 7:8 * b + 8, 0:2, :], I['w_dw_b'][l, 28:30].rearrange("(kb kk) c -> kb kk c", kk=2))
        S.lastw['stg'] = (c_st, c_st.cnt)
        dve_tt(prodA[:, l, :], stA[:], wrepA[:], MUL, ['stg'], ['prodA'])
        for kk in range(4):
            dve_tt(prodB[:, l, kk, :], stB[:, kk, :], wrepB[:, kk, :], MUL, ['stg'], ['prodB'])

    c_s5 = S.chan('c_s5')
    c_tok = S.chan('c_tok')
    c_out2 = S.chan('c_out2')
    NSC = dict(allow_slow_non_contiguous=True)

    def s5_prep(l):
        S.dma('sp', LR[:], I['lam_re'][l].rearrange("(j two) p -> (two p) j", two=2), c_s5, writes=['s5in'], **NSC)
        S.dma('sp', LI[:], I['lam_im'][l].rearrange("(j two) p -> (two p) j", two=2), c_s5, **NSC)
        for h in range(2):
            S.dma('sp', DT[64 * h:64 * h + 64, :], I['log_dt'][l].rearrange("(j two) -> two j", two=2)[h].partition_broadcast(64), c_s5, **NSC)
        S.dma('sp', BRq[:], I['b_ssm_re'][l].rearrange("(j two) p c -> (two p) j c", two=2), c_s5, **NSC)
        S.dma('sp', BIq[:], I['b_ssm_im'][l].rearrange("(j two) p c -> (two p) j c", two=2), c_s5, **NSC)
        for h in range(2):
            for j in range(16):
                S.dma('sp', CRq[64 * h:64 * h + 64, j, :], I['c_ssm_re'][l, 2 * j + h].rearrange("c p -> p c"), c_s5, **NSC)
                S.dma('sp', CIq[64 * h:64 * h + 64, j, :], I['c_ssm_im'][l, 2 * j + h].rearrange("c p -> p c"), c_s5, **NSC)
        S.lastw['s5in'] = (c_s5, c_s5.cnt)
        R = ['s5in']
        W = ['s5p']
        RW = ['s5in', 's5p']
        act(DT[:], DT[:], AF.Exp, reads=R, writes=W)
        dve_tt(P1[:], LR[:], DT[:], MUL, RW, W)
        dve_tt(P2[:], LI[:], DT[:], MUL, RW, W)
        act(P3[:], P1[:], AF.Exp, reads=RW, writes=W)
        MAGIC = 12582912.0
        for shift, dst in ((0.0, AI), (0.25, AR)):
            dve_ts(P4[:], P2[:], 1.0 / (2 * PI), shift, MUL, ADD, RW, W)
            dve_ts(P5[:], P4[:], MAGIC, None, ADD, None, RW, W)
            dve_ts(P5[:], P5[:], -MAGIC, None, ADD, None, RW, W)
            dve_tt(P4[:], P4[:], P5[:], SUB, RW, W)
            act(P4[:], P4[:], AF.Sin, reads=RW, writes=W, scale=6.283185)
            dve_tt(dst[:], P3[:], P4[:], MUL, RW, W)
        dve_ts(NAI[:], AI[:], -1.0, None, MUL, None, RW, W)
        dve_ts(P1[:], AR[:], -1.0, None, ADD, None, RW, W)
        dve_tt(P2[:], LR[:], LR[:], MUL, RW, W)
        dve_tt(P3[:], LI[:], LI[:], MUL, RW, W)
        dve_tt(P2[:], P2[:], P3[:], ADD, RW, W)
        S.op('dve', lambda e: e.reciprocal(P2[:], P2[:]), reads=RW, writes=W)
        dve_tt(P3[:], P1[:], LR[:], MUL, RW, W)
        dve_tt(P4[:], AI[:], LI[:], MUL, RW, W)
        dve_tt(P3[:], P3[:], P4[:], ADD, RW, W)
        dve_tt(QR[:], P3[:], P2[:], MUL, RW, W)
        dve_tt(P3[:], AI[:], LR[:], MUL, RW, W)
        dve_tt(P4[:], P1[:], LI[:], MUL, RW, W)
        dve_tt(P3[:], P3[:], P4[:], SUB, RW, W)
        dve_tt(QI[:], P3[:], P2[:], MUL, RW, W)
        dve_cp(AR8[:], AR[:], RW, W)
        dve_cp(AI8[:], AI[:], RW, W)
        for _ in range(3):
            dve_tt(P1[:], AR8[:], AR8[:], MUL, RW, W)
            dve_tt(P2[:], AI8[:], AI8[:], MUL, RW, W)
            dve_tt(P3[:], AR8[:], AI8[:], MUL, RW, W)
            dve_tt(AR8[:], P1[:], P2[:], SUB, RW, W)
            dve_ts(AI8[:], P3[:], 2.0, None, MUL, None, RW, W)
        QRb = QR[:].unsqueeze(2).broadcast_to([128, 16, 16])
        QIb = QI[:].unsqueeze(2).broadcast_to([128, 16, 16])
        dve_tt(BBR[:], BRq[:], QRb, MUL, RW, W)
        dve_tt(BT1[:], BIq[:], QIb, MUL, RW, W)
        dve_tt(BBR[:], BBR[:], BT1[:], SUB, RW, W)
        dve_tt(BBI[:], BIq[:], QRb, MUL, RW, W)
        dve_tt(BT1[:], BRq[:], QIb, MUL, RW, W)
        dve_tt(BBI[:], BBI[:], BT1[:], ADD, RW, W)
        S.op('pool', lambda e: e.memset(BQpad, 0.0), reads=[], writes=BQK)
        S.op('pool', lambda e: e.memset(CQpad[:, :, :, :], 0.0), reads=[], writes=['CQ'])
        for h in range(2):
            for jm in range(4):
                co = 32 * jm + 16 * h
                ps_ = slice(64 * h, 64 * h + 64)
                dve_cp(BQpad[ps_, 0, jm::4, co:co + 16], BBR[ps_, jm::4, :], RW + BQK, BQK)
                dve_cp(BQpad[ps_, 1, jm::4, co:co + 16], BBI[ps_, jm::4, :], RW + BQK, BQK)
                dve_cp(CQpad[ps_, jm::4, 0, co:co + 16], CRq[ps_, jm::4, :], RW + ['CQ'], ['CQ'])
                dve_ts(CQpad[ps_, jm::4, 1, co:co + 16], CIq[ps_, jm::4, :], -1.0, None, MUL, None, RW + ['CQ'], ['CQ'])
        for c8 in range(4):
            for ri in range(2):
                def f(e, c8=c8, ri=ri):
                    last = None
                    for jm in range(4):
                        last = e.transpose(pbf[:, jm * 128:(jm + 1) * 128], BQpad[:, ri, 4 * c8 + jm, :], identb[:])
                    return last
                S.op('pe', f, reads=BQK + CONST, writes=['pbf'])
                act(BT[:, c8, ri, :], pbf[:, 0:512], AF.Copy, reads=['pbf'], writes=['BT'])
        S.dma('act', wsc[l, SIDX['BT'], :, 0:4096], BT[:, :, :, :].rearrange("p a b c -> p (a b c)"), c_wtB, reads=['BT'])
        S.dma('act', wsc[l, SIDX['CQ'], :, 0:4096], CQpad[:, :, :, :].rearrange("p a b c -> p (a b c)"), c_wtQ, reads=['CQ'])
        for i_, t_ in enumerate([AR, AI, NAI, AR8, AI8]):
            dve_cp(COEF[:, l, i_, :], t_[:], RW, ['COEF'])
        dve_cp(CO2[:, l, 0, 0:16], AR8[:], RW, ['COEF'])
        dve_cp(CO2[:, l, 0, 16:32], AR8[:], RW, ['COEF'])
        dve_ts(CO2[:, l, 1, 0:16], AI8[:], -1.0, None, MUL, None, RW, ['COEF'])
        dve_cp(CO2[:, l, 1, 16:32], AI8[:], RW, ['COEF'])

    MATS = {'w_in': (1024, IN_W), 'w_glu_c': (512, 512), 'w_pa': (1024, 1024), 'w_pb': (512, 1024), 'w_pc': (512, 1024),
            'w_out': (1024, 1024), 'w_ffn_in': (1024, 2 * D_FF), 'w_ffn_out': (D_FF, 1024)}
    cvt = {'i': 0}

    STGK = [('stgf', i) for i in range(NSTG)] + [('stgb', i) for i in range(NSTG)]

    def convert_gen(l):
        index = {m: [] for m in MATS}
        for si, sp_ in enumerate(LSL):
            if isinstance(sp_, list):
                for (off, KC, W, m, col0) in sp_:
                    index[m].append((si, off, KC, W, col0))
        blocks = []
        for m, (K, N) in MATS.items():
            for k in range(K // 128):
                for a in range(0, N, CBK):
                    blocks.append((m, k, a, min(N, a + CBK)))
        nblk_ = len(blocks)
        base = cvt['i']
        cvt['i'] += nblk_

        def load(bi):
            m, k, a, b = blocks[bi]
            i = (base + bi) % NSTG
            S.dma('sp', stgf[i][:, 0:b - a], I[m][l][k * 128:(k + 1) * 128, a:b], c_stg[i], writes=[('stgf', i)])

        PF = NSTG - 1
        for bi in range(min(PF, nblk_)):
            load(bi)
        for bi in range(nblk_):
            if bi + PF < nblk_:
                load(bi + PF)
            m, k, a, b = blocks[bi]
            i = (base + bi) % NSTG
            dve_cp(stgb[i][:, 0:b - a], stgf[i][:, 0:b - a], [('stgf', i)], [('stgb', i)])
            for (si, off, KC, W, col0) in index[m]:
                if k >= KC:
                    continue
                lo, hi = max(a, col0), min(b, col0 + W)
                if lo >= hi:
                    continue
                d0 = off + k * W + (lo - col0)
                S.dma('sp', wsc[l, si, :, d0:d0 + (hi - lo)], stgb[i][:, lo - a:hi - a], c_ws[i], reads=[('stgb', i)])
            yield

    cgen = {'g': None}

    def pump(n):
        g = cgen['g']
        if g is None:
            return
        for _ in range(n):
            try:
                next(g)
            except StopIteration:
                cgen['g'] = None
                return

    def convert(l):
        cgen['g'] = convert_gen(l)
        pump(1 << 30)

    def tables(l, last_of=None):
        for e_ in range(8):
            for k in range(4):
                o_ = (e_ * 4 + k) * 128
                S.op('dve', lambda e, e_=e_, k=k, o_=o_: e.tensor_scalar(dstg[:, o_:o_ + 128], identb[:], wca[:, l, k, e_:e_ + 1], None, MUL),
                     reads=CONST, writes=BQK if ((e_ == 0 and k == 0) or (e_ == 7 and k == 3)) else [])
        S.dma('act', wsc[l, SIDX['diagA'], :, 0:4096], dstg, c_wtD, reads=BQK)
        for c in range(4):
            for k in range(31):
                S.op('dve', lambda e, c=c, k=k: e.tensor_scalar(dstg[:, k * 128:(k + 1) * 128], identb[:], wdb[:, l, k, c:c + 1], None, MUL),
                     reads=CONST, writes=BQK if k in (0, 30) else [])
            S.dma('act', wsc[l, SIDX[('diagB', c)], :, 0:31 * 128], dstg[:, 0:31 * 128], c_wtD, reads=BQK)
            pump(3)
        s5_prep(l)
        RWp = ['s5p', 'PW']
        S.op('dve', lambda e: e.memset(PWR[:, 0, :], 1.0), reads=[], writes=['PW'])
        S.op('dve', lambda e: e.memset(PWI[:, 0, :], 0.0), reads=['PW'], writes=['PW'])
        for m in range(1, 9):
            dve_tt(P1[:], PWR[:, m - 1, :], AR[:], MUL, RWp, ['s5p'])
            dve_tt(P2[:], PWI[:, m - 1, :], AI[:], MUL, RWp, ['s5p'])
            dve_tt(PWR[:, m, :], P1[:], P2[:], SUB, RWp, ['PW'])
            dve_tt(P1[:], PWR[:, m - 1, :], AI[:], MUL, RWp, ['s5p'])
            dve_tt(P2[:], PWI[:, m - 1, :], AR[:], MUL, RWp, ['s5p'])
            dve_tt(PWI[:, m, :], P1[:], P2[:], ADD, RWp, ['PW'])
        dve_ts(NPW[:, :, :], PWI[:, :, :], -1.0, None, MUL, None, RWp, ['PW'])
        BTflat = BT[:, :, :, :].rearrange("p a b c -> p (a b c)")
        bi_ = 0
        for c8 in range(4):
            for hf in range(2):
                for jm2 in range(2):
                    j = 4 * c8 + 2 * hf + jm2
                    pbs = [nb(), nb()]
                    for m in range(8):
                        pw = 7 - m
                        ww = WTw[bi_ % 2]
                        kw = ('WTw', bi_ % 2)
                        S.op('act', lambda e, ww=ww, j=j, pw=pw: e.activation(ww[:, :].rearrange("p (a b) -> p a b", a=2), BQpad[:, :, j, :], AF.Copy, scale=PWR[:, pw, j:j + 1]),
                             reads=BQK + ['PW'], writes=[kw])
                        for ri in range(2):
                            w2 = WT4[(2 * bi_ + ri) % 4]
                            k2 = ('WT4', (2 * bi_ + ri) % 4)
                            if ri == 0:
                                b_, sb_ = BQpad[:, 1, j, :], NPW[:, pw, j:j + 1]
                            else:
                                b_, sb_ = BQpad[:, 0, j, :], PWI[:, pw, j:j + 1]
                            S.op('dve', lambda e, w2=w2, b_=b_, sb_=sb_, ww=ww, ri=ri: e.scalar_tensor_tensor(w2[:], b_, sb_, ww[:, ri * 128:(ri + 1) * 128], MUL, ADD),
                                 reads=BQK + ['PW', kw], writes=[k2])
                            pb_, pkb = pbs[ri]
                            S.op('pe', lambda e, w2=w2, m=m, pb_=pb_: e.transpose(pb_[:, :].bitcast(BF16)[:, m * 128:(m + 1) * 128], w2[:], identb[:]),
                                 reads=[k2] + CONST, writes=[pkb] if m == 0 else [])
                            S.lastw[pkb] = (S.echan['pe'], S.echan['pe'].cnt)
                        bi_ += 1
                    for ri in range(2):
                        pb_, pkb = pbs[ri]
                        o_ = (jm2 * 2 + ri) * 1024
                        act(BTflat[:, o_:o_ + 1024], pb_[:, :].bitcast(BF16), AF.Copy, reads=[pkb], writes=['BT'])
                    pump(4)
                S.dma('act', wsc[l, SIDX[('WZ', c8 * 2 + hf)], :, 0:4096], BTflat, c_wtB, reads=['BT'])
        Kstg = hbf[:, :, :].rearrange("p a b -> p (a b)")[:, 0:4096]
        for c8 in range(4):
            kb = [nb(), nb()]
            def kgroup(m, rhs_fn, c8=c8, kb=kb):
                ps_, pk_ = kb[m // 4]
                def f(e):
                    last = None
                    i_ = 0
                    for jm in range(4):
                        for ri in range(2):
                            last = e.matmul(ps_[:, (m % 4) * 128:(m % 4 + 1) * 128], BQpad[:, ri, 4 * c8 + jm, :], rhs_fn(jm, ri), start=(i_ == 0), stop=(i_ == 7))
                            i_ += 1
                    return last
                S.op('pe', f, reads=BQK + ['CQ', 'BT'], writes=[pk_] if m % 4 == 0 else [])
                S.lastw[pk_] = (S.echan['pe'], S.echan['pe'].cnt)
            kgroup(0, lambda jm, ri, c8=c8: CQpad[:, 4 * c8 + jm, ri, :])
            for ph in range(2):
                for jm in range(4):
                    j = 4 * c8 + jm
                    for pwi in range(4):
                        pw = 4 * ph + pwi + 1
                        ww = WTw[bi_ % 2]
                        k1 = ('WTw', bi_ % 2)
                        bi_ += 1
                        S.op('act', lambda e, ww=ww, j=j, pw=pw: e.activation(ww[:], CQpad[:, j, :, :].rearrange("p a b -> p (a b)"), AF.Copy, scale=PWR[:, pw, j:j + 1]),
                             reads=['CQ', 'PW'], writes=[k1])
                        for ri in range(2):
                            o_ = ((jm * 2 + ri) * 4 + pwi) * 128
                            w1 = ww[:, ri * 128:(ri + 1) * 128]
                            if ri == 0:
                                b_, sb_ = CQpad[:, j, 1, :], PWI[:, pw, j:j + 1]
                            else:
                                b_, sb_ = CQpad[:, j, 0, :], NPW[:, pw, j:j + 1]
                            S.op('dve', lambda e, w1=w1, b_=b_, sb_=sb_, o_=o_: e.scalar_tensor_tensor(BTflat[:, o_:o_ + 128], b_, sb_, w1, MUL, ADD),
                                 reads=['CQ', 'PW', k1], writes=['BT'] if ((jm == 0 and pwi == 0 and ri == 0) or (jm == 3 and pwi == 3 and ri == 1)) else [])
                for pwi in range(4):
                    m = 4 * ph + pwi + 1
                    if m <= 7:
                        kgroup(m, lambda jm, ri, pwi=pwi: BTflat[:, ((jm * 2 + ri) * 4 + pwi) * 128:((jm * 2 + ri) * 4 + pwi + 1) * 128])
                S.dma('act', wsc[l, SIDX[('CA', c8 * 2 + ph)], :, 0:4096], BTflat, c_wtB, reads=['BT'])
                pump(8)
            for hb in range(2):
                ps_, pk_ = kb[hb]
                act(Kstg[:, (c8 * 8 + hb * 4) * 128:(c8 * 8 + hb * 4 + 4) * 128], ps_[:, :], AF.Copy, reads=[pk_], writes=['Kstg'])
            for ph in range(2):
                S.dma('act', wsc[l, SIDX[('CA', c8 * 2 + ph)], :, 4096:5120], Kstg[:, c8 * 1024:(c8 + 1) * 1024], c_wtK, reads=['Kstg'])
        S.barrier_keys(['Kstg'], HK)
        pump(1 << 30)
        for i_, ch_ in enumerate(WTCH + c_ws):
            S.lastw[('wscL', l, i_)] = (ch_, ch_.cnt)
            S.readers[('wscL', l, i_)] = {}

    def x_load(p):
        for tt in range(4):
            t0 = p * PT + tt * 128
            S.dma('sp', tokbuf[:], I['xp'][t0:t0 + 128, :], c_in, writes=['tokbuf'])
            for half in range(2):
                ps, pk = nb()
                def f(e, ps=ps, half=half):
                    last = None
                    for kk in range(4):
                        k = half * 4 + kk
                        last = e.transpose(ps[:, kk * 128:(kk + 1) * 128], tokbuf[:, k * 128:(k + 1) * 128], identf[:])
                    return last
                S.op('pe', f, reads=['tokbuf'] + CONST, writes=[pk])
                act(xres[:, half * 4:half * 4 + 4, tt * 128:(tt + 1) * 128], ps[:, :].rearrange("p (k t) -> p k t", t=128), AF.Copy,
                    reads=[pk], writes=[('x', half * 4 + kk) for kk in range(4)])
        if p == 0:
            S.dma('sp', tokbuf[0:NS, :], I['xs'], c_in, writes=['tokbuf'])
            for half in range(2):
                ps, pk = nb()
                def f(e, ps=ps, half=half):
                    last = None
                    for kk in range(4):
                        k = half * 4 + kk
                        last = e.transpose(ps[:, kk * NS:(kk + 1) * NS], tokbuf[0:NS, k * 128:(k + 1) * 128], identf[0:NS, 0:NS])
                    return last
                S.op('pe', f, reads=['tokbuf'] + CONST, writes=[pk])
                act(xres[:, half * 4:half * 4 + 4, PT:PT + NS], ps[:, 0:4 * NS].rearrange("p (k t) -> p k t", t=NS), AF.Copy,
                    reads=[pk], writes=[('x', half * 4 + kk) for kk in range(4)])

    HK = [('h', k) for k in range(8)]

    def layer(p, l):
        last = (p == NPASS - 1)
        ctiles = [(0, PT)] + ([(PT, NS)] if p == 0 else [])
        NC = PT + (NS if p == 0 else 0)
        rmsnorm(ctiles, gmix[:, l, :], 'h')
        dve_cp(UA[:, :, 0:3], UAh[:, l, :, :], ['UAh'], UAK, eng='pool')
        dve_cp(GLU[:, :, 0:30], GLUh[:, l, :, :], ['GLUh'], ['GLU'], eng='pool')
        for s_ in range(5):
            slab, rk = get_slab()
            for c4 in range(4):
                e_ = s_ * 4 + c4
                for (c0, n) in ctiles:
                    ps, pk = nb()
                    mm(ps[:, :n], [(slab[:, k * 512 + c4 * 128:k * 512 + c4 * 128 + 128], hbf[:, k, c0:c0 + n]) for k in range(8)],
                       reads=[rk] + HK, wkey=pk)
                    if e_ < 8:
                        act(UA[:, e_, 3 + c0:3 + c0 + n], ps[:, :n], AF.Copy, reads=[pk], writes=[('scr', 16 + e_)])
                        if c0 == PT:
                            act(keepAs[:, l, e_, :], ps[:, :n], AF.Copy, reads=[pk], writes=['keepAs'])
                        elif last:
                            act(keepA[:, l, e_, :], ps[:, PT - 3:PT], AF.Copy, reads=[pk], writes=['keepA'])
                    elif e_ < 12:
                        act(TMP[e_ - 8][:, c0:c0 + n], ps[:, :n], AF.Copy, reads=[pk], writes=[('t', e_ - 8)])
                    elif e_ < 16:
                        c = e_ - 12
                        act(TMP[4][:, c0:c0 + n], ps[:, :n], AF.Sigmoid, reads=[pk], writes=[('t', 4)])
                        dve_tt(GLU[:, c, 30 + c0:30 + c0 + n], TMP[c][:, c0:c0 + n], TMP[4][:, c0:c0 + n], MUL,
                               [('t', c), ('t', 4)], ['GLU'])
                        if c0 == PT:
                            dve_tt(keepBs[:, l, c, :], TMP[c][:, c0:c0 + n], TMP[4][:, c0:c0 + n], MUL, [('t', c), ('t', 4)], ['keepBs'])
                        elif last:
                            dve_tt(keepB[:, l, c, :], TMP[c][:, PT - 30:PT], TMP[4][:, PT - 30:PT], MUL, [('t', c), ('t', 4)], ['keepB'])
                    else:
                        act(UC[:, e_ - 16, c0:c0 + n], ps[:, :n], AF.Copy, reads=[pk], writes=['UC'])
        dve_cp(UAh[:, l, :, :], UA[:, :, PT:PT + 3], UAK, ['UAh'], eng='pool')
        dve_cp(GLUh[:, l, :, :], GLU[:, :, PT:PT + 30], ['GLU'], ['GLUh'], eng='pool')
        BTs = lambda c8, ri, jm: slabT[:, (c8 * 2 + ri) * 512 + jm * 128:(c8 * 2 + ri) * 512 + (jm + 1) * 128]
        CQs = lambda j, ri: slabQ[:, (j * 2 + ri) * 128:(j * 2 + ri + 1) * 128]
        cAR = lambda j: COEF[:, l, 0, j:j + 1]
        cAI = lambda j: COEF[:, l, 1, j:j + 1]
        cNAI = lambda j: COEF[:, l, 2, j:j + 1]
        AR8l = COEF[:, l, 3, :]
        AI8l = COEF[:, l, 4, :]
        SP_ = ['SPr', 'SPi']
        dve_cp(SP2[:, 0, :], SC[:, l, :, :].rearrange("p a b -> p (a b)"), ['SC'], ['SPr', 'SPi', ('S2', 0)])

        def xcompute(j):
            c8, jm = j // 4, j % 4
            psr, pkr = nb()
            mm(psr[:, :PT], [(BTs(c8, 0, jm), UC[:, c8, 0:PT])], reads=[rkT, 'UC'], wkey=pkr)
            psi, pki = nb()
            mm(psi[:, :PT], [(BTs(c8, 1, jm), UC[:, c8, 0:PT])], reads=[rkT, 'UC'], wkey=pki)
            act(XR[:, :PT], psr[:, :PT], AF.Copy, reads=[pkr], writes=XRK)
            act(XI[:, :PT], psi[:, :PT], AF.Copy, reads=[pki], writes=XIK)

        def horner(j):
            C_ = ['COEF']
            for jj in range(1, LCH):
                cr, ci = XR[:, jj:PT:LCH], XI[:, jj:PT:LCH]
                pr, pi_ = XR[:, jj - 1:PT:LCH], XI[:, jj - 1:PT:LCH]
                kr, ki, kpr, kpi = ('XR', jj), ('XI', jj), ('XR', jj - 1), ('XI', jj - 1)
                dve_stt(cr, pr, cAR(j), cr, MUL, ADD, C_ + [kpr, kr], [kr])
                dve_stt(ci, pi_, cAR(j), ci, MUL, ADD, C_ + [kpi, ki], [ki])
                dve_stt(cr, pi_, cNAI(j), cr, MUL, ADD, C_ + [kpi, kr], [kr])
                dve_stt(ci, pr, cAI(j), ci, MUL, ADD, C_ + [kpr, ki], [ki])

        for c8 in range(4):
            for hf in range(2):
                slabW, rkW = get_slab()
                for jm2 in range(2):
                    j = 4 * c8 + 2 * hf + jm2
                    for ri in range(2):
                        ps, pk = nb()
                        bb = (jm2 * 2 + ri) * 8
                        mm(ps[:, :NCH], [(slabW[:, (bb + m) * 128:(bb + m + 1) * 128], UC[:, c8, m:PT:LCH]) for m in range(LCH)],
                           reads=[rkW, 'UC'], wkey=pk)
                        act(Z2[:, :, ri * 16 + j], ps[:, :NCH], AF.Copy, reads=[pk], writes=(ZRK if ri == 0 else ZIK))
        def chunk_gen():
            CA2, CB2 = CO2[:, l, 0, :], CO2[:, l, 1, :]
            for c in range(NCH):
                cur = SP2[:, c, :]
                kc = ('S2', c)
                fin = (c == NCH - 1)
                dve_tt(QT[:], cur, CA2, MUL, ['COEF', kc], ['QT'])
                dve_tt(QU[:, 0:16], SP2[:, c, 16:32], CB2[:, 0:16], MUL, ['COEF', kc], ['QU0'])
                dve_tt(QU[:, 16:32], SP2[:, c, 0:16], CB2[:, 16:32], MUL, ['COEF', kc], ['QU1'])
                dve_tt(QW[:], QT[:], Z2[:, c, :], ADD, ['QT'] + ZRK + ZIK, ['QW'])
                dve_tt(SP2[:, c + 1, :], QW[:], QU[:], ADD, ['QW', 'QU0', 'QU1'], [('S2', c + 1)] + (['SPr', 'SPi'] if fin else []))
                yield
        cg = chunk_gen()

        def cpump(n):
            for _ in range(n):
                try:
                    next(cg)
                except StopIteration:
                    return
        act(sca[:], lama[:, l, :], AF.Exp, reads=CONST, writes=['sca'], scale=-1.0)
        act(sca[:], sca[:], AF.Ln, reads=['sca'], writes=['sca'], bias=oneb[:, 0:1])
        dve_ts(sca[:], sca[:], -8.0, None, MUL, None, ['sca'], ['sca'])
        slabA, rkA = get_slab()
        for e_ in range(8):
            dA = lambda k, e_=e_: slabA[:, (e_ * 4 + k) * 128:(e_ * 4 + k + 1) * 128]
            si_ = e_ % 2
            T0, T1, T2, T3 = TMP[4 * si_], TMP[4 * si_ + 1], TMP[4 * si_ + 2], TMP[4 * si_ + 3]
            k0, k1_, k2_, k3_ = ('t', 4 * si_), ('t', 4 * si_ + 1), ('t', 4 * si_ + 2), ('t', 4 * si_ + 3)
            TBe = TB[2] if si_ == 0 else TB[0]
            kb_ = ('tb', 2) if si_ == 0 else ('tb', 0)
            for (c0, n) in ctiles:
                ps, pk = nb()
                if c0 == 0:
                    mm(ps[:, :n], [(dA(k), UA[:, e_, k:k + n]) for k in range(4)], reads=[rkA, ('scr', 16 + e_)], wkey=pk)
                else:
                    mm(ps[:, :n], [(dA(3), UA[:, e_, 3 + c0:3 + c0 + n]),
                                   (prodA[:, l, e_ * 128:(e_ + 1) * 128], sel4[:, :])], reads=[rkA, ('scr', 16 + e_), 'prodA'] + CONST, wkey=pk)
                act(TBe[:, c0:c0 + n], ps[:, :n], AF.Identity, reads=[pk] + CONST, writes=[kb_], bias=bca[:, l, e_:e_ + 1])
                act(T0[:, c0:c0 + n], ps[:, :n], AF.Identity, reads=[pk] + CONST, writes=[k0], bias=bca[:, l, e_:e_ + 1])
                ps2, pk2 = nb()
                mm(ps2[:, :n], [(wrg[:, l, e_, :], TBe[:, c0:c0 + n])], reads=[kb_] + CONST, wkey=pk2)
                ps3, pk3 = nb()
                mm(ps3[:, :n], [(wig[:, l, e_, :], TBe[:, c0:c0 + n])], reads=[kb_] + CONST, wkey=pk3)
                act(T1[:, c0:c0 + n], ps2[:, :n], AF.Sigmoid, reads=[pk2] + CONST, writes=[k1_], bias=brg[:, l, e_:e_ + 1])
                act(T2[:, c0:c0 + n], ps3[:, :n], AF.Sigmoid, reads=[pk3] + CONST, writes=[k2_], bias=big[:, l, e_:e_ + 1])
            act(T1[:, :NC], T1[:, :NC], AF.Exp, reads=[k1_, 'sca'], writes=[k1_], scale=sca[:, e_:e_ + 1])
            dve_tt(T2[:, :NC], T2[:, :NC], T0[:, :NC], MUL, [k2_, k0], [k2_])
            dve_tt(T3[:, :NC], T1[:, :NC], T1[:, :NC], MUL, [k1_], [k3_])
            act(T3[:, :NC], T3[:, :NC], AF.Sqrt, reads=[k3_] + CONST, writes=[k3_], scale=-1.0, bias=oneb[:, 0:1])
            dve_tt(T2[:, :NC], T2[:, :NC], T3[:, :NC], MUL, [k2_, k3_], [k2_])
            S.op('dve', lambda e, e_=e_, T0=T0, T1=T1, T2=T2: e.tensor_tensor_scan(T0[:, 0:PT], T1[:, 0:PT], T2[:, 0:PT], HST[:, l, e_:e_ + 1], MUL, ADD),
                 reads=[k1_, k2_, 'HST', k0], writes=[k0])
            dve_cp(HST[:, l, e_:e_ + 1], T0[:, PT - 1:PT], [k0], ['HST'])
            if p == 0:
                dve_tt(T0[:, PT:NC], T1[:, PT:NC], H0[:, l, e_, :], MUL, [k1_, k0] + CONST, [k0])
                dve_tt(T0[:, PT:NC], T0[:, PT:NC], T2[:, PT:NC], ADD, [k0, k2_], [k0])
                dve_cp(keepH[:, l, e_, :], T0[:, PT:NC], [k0], ['keepH'])
            act(scr[:, e_, :NC], T0[:, :NC], AF.Copy, reads=[k0], writes=[('scr', e_)])
            cpump(NCH // 8)
        for c in range(4):
            slabB, rkB = get_slab()
            dB = lambda k: slabB[:, k * 128:(k + 1) * 128]
            for (c0, n) in ctiles:
                ps, pk = nb()
                if c0 == 0:
                    mm(ps[:, :n], [(dB(k), GLU[:, c, k:k + n]) for k in range(31)], reads=[rkB, 'GLU'], wkey=pk)
                else:
                    mm(ps[:, :n], [(dB(30), GLU[:, c, 30 + c0:30 + c0 + n])] +
                       [(prodB[:, l, kk, c * 128:(c + 1) * 128], sel8[:, :]) for kk in range(4)], reads=[rkB, 'GLU', 'prodB'] + CONST, wkey=pk)
                act(TMP[c][:, c0:c0 + n], ps[:, :n], AF.Identity, reads=[pk] + CONST, writes=[('t', c)], bias=bdb[:, l, c:c + 1])
                if c0 == 0:
                    act(TB[0][:, :n], ps[:, :n], AF.Identity, reads=[pk] + CONST, writes=[('tb', 0)], bias=bdb[:, l, c:c + 1])
                    act(TB[1][:, :n], ps[:, :n], AF.Square, reads=[pk] + CONST, writes=[('tb', 1)], bias=bdb[:, l, c:c + 1])
                    S.op('pe', lambda e, c=c, n=n: e.matmul(pst1[:, :n], onesb[:], TB[0][:, :n], start=(c == 0), stop=(c == 3)),
                         reads=[('tb', 0)] + CONST, writes=['pst1'] if c == 0 else [])
                    S.lastw['pst1'] = (S.echan['pe'], S.echan['pe'].cnt)
                    S.op('pe', lambda e, c=c, n=n: e.matmul(pst2[:, :n], onesb[:], TB[1][:, :n], start=(c == 0), stop=(c == 3)),
                         reads=[('tb', 1)] + CONST, writes=['pst2'] if c == 0 else [])
                    S.lastw['pst2'] = (S.echan['pe'], S.echan['pe'].cnt)
                else:
                    act(CBs[:, c, :], ps[:, :n], AF.Identity, reads=[pk] + CONST, writes=['CBs'], bias=bdb[:, l, c:c + 1])
                    act(CB2s[:, c, :], ps[:, :n], AF.Square, reads=[pk] + CONST, writes=['CBs'], bias=bdb[:, l, c:c + 1])
        for (c0, n) in ctiles:
            if c0 == 0:
                s1, k1, s2, k2 = pst1, 'pst1', pst2, 'pst2'
            else:
                s1, k1 = nb()
                mm(s1[:, :n], [(onesb[:], CBs[:, c, :]) for c in range(4)], reads=['CBs'] + CONST, wkey=k1)
                s2, k2 = nb()
                mm(s2[:, :n], [(onesb[:], CB2s[:, c, :]) for c in range(4)], reads=['CBs'] + CONST, wkey=k2)
            act(TMP[4][:, :n], s1[:, :n], AF.Copy, reads=[k1], writes=[('t', 4)], scale=1.0 / W_B)
            dve_tt(TMP[5][:, :n], TMP[4][:, :n], TMP[4][:, :n], MUL, [('t', 4)], [('t', 5)])
            dve_stt(TMP[5][:, :n], s2[:, :n], 1.0 / W_B, TMP[5][:, :n], MUL, SUB, [k2, ('t', 5)], [('t', 5)])
            act(TMP[5][:, :n], TMP[5][:, :n], AF.Sqrt, reads=[('t', 5)] + CONST, writes=[('t', 5)], bias=epsb[:, 0:1])
            S.op('dve', lambda e, n=n: e.reciprocal(TMP[5][:, :n], TMP[5][:, :n]), reads=[('t', 5)], writes=[('t', 5)])
            for c in range(4):
                dve_tt(TMP[6][:, :n], TMP[c][:, c0:c0 + n], TMP[4][:, :n], SUB, [('t', c), ('t', 4)], [('t', 6)])
                dve_tt(TMP[6][:, :n], TMP[6][:, :n], TMP[5][:, :n], MUL, [('t', 6), ('t', 5)], [('t', 6)])
                act(scr[:, 8 + c, c0:c0 + n], TMP[6][:, :n], AF.Silu, reads=[('t', 6)] + CONST, writes=[('scr', 8 + c)],
                    scale=lng[:, l, c:c + 1], bias=lnb[:, l, c:c + 1])
        cpump(NCH)
        dve_cp(SC[:, l, :, :].rearrange("p a b -> p (a b)"), SP2[:, NCH, :], SP_, ['SC'])
        SPb = S5S[:, :, :, :].rearrange("p a b c -> p (a b c)")[:, 0:2 * 16 * NCH].rearrange("p (j r c) -> p j r c", r=2, c=NCH)
        XSb = S5S[:, :, :, :].rearrange("p a b c -> p (a b c)")[:, 2048:2048 + 32 * NS].rearrange("p (j r c) -> p j r c", r=2, c=NS)
        act(SPb[:, :, 0, :], SP2[:, 0:NCH, 0:16].rearrange("p c j -> p j c"), AF.Copy, reads=SP_, writes=BQK)
        act(SPb[:, :, 1, :], SP2[:, 0:NCH, 16:32].rearrange("p c j -> p j c"), AF.Copy, reads=SP_, writes=BQK)
        for c8 in range(4):
            ps, pk = nb()
            for ph in range(2):
                slabC, rkC = get_slab()
                for pwi in range(4):
                    jp = 4 * ph + pwi
                    pairs = [(slabC[:, ((jm * 2 + ri) * 4 + pwi) * 128:((jm * 2 + ri) * 4 + pwi + 1) * 128], SPb[:, 4 * c8 + jm, ri, :])
                             for jm in range(4) for ri in range(2)]
                    pairs += [(slabC[:, 4096 + m * 128:4096 + (m + 1) * 128], UC[:, c8, (jp - m):PT:LCH]) for m in range(jp + 1)]
                    def f(e, pairs=pairs, jp=jp, ps=ps):
                        last = None
                        for i_, (a_, b_) in enumerate(pairs):
                            last = e.matmul(ps[:, jp:PT:LCH], a_, b_, start=(i_ == 0), stop=(i_ == len(pairs) - 1))
                        return last
                    S.op('pe', f, reads=[rkC, 'UC'] + BQK, writes=[pk] if jp == 0 else [])
                    S.lastw[pk] = (S.echan['pe'], S.echan['pe'].cnt)
            dve_stt(TMP[0][:, :PT], UC[:, c8, 0:PT], dss[:, l, c8:c8 + 1], ps[:, :PT], MUL, ADD, ['UC', pk] + CONST, [('t', 0)])
            act(YG[:, c8, 0:PT], TMP[0][:, :PT], AF.Gelu, reads=[('t', 0)], writes=['YG'])
        slabT, rkT = get_slab()
        slabQ, rkQ = get_slab(held=1)
        if p == 0:
            for c8 in range(4):
                for jm in range(4):
                    j = 4 * c8 + jm
                    psr, pkr = nb()
                    mm(psr[:, :NS], [(BTs(c8, 0, jm), UC[:, c8, PT:PT + NS])], reads=[rkT, 'UC'], wkey=pkr)
                    psi, pki = nb()
                    mm(psi[:, :NS], [(BTs(c8, 1, jm), UC[:, c8, PT:PT + NS])], reads=[rkT, 'UC'], wkey=pki)
                    xs_r, xs_i = XSs[:, l, 0, j, :], XSs[:, l, 1, j, :]
                    RS = CONST + ['COEF', 'XSs']
                    dve_stt(xs_r, S0[:, l, 0, j, :], cAR(j), psr[:, :NS], MUL, ADD, RS + [pkr], ['XSs'])
                    dve_stt(xs_r, S0[:, l, 1, j, :], cNAI(j), xs_r, MUL, ADD, RS, ['XSs'])
                    dve_stt(xs_i, S0[:, l, 1, j, :], cAR(j), psi[:, :NS], MUL, ADD, RS + [pki], ['XSs'])
                    dve_stt(xs_i, S0[:, l, 0, j, :], cAI(j), xs_i, MUL, ADD, RS, ['XSs'])
                    act(XSb[:, j, 0, :], xs_r, AF.Copy, reads=['XSs'], writes=['XSb'])
                    act(XSb[:, j, 1, :], xs_i, AF.Copy, reads=['XSs'], writes=['XSb'])
                ps, pk = nb()
                mm(ps[:, :NS], [(CQs(4 * c8 + jm, ri), XSb[:, 4 * c8 + jm, ri, :]) for jm in range(4) for ri in range(2)],
                   reads=[rkQ, 'XSb'], wkey=pk)
                dve_stt(TMP[0][:, :NS], UC[:, c8, PT:PT + NS], dss[:, l, c8:c8 + 1], ps[:, :NS], MUL, ADD, ['UC', pk] + CONST, [('t', 0)])
                act(YG[:, c8, PT:PT + NS], TMP[0][:, :NS], AF.Gelu, reads=[('t', 0)], writes=['YG'])
        slab, rk = get_slab()
        for c in range(4):
            for (c0, n) in ctiles:
                ps, pk = nb()
                mm(ps[:, :n], [(slab[:, k * 512 + c * 128:k * 512 + c * 128 + 128], YG[:, k, c0:c0 + n]) for k in range(4)], reads=[rk, 'YG'], wkey=pk)
                act(TMP[1][:, :n], ps[:, :n], AF.Sigmoid, reads=[pk] + CONST, writes=[('t', 1)], bias=bglu[:, l, c:c + 1])
                dve_tt(scr[:, 12 + c, c0:c0 + n], YG[:, c, c0:c0 + n], TMP[1][:, :n], MUL, ['YG', ('t', 1)], [('scr', 12 + c)])
        for d in range(8):
            slab, rk = get_slab()
            for (c0, n) in ctiles:
                specs = [(0, 3072, 8, 0, 0), (1024, 4096, 4, 8, 8), (2048, 4608, 4, 12, 16)]
                for bi_, (goff, poff, KC, sbase, gb) in enumerate(specs):
                    psg, pkg = nb()
                    mm(psg[:, :n], [(slab[:, goff + k * 128:goff + (k + 1) * 128], hbf[:, k, c0:c0 + n]) for k in range(8)], reads=[rk] + HK, wkey=pkg)
                    gt = TMP[0] if bi_ == 0 else TMP[2]
                    gk = ('t', 0) if bi_ == 0 else ('t', 2)
                    act(gt[:, :n], psg[:, :n], AF.Sigmoid, reads=[pkg] + CONST, writes=[gk], bias=bgate[:, l, gb + d:gb + d + 1])
                    psp, pkp = nb()
                    mm(psp[:, :n], [(slab[:, poff + k * 128:poff + (k + 1) * 128], scr[:, sbase + k, c0:c0 + n]) for k in range(KC)],
                       reads=[rk] + [('scr', sbase + k) for k in range(KC)], wkey=pkp)
                    if bi_ == 0:
                        dve_tt(TMP[1][:, :n], TMP[0][:, :n], psp[:, :n], MUL, [gk, pkp], [('t', 1)])
                    else:
                        dve_tt(TMP[3][:, :n], TMP[2][:, :n], psp[:, :n], MUL, [gk, pkp], [('t', 3)])
                        if bi_ == 1:
                            dve_tt(TMP[1][:, :n], TMP[1][:, :n], TMP[3][:, :n], ADD, [('t', 1), ('t', 3)], [('t', 1)])
                        else:
                            dve_tt(scr[:, 16 + d, c0:c0 + n], TMP[1][:, :n], TMP[3][:, :n], ADD, [('t', 1), ('t', 3)], [('scr', 16 + d)])
        for s_ in range(2):
            slab, rk = get_slab()
            for c4 in range(4):
                d = s_ * 4 + c4
                for (c0, n) in ctiles:
                    ps, pk = nb()
                    mm(ps[:, :n], [(slab[:, k * 512 + c4 * 128:k * 512 + c4 * 128 + 128], scr[:, 16 + k, c0:c0 + n]) for k in range(8)],
                       reads=[rk] + [('scr', 16 + k) for k in range(8)], wkey=pk)
                    dve_tt(xres[:, d, c0:c0 + n], xres[:, d, c0:c0 + n], ps[:, :n], ADD, [('x', d), pk], [('x', d)])
        rmsnorm(ctiles, gffn[:, l, :], 'h')
        for s_ in range(11):
            slab, rk = get_slab()
            for c4 in range(4):
                ci_ = s_ * 4 + c4
                for (c0, n) in ctiles:
                    ps, pk = nb()
                    mm(ps[:, :n], [(slab[:, k * 512 + c4 * 128:k * 512 + c4 * 128 + 128], hbf[:, k, c0:c0 + n]) for k in range(8)],
                       reads=[rk] + HK, wkey=pk)
                    if ci_ < 22:
                        act(scr[:, ci_, c0:c0 + n], ps[:, :n], AF.Silu, reads=[pk], writes=[('scr', ci_)])
                    else:
                        f_ = ci_ - 22
                        dve_tt(scr[:, f_, c0:c0 + n], scr[:, f_, c0:c0 + n], ps[:, :n], MUL, [('scr', f_), pk], [('scr', f_)])
        for d in range(8):
            slab, rk = get_slab()
            for (c0, n) in ctiles:
                ps, pk = nb()
                mm(ps[:, :n], [(slab[:, k * 128:(k + 1) * 128], scr[:, k, c0:c0 + n]) for k in range(22)],
                   reads=[rk] + [('scr', k) for k in range(22)], wkey=pk)
                dve_tt(xres[:, d, c0:c0 + n], xres[:, d, c0:c0 + n], ps[:, :n], ADD, [('x', d), pk], [('x', d)])
        def tr_out(src_fn, nblk, nrows, dst_ap):
            for g0 in range(0, nblk, 4):
                ps, pk = nb()
                gn = min(4, nblk - g0)
                def f(e, ps=ps, g0=g0, gn=gn):
                    last = None
                    for i in range(gn):
                        last = e.transpose(ps[0:nrows, i * 128:(i + 1) * 128], src_fn(g0 + i), identf[:])
                    return last
                S.op('pe', f, reads=KEEPK + CONST, writes=[pk])
                act(tokbuf[0:nrows, g0 * 128:(g0 + gn) * 128], ps[0:nrows, 0:gn * 128], AF.Copy, reads=[pk], writes=['tokbuf'])
            S.dma('sp', dst_ap, tokbuf[0:nrows, 0:nblk * 128], c_tok, reads=['tokbuf'])

        KEEPK = ['keepAs', 'keepH', 'keepBs', 'XSs', 'keepA', 'keepB', 'HST', 'SC']
        if p == 0:
            S.dma('sp', O['s_lru_conv'][l, :, 0:2, :], I['st_lru_conv'][l, :, 1:3, :], c_out)
            S.dma('sp', O['s_cfm_conv'][l, :, 0:29, :], I['st_cfm_conv'][l, :, 1:30, :], c_out)
            tr_out(lambda i: keepAs[:, l, i, :], 8, NS, O['s_lru_conv'][l, :, 2, :])
            tr_out(lambda i: keepH[:, l, i, :], 8, NS, O['s_lru_h'][l, :, :])
            tr_out(lambda i: keepBs[:, l, i, :], 4, NS, O['s_cfm_conv'][l, :, 29, :])
            for ri, nm in enumerate(['s_ssm_re', 's_ssm_im']):
                dstv = O[nm][l].rearrange("b g p -> b (g p)")
                for hf in range(2):
                    tr_out(lambda i, ri=ri, hf=hf: XSs[:, l, ri, hf * 8 + i, :], 8, NS, dstv[:, hf * 1024:(hf + 1) * 1024])
        if last:
            tr_out(lambda i: keepA[:, l, i, :], 8, 3, O['p_lru_conv'][l, :, :])
            tr_out(lambda i: keepB[:, l, i, :], 4, 30, O['p_cfm_conv'][l, :, :])
            tr_out(lambda i: HST[:, l, :], 1, 8, O['p_lru_h'][l].rearrange("(e p) -> e p", p=128))
            tr_out(lambda i: SC[:, l, 0, :], 1, 16, O['p_ssm_re'][l].rearrange("(j two) p -> j (two p)", two=2))
            tr_out(lambda i: SC[:, l, 1, :], 1, 16, O['p_ssm_im'][l].rearrange("(j two) p -> j (two p)", two=2))

    def final_out(p):
        ctiles = [(0, PT)] + ([(PT, NS)] if p == 0 else [])
        for (c0, n) in ctiles:
            ps, pk = nb()
            for k in range(8):
                tb = TB[k % 2]
                act(tb[:, :n], xres[:, k, c0:c0 + n], AF.Square, reads=[('x', k)], writes=[('tb', k % 2)])
                S.op('pe', lambda e, k=k, tb=tb, ps=ps, n=n: e.matmul(ps[:, :n], onesb[:], tb[:, :n], start=(k == 0), stop=(k == 7)),
                     reads=[('tb', k % 2)] + CONST, writes=[pk] if k == 0 else [])
                S.lastw[pk] = (S.echan['pe'], S.echan['pe'].cnt)
            r = TMP[7]
            act(r[:, :n], ps[:, :n], AF.Sqrt, reads=[pk] + CONST, writes=['t7'], scale=1.0 / D, bias=epsb[:, 0:1])
            S.op('dve', lambda e, r=r, n=n: e.reciprocal(r[:, :n], r[:, :n]), reads=['t7'], writes=['t7'])
            for k in range(8):
                dve_stt(xres[:, k, c0:c0 + n], xres[:, k, c0:c0 + n], gfin[:, k:k + 1], r[:, :n], MUL, MUL,
                        reads=[('x', k), 't7'] + CONST, writes=[('x', k)])
        XK = [('x', k) for k in range(8)]
        for tt in range(4):
            for half in range(2):
                ps, pk = nb()
                def f(e, ps=ps, half=half, tt=tt):
                    last = None
                    for kk in range(4):
                        k = half * 4 + kk
                        last = e.transpose(ps[:, kk * 128:(kk + 1) * 128], xres[:, k, tt * 128:(tt + 1) * 128], identf[:])
                    return last
                S.op('pe', f, reads=XK + CONST, writes=[pk])
                act(tokout[:, half * 512:(half + 1) * 512], ps[:, :], AF.Copy, reads=[pk], writes=['tokbuf'])
            t0 = p * PT + tt * 128
            S.dma('sp', O['y_p'][t0:t0 + 128, :], tokout[:], c_tok, reads=['tokbuf'])
        if p == 0:
            for half in range(2):
                ps, pk = nb()
                def f(e, ps=ps, half=half):
                    last = None
                    for kk in range(4):
                        k = half * 4 + kk
                        last = e.transpose(ps[0:NS, kk * 128:(kk + 1) * 128], xres[:, k, PT:PT + NS], identf[:])
                    return last
                S.op('pe', f, reads=XK + CONST, writes=[pk])
                act(tokout[0:NS, half * 512:(half + 1) * 512], ps[0:NS, :], AF.Copy, reads=[pk], writes=['tokbuf'])
            S.dma('sp', O['y_s'][:, :], tokout[0:NS, :], c_tok, reads=['tokbuf'])

    emit_const_loads()

    def tr_in(dst_fn, src_ap, ncols):
        S.dma('sp', tokbuf[0:NS, 0:ncols], src_ap, c_in, writes=['tokbuf'])
        nblk = ncols // 128
        for g0 in range(0, nblk, 4):
            ps, pk = nb()
            gn = min(4, nblk - g0)
            def f(e, ps=ps, g0=g0, gn=gn):
                last = None
                for i in range(gn):
                    last = e.transpose(ps[:, i * NS:(i + 1) * NS], tokbuf[0:NS, (g0 + i) * 128:(g0 + i + 1) * 128], identf[0:NS, 0:NS])
                return last
            S.op('pe', f, reads=['tokbuf'] + CONST, writes=[pk])
            for i in range(gn):
                act(dst_fn(g0 + i), ps[:, i * NS:(i + 1) * NS], AF.Copy, reads=[pk], writes=['H0S0'])

    for l in range(DEPTH):
        tr_in(lambda i, l=l: H0[:, l, i, :], I['st_lru_h'][l], 1024)
        for ri, nm in enumerate(['st_ssm_re', 'st_ssm_im']):
            srcv = I[nm][l].rearrange("b g p -> b (g p)")
            for hf in range(2):
                tr_in(lambda i, l=l, ri=ri, hf=hf: S0[:, l, ri, hf * 8 + i, :], srcv[:, hf * 1024:(hf + 1) * 1024], 1024)
    CONST.append('H0S0')
    cgen['g'] = convert_gen(0)
    tables(0)
    cgen['g'] = convert_gen(1)
    tables(1)
    S.barrier_keys(['BT'], ['ZR', 'ZI'])
    S.barrier_keys(STGK + ['stg'], [('t', i_) for i_ in range(8)] + [('scr', k_) for k_ in range(24)])
    S.barrier_keys(['s5in', 's5p', 'CQ', 'BT', 'pbf'] + BQK, [('x', k_) for k_ in range(8)])
    for p in range(NPASS):
        x_load(p)
        for l in range(DEPTH):
            layer(p, l)
        final_out(p)
    S.finish([c_out, c_tok])
    return nc


_CACHE = {}


def kernel(**inputs):
    n = 8
    consts = {
        'identf': np.eye(128, dtype=np.float32),
        'onesf': np.ones((128, 128), dtype=np.float32),
        'sel4f': np.repeat(np.eye(NS, dtype=np.float32), 4, axis=0),
        'sel8f': np.repeat(np.eye(NS, dtype=np.float32), 8, axis=0),
    }
    in_maps = []
    for i in range(n):
        m = {}
        m['xp'] = np.ascontiguousarray(inputs['x_prompt'][i])
        m['xs'] = np.ascontiguousarray(inputs['x_sample'][NS * i:NS * (i + 1), 0, :])
        m['st_lru_conv'] = np.ascontiguousarray(inputs['state_lru_conv'][:, NS * i:NS * (i + 1)])
        m['st_lru_h'] = np.ascontiguousarray(inputs['state_lru_h'][:, NS * i:NS * (i + 1)])
        m['st_cfm_conv'] = np.ascontiguousarray(inputs['state_cfm_conv'][:, NS * i:NS * (i + 1)])
        m['st_ssm_re'] = np.ascontiguousarray(inputs['state_ssm_re'][:, NS * i:NS * (i + 1)])
        m['st_ssm_im'] = np.ascontiguousarray(inputs['state_ssm_im'][:, NS * i:NS * (i + 1)])
        for k in PARAMS:
            m[k] = np.ascontiguousarray(np.asarray(inputs[k], dtype=np.float32))
        m.update(consts)
        in_maps.append(m)
    shapes = {k: v.shape for k, v in in_maps[0].items()}
    if 'nc' not in _CACHE:
        _CACHE['nc'] = build_program(shapes)
    nc = _CACHE['nc']
    res = run_bass_kernel_spmd(nc, in_maps, core_ids=list(range(n)))
    R = res.results
    cat = lambda name, ax: np.concatenate([np.asarray(r[name]) for r in R], axis=ax)
    y_prompt = np.stack([np.asarray(r['y_p']) for r in R], axis=0)
    y_sample = cat('y_s', 0)[:, None, :]
    outs = [y_prompt, y_sample]
    for nm in ['p_lru_conv', 'p_lru_h', 'p_cfm_conv', 'p_ssm_re', 'p_ssm_im']:
        outs.append(np.stack([np.asarray(r[nm]) for r in R], axis=1))
    for nm in ['s_lru_conv', 's_lru_h', 's_cfm_conv', 's_ssm_re', 's_ssm_im']:
        outs.append(cat(nm, 1))
    return tuple(np.ascontiguousarray(o.astype(np.float32)) for o in outs)
```

```python
import contextlib
import numpy as np
import concourse.bass as bass
import concourse.mybir as mybir
from concourse.bass_utils import run_bass_kernel_spmd

F32 = mybir.dt.float32
BF16 = mybir.dt.bfloat16
AF = mybir.ActivationFunctionType
ALU = mybir.AluOpType


class Chan:
    def __init__(self, sem, name):
        self.sem = sem
        self.cnt = 0
        self.name = name


class Sched:
    def __init__(self, nc, es):
        self.nc = nc
        self.es = es
        self.names = ['pe', 'act', 'dve', 'pool', 'sp']
        self.echan = {n: self.chan('e_' + n) for n in self.names}
        self.prog = {n: [] for n in self.names}
        self.seen = {n: {} for n in self.names}
        self.lastw = {}
        self.readers = {}
        self.nins = 0
        self.snap = {}

    def chan(self, name):
        sem = self.es.enter_context(self.nc.semaphore(name))
        return Chan(sem, name)

    def sbuf(self, name, shape, dtype):
        return self.es.enter_context(self.nc.sbuf_tensor('sb_' + name, list(shape), dtype))

    def psum(self, name, shape, dtype):
        return self.es.enter_context(self.nc.psum_tensor('ps_' + name, list(shape), dtype))

    def _deps(self, ename, reads, writes):
        own = self.echan[ename]
        need = {}

        def add(c, v):
            if need.get(c, 0) < v:
                need[c] = v

        for k in reads:
            lw = self.lastw.get(k)
            if lw is not None:
                if lw[0] is own and ename == 'pe':
                    continue
                add(*lw)
        for k in writes:
            lw = self.lastw.get(k)
            if lw is not None:
                if not (lw[0] is own and ename == 'pe'):
                    add(*lw)
            for c, v in self.readers.get(k, {}).items():
                if c is own:
                    continue
                add(c, v)
        out = []
        seen = self.seen[ename]
        items = sorted(need.items(), key=lambda cv: -cv[1])
        for c, v in items:
            if seen.get(c, 0) < v:
                seen[c] = v
                out.append((c.sem, v))
                sn = self.snap.get((c, v))
                if sn is not None and c is not own:
                    for c2, v2 in sn.items():
                        if c2 is not own and seen.get(c2, 0) < v2:
                            seen[c2] = v2
        return out

    def _record(self, reads, writes, c, v):
        for k in writes:
            self.lastw[k] = (c, v)
            self.readers[k] = {}
        for k in reads:
            self.readers.setdefault(k, {})[c] = v

    def barrier_keys(self, src_keys, dst_keys):
        for d in dst_keys:
            rd = self.readers.setdefault(d, {})
            for k in src_keys:
                lw = self.lastw.get(k)
                if lw is not None and rd.get(lw[0], 0) < lw[1]:
                    rd[lw[0]] = lw[1]
                for c, v in self.readers.get(k, {}).items():
                    if rd.get(c, 0) < v:
                        rd[c] = v

    def op(self, ename, fn, reads=(), writes=()):
        waits = self._deps(ename, reads, writes)
        c = self.echan[ename]
        c.cnt += 1
        sem = c.sem

        embed = ename in ('act', 'dve', 'pool') and len(waits) > 0

        def run(e):
            ws = waits[1:] if embed else waits
            for s, v in ws:
                e.wait_ge(s, v)
            ins = fn(e)
            if embed:
                ins._wait_ge(waits[0][0], waits[0][1])
            ins.then_inc(sem, 1)

        self.prog[ename].append(run)
        self._record(reads, writes, c, c.cnt)
        sn = dict(self.seen[ename])
        sn.pop(c, None)
        self.snap[(c, c.cnt)] = sn
        self.nins += 1

    def dma(self, ename, out, in_, chan, reads=(), writes=(), **kw):
        waits = self._deps(ename, reads, writes)
        chan.cnt += 16
        sem = chan.sem

        def run(e):
            for s, v in waits:
                e.wait_ge(s, v)
            e.dma_start(out=out, in_=in_, **kw).then_inc(sem, 16)

        self.prog[ename].append(run)
        self._record(reads, writes, chan, chan.cnt)
        sn = dict(self.seen[ename])
        sn.pop(self.echan[ename], None)
        self.snap[(chan, chan.cnt)] = sn
        self.nins += 1

    def finish(self, chans):
        fw = [(c.sem, c.cnt) for c in chans if c.cnt > 0]

        def run(e):
            for s, v in fw:
                e.wait_ge(s, v)

        self.prog['sp'].append(run)
        prog = self.prog
        with self.nc.Block() as block:
            @block.sync
            def _(e):
                for r in prog['sp']:
                    r(e)

            @block.tensor
            def _(e):
                for r in prog['pe']:
                    r(e)

            @block.scalar
            def _(e):
                for r in prog['act']:
                    r(e)

            @block.vector
            def _(e):
                for r in prog['dve']:
                    r(e)

            @block.gpsimd
            def _(e):
                for r in prog['pool']:
                    r(e)
        self.es.close()


D = 1024
SEQ = 2048
DEPTH = 2
NS = 16
W_B = 512
W_C = 512
IN_W = 5632
D_FF = 2816
O1, O2, O3 = 1024, 2048, 2560
EPS = 1e-6
NPASS = 4
PT = SEQ // NPASS
LCH = 8
NCH = PT // LCH
PI = float(np.pi)

PARAMS = ['g_mix', 'w_in', 'w_conv_a', 'b_conv_a', 'w_rg', 'b_rg', 'w_ig', 'b_ig', 'lam_a',
          'w_dw_b', 'b_dw_b', 'ln_g_b', 'ln_b_b', 'lam_re', 'lam_im', 'log_dt', 'b_ssm_re', 'b_ssm_im',
          'c_ssm_re', 'c_ssm_im', 'd_ssm', 'w_glu_c', 'b_glu_c', 'b_gate', 'w_pa', 'w_pb', 'w_pc',
          'w_out', 'g_ffn', 'w_ffn_in', 'w_ffn_out', 'g_final']


def build_program(shapes):
    nc = bass.Bass("TRN2", target_bir_lowering=False)
    es = contextlib.ExitStack()
    S = Sched(nc, es)
    I = {}
    for k, shp in shapes.items():
        I[k] = nc.dram_tensor(k, list(shp), F32, kind="ExternalInput").ap()
    O = {}

    def outp(name, shp):
        O[name] = nc.dram_tensor(name, list(shp), F32, kind="ExternalOutput").ap()

    outp('y_p', [SEQ, D]); outp('y_s', [NS, D])
    outp('p_lru_conv', [DEPTH, 3, D]); outp('p_lru_h', [DEPTH, D]); outp('p_cfm_conv', [DEPTH, 30, W_B])
    outp('p_ssm_re', [DEPTH, 32, 64]); outp('p_ssm_im', [DEPTH, 32, 64])
    outp('s_lru_conv', [DEPTH, NS, 3, D]); outp('s_lru_h', [DEPTH, NS, D]); outp('s_cfm_conv', [DEPTH, NS, 30, W_B])
    outp('s_ssm_re', [DEPTH, NS, 32, 64]); outp('s_ssm_im', [DEPTH, NS, 32, 64])

    NCM = PT + NS
    CBK = 1024
    NSTG = 4
    sb = S.sbuf
    xres = sb('xres', [128, 8, NCM], F32)
    hbf = sb('hbf', [128, 8, NCM], BF16)
    scr = sb('scr', [128, 24, NCM + 4], BF16)
    UA = scr[:, 16:24, :]
    GLU = sb('GLU', [128, 4, 30 + NCM], BF16)
    UC = sb('UC', [128, 4, NCM], BF16)
    YG = sb('YG', [128, 4, NCM], BF16)
    NTMP = 8
    tmpA = sb('tmpA', [128, NTMP, NCM], F32)
    TMP = [tmpA[:, i, :] for i in range(NTMP)]
    TB = [sb('tb%d' % i, [128, NCM], BF16) for i in range(3)]
    RSLOT = 5120
    NRING = 4
    ring = [sb('ring%d' % i, [128, RSLOT], BF16) for i in range(NRING)]
    ring_ch = [S.chan('ring%d' % i) for i in range(NRING)]
    identf = sb('identf', [128, 128], F32)
    identb = sb('identb', [128, 128], BF16)
    onesb = sb('onesb', [128, 128], BF16)
    sel4 = sb('sel4', [64, NS], BF16)
    sel8 = sb('sel8', [128, NS], BF16)
    gmix = sb('gmix', [128, DEPTH, 8], F32); gffn = sb('gffn', [128, DEPTH, 8], F32); gfin = sb('gfin', [128, 8], F32)
    wca = sb('wca', [128, DEPTH, 4, 8], F32); bca = sb('bca', [128, DEPTH, 8], F32)
    brg = sb('brg', [128, DEPTH, 8], F32); big = sb('big', [128, DEPTH, 8], F32); lama = sb('lama', [128, DEPTH, 8], F32)
    wdb = sb('wdb', [128, DEPTH, 31, 4], F32); bdb = sb('bdb', [128, DEPTH, 4], F32)
    lng = sb('lng', [128, DEPTH, 4], F32); lnb = sb('lnb', [128, DEPTH, 4], F32)
    dss = sb('dss', [128, DEPTH, 4], F32); bglu = sb('bglu', [128, DEPTH, 4], F32); bgate = sb('bgate', [128, DEPTH, 24], F32)
    wrg = sb('wrg', [128, DEPTH, 8, 128], BF16); wig = sb('wig', [128, DEPTH, 8, 128], BF16)
    HST = sb('HST', [128, DEPTH, 8], F32)
    SP2 = sb('SP2', [128, NCH + 1, 32], F32); SC = sb('SC', [128, DEPTH, 2, 16], F32)
    CO2 = sb('CO2', [128, DEPTH, 2, 32], F32)
    QT = sb('QT', [128, 32], F32); QU = sb('QU', [128, 32], F32); QW = sb('QW', [128, 32], F32)
    _sf = scr[:, :, :].rearrange("p a b -> p (a b)")
    stB = _sf[:, 4096:6144].rearrange("p (a b) -> p a b", a=4); wrepB = _sf[:, 6144:8192].rearrange("p (a b) -> p a b", a=4)
    stA = _sf[0:64, 8192:9216]; wrepA = _sf[0:64, 9216:10240]
    prodA = sb('prodA', [64, DEPTH, D], BF16); prodB = sb('prodB', [128, DEPTH, 4, W_B], BF16)
    H0 = sb('H0', [128, DEPTH, 8, NS], F32)
    S0 = sb('S0', [128, DEPTH, 2, 16, NS], F32)
    LR = sb('LR', [128, 16], F32); LI = sb('LI', [128, 16], F32); DT = sb('DT', [128, 16], F32)
    AR = sb('AR', [128, 16], F32); AI = sb('AI', [128, 16], F32); NAI = sb('NAI', [128, 16], F32)
    AR8 = sb('AR8', [128, 16], F32); AI8 = sb('AI8', [128, 16], F32)
    QR = sb('QR', [128, 16], F32); QI = sb('QI', [128, 16], F32)
    P1 = sb('P1', [128, 16], F32); P2 = sb('P2', [128, 16], F32); P3 = sb('P3', [128, 16], F32); P4 = sb('P4', [128, 16], F32); P5 = sb('P5', [128, 16], F32)
    _xf = xres[:, :, :].rearrange("p a b -> p (a b)")
    BRq, BIq, CRq, CIq, BBR, BBI, BT1 = [_xf[:, i_ * 256:(i_ + 1) * 256].rearrange("p (j c) -> p j c", c=16) for i_ in range(7)]
    CQpad = _xf[:, 1792:3840].bitcast(BF16).rearrange("p (j r m) -> p j r m", r=2, m=128)
    BTf = sb('BTf', [128, 2048], F32)
    BT = BTf[:, :].bitcast(BF16).rearrange("p (a b c) -> p a b c", a=4, b=2)
    stgf = [tmpA[:, :, :].rearrange("p a b -> p (a b)")[:, i * CBK:(i + 1) * CBK] for i in range(NSTG)]
    stgb = [scr[:, :, :].rearrange("p a b -> p (a b)")[:, i * CBK:(i + 1) * CBK] for i in range(NSTG)]
    Z2 = BTf[:, 0:32 * NCH].rearrange("p (c k) -> p c k", k=32)
    ZRK = ['ZR']; ZIK = ['ZI']
    S5S = sb('S5S', [128, 4, 2, NCM], BF16)
    BQpad = S5S[:, :, :, :].rearrange("p a b c -> p (a b c)")[:, 0:4096].rearrange("p (r j m) -> p r j m", r=2, j=16)
    BQK = [('S5S', jm) for jm in range(4)]
    dstg = S5S[:, :, :, :].rearrange("p a b c -> p (a b c)")[:, 0:4096]
    XR = TMP[5]; XI = TMP[6]
    XRK = [('t', 5)] + [('XR', jj) for jj in range(LCH)]; XIK = [('t', 6)] + [('XI', jj) for jj in range(LCH)]
    XSs = sb('XSs', [128, DEPTH, 2, 16, NS], F32)
    PWR = sb('PWR', [128, 9, 16], F32); PWI = sb('PWI', [128, 9, 16], F32); NPW = sb('NPW', [128, 9, 16], F32)
    WT = [sb('WT%d' % i, [128, 128], BF16) for i in range(2)]
    WT2 = [sb('WT2%d' % i, [128, 128], BF16) for i in range(2)]
    WTw = [sb('WTw%d' % i, [128, 256], BF16) for i in range(2)]
    WT4 = [sb('WT4%d' % i, [128, 128], BF16) for i in range(4)]
    COEF = sb('COEF', [128, DEPTH, 5, 16], F32)
    CBs = sb('CBs', [128, 4, NS], BF16); CB2s = sb('CB2s', [128, 4, NS], BF16)
    sca = sb('sca', [128, 8], F32)
    tokbuf = sb('tokbuf', [128, D], F32)
    keepA = sb('keepA', [128, DEPTH, 8, 3], F32); keepB = sb('keepB', [128, DEPTH, 4, 30], F32)
    keepAs = sb('keepAs', [128, DEPTH, 8, NS], F32); keepBs = sb('keepBs', [128, DEPTH, 4, NS], F32); keepH = sb('keepH', [128, DEPTH, 8, NS], F32)
    UAh = sb('UAh', [128, DEPTH, 8, 3], BF16); GLUh = sb('GLUh', [128, DEPTH, 4, 30], BF16)
    Q1 = sb('Q1', [128, 16], F32); Q2 = sb('Q2', [128, 16], F32); Q3 = sb('Q3', [128, 16], F32); Q4 = sb('Q4', [128, 16], F32)
    tokout = tokbuf
    NB = 5
    pst1 = S.psum('pst1', [128, 512], F32); pst2 = S.psum('pst2', [128, 512], F32)
    banks = [S.psum('pb%d' % i, [128, 512], F32) for i in range(NB)]
    pbf = S.psum('pbf', [128, 1024], BF16)
    bank_i = [0]

    def nb():
        i = bank_i[0] % NB
        bank_i[0] += 1
        return banks[i], ('ps', i)

    c_const = S.chan('c_const')
    c_out = S.chan('c_out')
    c_in = S.chan('c_in')

    cl = []

    def cload(dst, src, key):
        cl.append((dst, src, key))

    cload(identf[:], I['identf'], 'identf'); cload(sel4[:], None, None) if False else None
    for l in range(DEPTH):
        cload(gmix[:, l, :], I['g_mix'][l].rearrange("(k p) -> p k", p=128), 'par')
        cload(gffn[:, l, :], I['g_ffn'][l].rearrange("(k p) -> p k", p=128), 'par')
        for k in range(4):
            cload(wca[:, l, k, :], I['w_conv_a'][l, k].rearrange("(k p) -> p k", p=128), 'par')
        cload(bca[:, l, :], I['b_conv_a'][l].rearrange("(k p) -> p k", p=128), 'par')
        cload(brg[:, l, :], I['b_rg'][l].rearrange("(k p) -> p k", p=128), 'par')
        cload(big[:, l, :], I['b_ig'][l].rearrange("(k p) -> p k", p=128), 'par')
        cload(lama[:, l, :], I['lam_a'][l].rearrange("(k p) -> p k", p=128), 'par')
        cload(wdb[:, l, :, :], I['w_dw_b'][l].rearrange("t (k p) -> p t k", p=128), 'par')
        cload(bdb[:, l, :], I['b_dw_b'][l].rearrange("(k p) -> p k", p=128), 'par')
        cload(lng[:, l, :], I['ln_g_b'][l].rearrange("(k p) -> p k", p=128), 'par')
        cload(lnb[:, l, :], I['ln_b_b'][l].rearrange("(k p) -> p k", p=128), 'par')
        cload(dss[:, l, :], I['d_ssm'][l].rearrange("(k p) -> p k", p=128), 'par')
        cload(bglu[:, l, :], I['b_glu_c'][l].rearrange("(k p) -> p k", p=128), 'par')
        cload(bgate[:, l, :], I['b_gate'][l].rearrange("(k p) -> p k", p=128), 'par')
    cload(gfin[:], I['g_final'].rearrange("(k p) -> p k", p=128), 'par')
    cl = [c for c in cl if c is not None]
    S.op('pool', lambda e: e.memset(stA[:], 0.0), writes=['stg'])
    S.op('pool', lambda e: e.memset(wrepA[:], 0.0), writes=['stg'])
    S.op('pool', lambda e: e.memset(stB[:], 0.0), writes=['stg'])
    S.op('pool', lambda e: e.memset(wrepB[:], 0.0), writes=['stg'])
    c_csw = S.chan('c_csw')
    CONST = ['const', 'const2']

    def emit_const_loads():
        for dst, src, key in cl:
            if key == 'sw':
                S.dma('pool', dst, src, c_csw, allow_slow_non_contiguous=True)
            else:
                S.dma('sp', dst, src, c_const, allow_slow_non_contiguous=True)
        S.dma('pool', identb[:], I['identf'], c_csw)
        S.dma('pool', onesb[:], I['onesf'], c_csw)
        S.dma('pool', sel4[:], I['sel4f'], c_csw)
        S.dma('pool', sel8[:], I['sel8f'], c_csw)
        for l in range(DEPTH):
            S.dma('pool', wrg[:, l, :, :], I['w_rg'][l].rearrange("h i j -> i h j"), c_csw)
            S.dma('pool', wig[:, l, :, :], I['w_ig'][l].rearrange("h i j -> i h j"), c_csw)
        S.lastw['const'] = (c_const, c_const.cnt)
        S.readers['const'] = {}
        S.lastw['const2'] = (c_csw, c_csw.cnt)
        S.readers['const2'] = {}

    NSL = 58
    wsc = nc.dram_tensor("wsc", [DEPTH, NSL, 128, RSLOT], BF16, kind="Internal").ap()
    c_wtD = S.chan('c_wtD'); c_wtB = S.chan('c_wtB'); c_wtQ = S.chan('c_wtQ'); c_wtK = S.chan('c_wtK')
    WTCH = [c_wtD, c_wtB, c_wtQ, c_wtK]
    c_ws = [S.chan('c_ws%d' % i) for i in range(NSTG)]
    c_stg = [S.chan('c_stg%d' % i) for i in range(NSTG)]

    def layer_slabs():
        sl = []
        for s_ in range(5):
            sl.append([(0, 8, 512, 'w_in', s_ * 512)])
        for i_ in range(8):
            sl.append(('WZ', i_))
        sl.append('diagA')
        for c in range(4):
            sl.append(('diagB', c))
        for i_ in range(8):
            sl.append(('CA', i_))
        sl.append('BT')
        sl.append('CQ')
        sl.append([(0, 4, 512, 'w_glu_c', 0)])
        for d in range(8):
            sl.append([(0, 8, 128, 'w_in', O3 + d * 128), (1024, 8, 128, 'w_in', O3 + 1024 + d * 128),
                       (2048, 8, 128, 'w_in', O3 + 2048 + d * 128), (3072, 8, 128, 'w_pa', d * 128),
                       (4096, 4, 128, 'w_pb', d * 128), (4608, 4, 128, 'w_pc', d * 128)])
        for s_ in range(2):
            sl.append([(0, 8, 512, 'w_out', s_ * 512)])
        for s_ in range(11):
            sl.append([(0, 8, 512, 'w_ffn_in', s_ * 512)])
        for d in range(8):
            sl.append([(0, 22, 128, 'w_ffn_out', d * 128)])
        return sl

    LSL = layer_slabs()
    SIDX = {(sp_ if not isinstance(sp_, list) else None): i_ for i_, sp_ in enumerate(LSL)}
    assert len(LSL) == NSL
    SLAB_N = []
    for sp_ in LSL:
        if isinstance(sp_, list):
            SLAB_N.append(max(off + KC * W for (off, KC, W, _, _) in sp_))
        elif sp_ == 'diagB' or (isinstance(sp_, tuple) and sp_[0] == 'diagB'):
            SLAB_N.append(31 * 128)
        elif isinstance(sp_, tuple) and sp_[0] == 'CA':
            SLAB_N.append(5120)
        else:
            SLAB_N.append(4096)
    slabs = [(l, si) for p in range(NPASS) for l in range(DEPTH) for si in range(NSL)]
    rs = {'issued': 0, 'next': 0}

    def issue_slab():
        n = rs['issued']
        if n >= len(slabs):
            return
        slot = n % NRING
        l, si = slabs[n]
        ne = SLAB_N[si]
        S.dma('sp', ring[slot][:, 0:ne], wsc[l, si, :, 0:ne], ring_ch[slot], reads=[('wscL', l, i_) for i_ in range(NSTG + 4)], writes=[('ring', slot)])
        rs['issued'] += 1

    def get_slab(held=0):
        n = rs['next']
        while rs['issued'] < min(n + NRING - held, len(slabs)):
            issue_slab()
        rs['next'] += 1
        slot = n % NRING
        return ring[slot], ('ring', slot)

    def mm(out_ap, pairs, reads, wkey):
        def f(e):
            last = None
            n = len(pairs)
            for i, (a, b) in enumerate(pairs):
                last = e.matmul(out_ap, a, b, start=(i == 0), stop=(i == n - 1))
            return last
        S.op('pe', f, reads=reads, writes=[wkey])

    def act(out, in_, func, reads, writes, **kw):
        S.op('act', lambda e: e.activation(out, in_, func, **kw), reads=reads, writes=writes)

    def dve_tt(out, a, b, op, reads, writes, eng='dve'):
        S.op(eng, lambda e: e.tensor_tensor(out, a, b, op), reads=reads, writes=writes)

    def dve_ts(out, a, s1, s2, op0, op1, reads, writes, eng='dve'):
        if op1 is None:
            S.op(eng, lambda e: e.tensor_scalar(out, a, s1, None, op0), reads=reads, writes=writes)
        else:
            S.op(eng, lambda e: e.tensor_scalar(out, a, s1, s2, op0, op1), reads=reads, writes=writes)

    def dve_stt(out, a, sc_, b, op0, op1, reads, writes):
        S.op('dve', lambda e: e.scalar_tensor_tensor(out, a, sc_, b, op0, op1), reads=reads, writes=writes)

    def dve_cp(out, a, reads, writes, eng='dve'):
        S.op(eng, lambda e: e.tensor_copy(out, a), reads=reads, writes=writes)

    MUL, ADD, SUB = ALU.mult, ALU.add, ALU.subtract

    def rmsnorm(ctiles, gsc, okey):
        for (c0, n) in ctiles:
            ps, pk = nb()
            for k in range(8):
                tb = TB[k % 2]
                act(tb[:, :n], xres[:, k, c0:c0 + n], AF.Square, reads=[('x', k)], writes=[('tb', k % 2)])
                def f(e, k=k, tb=tb, ps=ps, n=n):
                    return e.matmul(ps[:, :n], onesb[:], tb[:, :n], start=(k == 0), stop=(k == 7))
                S.op('pe', f, reads=[('tb', k % 2)] + CONST, writes=[pk] if k == 0 else [])
                S.lastw[pk] = (S.echan['pe'], S.echan['pe'].cnt)
            r = TMP[7]
            act(r[:, :n], ps[:, :n], AF.Sqrt, reads=[pk], writes=['t7'], scale=1.0 / D, bias=epsb[:, 0:1])
            S.op('dve', lambda e, r=r, n=n: e.reciprocal(r[:, :n], r[:, :n]), reads=['t7'], writes=['t7'])
            for k in range(8):
                dve_stt(hbf[:, k, c0:c0 + n], xres[:, k, c0:c0 + n], gsc[:, k:k + 1], r[:, :n], MUL, MUL,
                        reads=[('x', k), 't7'] + CONST, writes=[(okey, k)])

    epsb = sb('epsb', [128, 1], F32)
    oneb = sb('oneb', [128, 1], F32)
    S.op('pool', lambda e: e.memset(epsb[:], EPS), writes=['epsb'])
    S.op('pool', lambda e: e.memset(oneb[:], 1.0), writes=['epsb'])
    CONST.append('epsb')
    S.op('pool', lambda e: e.memset(HST[:], 0.0), writes=['HST'])
    S.op('pool', lambda e: e.memset(SC[:], 0.0), writes=['SC'])
    S.op('pool', lambda e: e.memset(UAh[:], 0.0), writes=['UAh'])
    UAK = [('scr', 16 + k) for k in range(8)]
    S.op('pool', lambda e: e.memset(GLUh[:], 0.0), writes=['GLUh'])

    c_st = S.chan('c_st')
    for l in range(DEPTH):
        first = True
        def sd(dst, src):
            nonlocal first
            S.dma('pool', dst, src, c_st, writes=['stg'] if first else [], allow_slow_non_contiguous=True)
            first = False
        for b in range(NS):
            sd(stA[4 * b:4 * b + 3, :], I['st_lru_conv'][l, b])
            sd(wrepA[4 * b:4 * b + 3, :], I['w_conv_a'][l, 0:3])
            sd(stB[8 * b:8 * b + 7, :, :], I['st_cfm_conv'][l, b, 0:28].rearrange("(kb kk) c -> kb kk c", kk=4))
            sd(stB[8 * b + 7:8 * b + 8, 0:2, :], I['st_cfm_conv'][l, b, 28:30].rearrange("(kb kk) c -> kb kk c", kk=2))
            sd(wrepB[8 * b:8 * b + 7, :, :], I['w_dw_b'][l, 0:28].rearrange("(kb kk) c -> kb kk c", kk=4))
            sd(wrepB[8 * b + 7:8 * b + 8, 0:2, :], I['w_dw_b'][l, 28:30].rearrange("(kb kk) c -> kb kk c", kk=2))
        S.lastw['stg'] = (c_st, c_st.cnt)
        dve_tt(prodA[:, l, :], stA[:], wrepA[:], MUL, ['stg'], ['prodA'])
        for kk in range(4):
            dve_tt(prodB[:, l, kk, :], stB[:, kk, :], wrepB[:, kk, :], MUL, ['stg'], ['prodB'])

    c_s5 = S.chan('c_s5')
    c_tok = S.chan('c_tok')
    c_out2 = S.chan('c_out2')
    NSC = dict(allow_slow_non_contiguous=True)

    def s5_prep(l):
        S.dma('sp', LR[:], I['lam_re'][l].rearrange("(j two) p -> (two p) j", two=2), c_s5, writes=['s5in'], **NSC)
        S.dma('sp', LI[:], I['lam_im'][l].rearrange("(j two) p -> (two p) j", two=2), c_s5, **NSC)
        for h in range(2):
            S.dma('sp', DT[64 * h:64 * h + 64, :], I['log_dt'][l].rearrange("(j two) -> two j", two=2)[h].partition_broadcast(64), c_s5, **NSC)
        S.dma('sp', BRq[:], I['b_ssm_re'][l].rearrange("(j two) p c -> (two p) j c", two=2), c_s5, **NSC)
        S.dma('sp', BIq[:], I['b_ssm_im'][l].rearrange("(j two) p c -> (two p) j c", two=2), c_s5, **NSC)
        for h in range(2):
            for j in range(16):
                S.dma('sp', CRq[64 * h:64 * h + 64, j, :], I['c_ssm_re'][l, 2 * j + h].rearrange("c p -> p c"), c_s5, **NSC)
                S.dma('sp', CIq[64 * h:64 * h + 64, j, :], I['c_ssm_im'][l, 2 * j + h].rearrange("c p -> p c"), c_s5, **NSC)
        S.lastw['s5in'] = (c_s5, c_s5.cnt)
        R = ['s5in']
        W = ['s5p']
        RW = ['s5in', 's5p']
        pump(4)
        act(DT[:], DT[:], AF.Exp, reads=R, writes=W)
        dve_tt(P1[:], LR[:], DT[:], MUL, RW, W)
        dve_tt(P2[:], LI[:], DT[:], MUL, RW, W)
        act(P3[:], P1[:], AF.Exp, reads=RW, writes=W)
        MAGIC = 12582912.0
        for shift, dst in ((0.0, AI), (0.25, AR)):
            dve_ts(P4[:], P2[:], 1.0 / (2 * PI), shift, MUL, ADD, RW, W)
            dve_ts(P5[:], P4[:], MAGIC, None, ADD, None, RW, W)
            dve_ts(P5[:], P5[:], -MAGIC, None, ADD, None, RW, W)
            dve_tt(P4[:], P4[:], P5[:], SUB, RW, W)
            act(P4[:], P4[:], AF.Sin, reads=RW, writes=W, scale=6.283185)
            dve_tt(dst[:], P3[:], P4[:], MUL, RW, W)
        dve_ts(NAI[:], AI[:], -1.0, None, MUL, None, RW, W)
        dve_ts(P1[:], AR[:], -1.0, None, ADD, None, RW, W)
        dve_tt(P2[:], LR[:], LR[:], MUL, RW, W)
        dve_tt(P3[:], LI[:], LI[:], MUL, RW, W)
        dve_tt(P2[:], P2[:], P3[:], ADD, RW, W)
        S.op('dve', lambda e: e.reciprocal(P2[:], P2[:]), reads=RW, writes=W)
        dve_tt(P3[:], P1[:], LR[:], MUL, RW, W)
        dve_tt(P4[:], AI[:], LI[:], MUL, RW, W)
        dve_tt(P3[:], P3[:], P4[:], ADD, RW, W)
        dve_tt(QR[:], P3[:], P2[:], MUL, RW, W)
        dve_tt(P3[:], AI[:], LR[:], MUL, RW, W)
        dve_tt(P4[:], P1[:], LI[:], MUL, RW, W)
        dve_tt(P3[:], P3[:], P4[:], SUB, RW, W)
        dve_tt(QI[:], P3[:], P2[:], MUL, RW, W)
        pump(4)
        dve_cp(AR8[:], AR[:], RW, W)
        dve_cp(AI8[:], AI[:], RW, W)
        for _ in range(3):
            dve_tt(P1[:], AR8[:], AR8[:], MUL, RW, W)
            dve_tt(P2[:], AI8[:], AI8[:], MUL, RW, W)
            dve_tt(P3[:], AR8[:], AI8[:], MUL, RW, W)
            dve_tt(AR8[:], P1[:], P2[:], SUB, RW, W)
            dve_ts(AI8[:], P3[:], 2.0, None, MUL, None, RW, W)
        QRb = QR[:].unsqueeze(2).broadcast_to([128, 16, 16])
        QIb = QI[:].unsqueeze(2).broadcast_to([128, 16, 16])
        dve_tt(BBR[:], BRq[:], QRb, MUL, RW, W)
        dve_tt(BT1[:], BIq[:], QIb, MUL, RW, W)
        dve_tt(BBR[:], BBR[:], BT1[:], SUB, RW, W)
        dve_tt(BBI[:], BIq[:], QRb, MUL, RW, W)
        dve_tt(BT1[:], BRq[:], QIb, MUL, RW, W)
        dve_tt(BBI[:], BBI[:], BT1[:], ADD, RW, W)
        pump(4)
        S.op('pool', lambda e: e.memset(BQpad, 0.0), reads=[], writes=BQK)
        S.op('pool', lambda e: e.memset(CQpad[:, :, :, :], 0.0), reads=[], writes=['CQ'])
        for h in range(2):
            for jm in range(4):
                co = 32 * jm + 16 * h
                ps_ = slice(64 * h, 64 * h + 64)
                dve_cp(BQpad[ps_, 0, jm::4, co:co + 16], BBR[ps_, jm::4, :], RW + BQK, BQK)
                dve_cp(BQpad[ps_, 1, jm::4, co:co + 16], BBI[ps_, jm::4, :], RW + BQK, BQK)
                dve_cp(CQpad[ps_, jm::4, 0, co:co + 16], CRq[ps_, jm::4, :], RW + ['CQ'], ['CQ'])
                dve_ts(CQpad[ps_, jm::4, 1, co:co + 16], CIq[ps_, jm::4, :], -1.0, None, MUL, None, RW + ['CQ'], ['CQ'])
        for c8 in range(4):
            for ri in range(2):
                def f(e, c8=c8, ri=ri):
                    last = None
                    for jm in range(4):
                        last = e.transpose(pbf[:, jm * 128:(jm + 1) * 128], BQpad[:, ri, 4 * c8 + jm, :], identb[:])
                    return last
                S.op('pe', f, reads=BQK + CONST, writes=['pbf'])
                act(BT[:, c8, ri, :], pbf[:, 0:512], AF.Copy, reads=['pbf'], writes=['BT'])
        S.dma('act', wsc[l, SIDX['BT'], :, 0:4096], BT[:, :, :, :].rearrange("p a b c -> p (a b c)"), c_wtB, reads=['BT'])
        S.dma('act', wsc[l, SIDX['CQ'], :, 0:4096], CQpad[:, :, :, :].rearrange("p a b c -> p (a b c)"), c_wtQ, reads=['CQ'])
        for i_, t_ in enumerate([AR, AI, NAI, AR8, AI8]):
            dve_cp(COEF[:, l, i_, :], t_[:], RW, ['COEF'])
        dve_cp(CO2[:, l, 0, 0:16], AR8[:], RW, ['COEF'])
        dve_cp(CO2[:, l, 0, 16:32], AR8[:], RW, ['COEF'])
        dve_ts(CO2[:, l, 1, 0:16], AI8[:], -1.0, None, MUL, None, RW, ['COEF'])
        dve_cp(CO2[:, l, 1, 16:32], AI8[:], RW, ['COEF'])

    MATS = {'w_in': (1024, IN_W), 'w_glu_c': (512, 512), 'w_pa': (1024, 1024), 'w_pb': (512, 1024), 'w_pc': (512, 1024),
            'w_out': (1024, 1024), 'w_ffn_in': (1024, 2 * D_FF), 'w_ffn_out': (D_FF, 1024)}
    cvt = {'i': 0}

    STGK = [('stgf', i) for i in range(NSTG)] + [('stgb', i) for i in range(NSTG)]

    def convert_gen(l):
        index = {m: [] for m in MATS}
        for si, sp_ in enumerate(LSL):
            if isinstance(sp_, list):
                for (off, KC, W, m, col0) in sp_:
                    index[m].append((si, off, KC, W, col0))
        blocks = []
        for m, (K, N) in MATS.items():
            for k in range(K // 128):
                for a in range(0, N, CBK):
                    blocks.append((m, k, a, min(N, a + CBK)))
        nblk_ = len(blocks)
        base = cvt['i']
        cvt['i'] += nblk_

        def load(bi):
            m, k, a, b = blocks[bi]
            i = (base + bi) % NSTG
            S.dma('sp', stgf[i][:, 0:b - a], I[m][l][k * 128:(k + 1) * 128, a:b], c_stg[i], writes=[('stgf', i)])

        PF = NSTG - 1
        for bi in range(min(PF, nblk_)):
            load(bi)
        for bi in range(nblk_):
            if bi + PF < nblk_:
                load(bi + PF)
            m, k, a, b = blocks[bi]
            i = (base + bi) % NSTG
            dve_cp(stgb[i][:, 0:b - a], stgf[i][:, 0:b - a], [('stgf', i)], [('stgb', i)])
            for (si, off, KC, W, col0) in index[m]:
                if k >= KC:
                    continue
                lo, hi = max(a, col0), min(b, col0 + W)
                if lo >= hi:
                    continue
                d0 = off + k * W + (lo - col0)
                S.dma('sp', wsc[l, si, :, d0:d0 + (hi - lo)], stgb[i][:, lo - a:hi - a], c_ws[i], reads=[('stgb', i)])
            yield

    cgen = {'g': None}

    def pump(n):
        g = cgen['g']
        if g is None:
            return
        for _ in range(n):
            try:
                next(g)
            except StopIteration:
                cgen['g'] = None
                return

    def convert(l):
        cgen['g'] = convert_gen(l)
        pump(1 << 30)

    def tables(l, last_of=None):
        for e_ in range(8):
            for k in range(4):
                o_ = (e_ * 4 + k) * 128
                S.op('dve', lambda e, e_=e_, k=k, o_=o_: e.tensor_scalar(dstg[:, o_:o_ + 128], identb[:], wca[:, l, k, e_:e_ + 1], None, MUL),
                     reads=CONST, writes=BQK if ((e_ == 0 and k == 0) or (e_ == 7 and k == 3)) else [])
        S.dma('act', wsc[l, SIDX['diagA'], :, 0:4096], dstg, c_wtD, reads=BQK)
        for c in range(4):
            for k in range(31):
                S.op('dve', lambda e, c=c, k=k: e.tensor_scalar(dstg[:, k * 128:(k + 1) * 128], identb[:], wdb[:, l, k, c:c + 1], None, MUL),
                     reads=CONST, writes=BQK if k in (0, 30) else [])
            S.dma('act', wsc[l, SIDX[('diagB', c)], :, 0:31 * 128], dstg[:, 0:31 * 128], c_wtD, reads=BQK)
            pump(3)
        s5_prep(l)
        RWp = ['s5p', 'PW']
        S.op('dve', lambda e: e.memset(PWR[:, 0, :], 1.0), reads=[], writes=['PW'])
        S.op('dve', lambda e: e.memset(PWI[:, 0, :], 0.0), reads=['PW'], writes=['PW'])
        for m in range(1, 9):
            dve_tt(P1[:], PWR[:, m - 1, :], AR[:], MUL, RWp, ['s5p'])
            dve_tt(P2[:], PWI[:, m - 1, :], AI[:], MUL, RWp, ['s5p'])
            dve_tt(PWR[:, m, :], P1[:], P2[:], SUB, RWp, ['PW'])
            dve_tt(P1[:], PWR[:, m - 1, :], AI[:], MUL, RWp, ['s5p'])
            dve_tt(P2[:], PWI[:, m - 1, :], AR[:], MUL, RWp, ['s5p'])
            dve_tt(PWI[:, m, :], P1[:], P2[:], ADD, RWp, ['PW'])
        dve_ts(NPW[:, :, :], PWI[:, :, :], -1.0, None, MUL, None, RWp, ['PW'])
        BTflat = BT[:, :, :, :].rearrange("p a b c -> p (a b c)")
        bi_ = 0
        for c8 in range(4):
            for hf in range(2):
                for jm2 in range(2):
                    j = 4 * c8 + 2 * hf + jm2
                    pbs = [nb(), nb()]
                    for m in range(8):
                        pw = 7 - m
                        ww = WTw[bi_ % 2]
                        kw = ('WTw', bi_ % 2)
                        S.op('act', lambda e, ww=ww, j=j, pw=pw: e.activation(ww[:, :].rearrange("p (a b) -> p a b", a=2), BQpad[:, :, j, :], AF.Copy, scale=PWR[:, pw, j:j + 1]),
                             reads=BQK + ['PW'], writes=[kw])
                        for ri in range(2):
                            w2 = WT4[(2 * bi_ + ri) % 4]
                            k2 = ('WT4', (2 * bi_ + ri) % 4)
                            if ri == 0:
                                b_, sb_ = BQpad[:, 1, j, :], NPW[:, pw, j:j + 1]
                            else:
                                b_, sb_ = BQpad[:, 0, j, :], PWI[:, pw, j:j + 1]
                            S.op('dve', lambda e, w2=w2, b_=b_, sb_=sb_, ww=ww, ri=ri: e.scalar_tensor_tensor(w2[:], b_, sb_, ww[:, ri * 128:(ri + 1) * 128], MUL, ADD),
                                 reads=BQK + ['PW', kw], writes=[k2])
                            pb_, pkb = pbs[ri]
                            S.op('pe', lambda e, w2=w2, m=m, pb_=pb_: e.transpose(pb_[:, :].bitcast(BF16)[:, m * 128:(m + 1) * 128], w2[:], identb[:]),
                                 reads=[k2] + CONST, writes=[pkb] if m == 0 else [])
                            S.lastw[pkb] = (S.echan['pe'], S.echan['pe'].cnt)
                        bi_ += 1
                    for ri in range(2):
                        pb_, pkb = pbs[ri]
                        o_ = (jm2 * 2 + ri) * 1024
                        act(BTflat[:, o_:o_ + 1024], pb_[:, :].bitcast(BF16), AF.Copy, reads=[pkb], writes=['BT'])
                    pump(4)
                S.dma('act', wsc[l, SIDX[('WZ', c8 * 2 + hf)], :, 0:4096], BTflat, c_wtB, reads=['BT'])
        Kstg = hbf[:, :, :].rearrange("p a b -> p (a b)")[:, 0:4096]
        for c8 in range(4):
            kb = [nb(), nb()]
            def kgroup(m, rhs_fn, c8=c8, kb=kb):
                ps_, pk_ = kb[m // 4]
                def f(e):
                    last = None
                    i_ = 0
                    for jm in range(4):
                        for ri in range(2):
                            last = e.matmul(ps_[:, (m % 4) * 128:(m % 4 + 1) * 128], BQpad[:, ri, 4 * c8 + jm, :], rhs_fn(jm, ri), start=(i_ == 0), stop=(i_ == 7))
                            i_ += 1
                    return last
                S.op('pe', f, reads=BQK + ['CQ', 'BT'], writes=[pk_] if m % 4 == 0 else [])
                S.lastw[pk_] = (S.echan['pe'], S.echan['pe'].cnt)
            kgroup(0, lambda jm, ri, c8=c8: CQpad[:, 4 * c8 + jm, ri, :])
            for ph in range(2):
                for jm in range(4):
                    j = 4 * c8 + jm
                    for pwi in range(4):
                        pw = 4 * ph + pwi + 1
                        ww = WTw[bi_ % 2]
                        k1 = ('WTw', bi_ % 2)
                        bi_ += 1
                        S.op('act', lambda e, ww=ww, j=j, pw=pw: e.activation(ww[:], CQpad[:, j, :, :].rearrange("p a b -> p (a b)"), AF.Copy, scale=PWR[:, pw, j:j + 1]),
                             reads=['CQ', 'PW'], writes=[k1])
                        for ri in range(2):
                            o_ = ((jm * 2 + ri) * 4 + pwi) * 128
                            w1 = ww[:, ri * 128:(ri + 1) * 128]
                            if ri == 0:
                                b_, sb_ = CQpad[:, j, 1, :], PWI[:, pw, j:j + 1]
                            else:
                                b_, sb_ = CQpad[:, j, 0, :], NPW[:, pw, j:j + 1]
                            S.op('dve', lambda e, w1=w1, b_=b_, sb_=sb_, o_=o_: e.scalar_tensor_tensor(BTflat[:, o_:o_ + 128], b_, sb_, w1, MUL, ADD),
                                 reads=['CQ', 'PW', k1], writes=['BT'] if ((jm == 0 and pwi == 0 and ri == 0) or (jm == 3 and pwi == 3 and ri == 1)) else [])
                for pwi in range(4):
                    m = 4 * ph + pwi + 1
                    if m <= 7:
                        kgroup(m, lambda jm, ri, pwi=pwi: BTflat[:, ((jm * 2 + ri) * 4 + pwi) * 128:((jm * 2 + ri) * 4 + pwi + 1) * 128])
                S.dma('act', wsc[l, SIDX[('CA', c8 * 2 + ph)], :, 0:4096], BTflat, c_wtB, reads=['BT'])
                pump(8)
            for hb in range(2):
                ps_, pk_ = kb[hb]
                act(Kstg[:, (c8 * 8 + hb * 4) * 128:(c8 * 8 + hb * 4 + 4) * 128], ps_[:, :], AF.Copy, reads=[pk_], writes=['Kstg'])
            for ph in range(2):
                S.dma('act', wsc[l, SIDX[('CA', c8 * 2 + ph)], :, 4096:5120], Kstg[:, c8 * 1024:(c8 + 1) * 1024], c_wtK, reads=['Kstg'])
        S.barrier_keys(['Kstg'], HK)
        pump(1 << 30)
        for i_, ch_ in enumerate(WTCH + c_ws):
            S.lastw[('wscL', l, i_)] = (ch_, ch_.cnt)
            S.readers[('wscL', l, i_)] = {}

    def x_load(p):
        for tt in range(4):
            t0 = p * PT + tt * 128
            S.dma('sp', tokbuf[:], I['xp'][t0:t0 + 128, :], c_in, writes=['tokbuf'])
            for half in range(2):
                ps, pk = nb()
                def f(e, ps=ps, half=half):
                    last = None
                    for kk in range(4):
                        k = half * 4 + kk
                        last = e.transpose(ps[:, kk * 128:(kk + 1) * 128], tokbuf[:, k * 128:(k + 1) * 128], identf[:])
                    return last
                S.op('pe', f, reads=['tokbuf'] + CONST, writes=[pk])
                act(xres[:, half * 4:half * 4 + 4, tt * 128:(tt + 1) * 128], ps[:, :].rearrange("p (k t) -> p k t", t=128), AF.Copy,
                    reads=[pk], writes=[('x', half * 4 + kk) for kk in range(4)])
        if p == 0:
            S.dma('sp', tokbuf[0:NS, :], I['xs'], c_in, writes=['tokbuf'])
            for half in range(2):
                ps, pk = nb()
                def f(e, ps=ps, half=half):
                    last = None
                    for kk in range(4):
                        k = half * 4 + kk
                        last = e.transpose(ps[:, kk * NS:(kk + 1) * NS], tokbuf[0:NS, k * 128:(k + 1) * 128], identf[0:NS, 0:NS])
                    return last
                S.op('pe', f, reads=['tokbuf'] + CONST, writes=[pk])
                act(xres[:, half * 4:half * 4 + 4, PT:PT + NS], ps[:, 0:4 * NS].rearrange("p (k t) -> p k t", t=NS), AF.Copy,
                    reads=[pk], writes=[('x', half * 4 + kk) for kk in range(4)])

    HK = [('h', k) for k in range(8)]

    def layer(p, l):
        last = (p == NPASS - 1)
        ctiles = [(0, PT)] + ([(PT, NS)] if p == 0 else [])
        NC = PT + (NS if p == 0 else 0)
        rmsnorm(ctiles, gmix[:, l, :], 'h')
        dve_cp(UA[:, :, 0:3], UAh[:, l, :, :], ['UAh'], UAK, eng='pool')
        dve_cp(GLU[:, :, 0:30], GLUh[:, l, :, :], ['GLUh'], ['GLU'], eng='pool')
        for s_ in range(5):
            slab, rk = get_slab()
            for c4 in range(4):
                e_ = s_ * 4 + c4
                for (c0, n) in ctiles:
                    ps, pk = nb()
                    mm(ps[:, :n], [(slab[:, k * 512 + c4 * 128:k * 512 + c4 * 128 + 128], hbf[:, k, c0:c0 + n]) for k in range(8)],
                       reads=[rk] + HK, wkey=pk)
                    if e_ < 8:
                        act(UA[:, e_, 3 + c0:3 + c0 + n], ps[:, :n], AF.Copy, reads=[pk], writes=[('scr', 16 + e_)])
                        if c0 == PT:
                            act(keepAs[:, l, e_, :], ps[:, :n], AF.Copy, reads=[pk], writes=['keepAs'])
                        elif last:
                            act(keepA[:, l, e_, :], ps[:, PT - 3:PT], AF.Copy, reads=[pk], writes=['keepA'])
                    elif e_ < 12:
                        act(TMP[e_ - 8][:, c0:c0 + n], ps[:, :n], AF.Copy, reads=[pk], writes=[('t', e_ - 8)])
                    elif e_ < 16:
                        c = e_ - 12
                        act(TMP[4][:, c0:c0 + n], ps[:, :n], AF.Sigmoid, reads=[pk], writes=[('t', 4)])
                        dve_tt(GLU[:, c, 30 + c0:30 + c0 + n], TMP[c][:, c0:c0 + n], TMP[4][:, c0:c0 + n], MUL,
                               [('t', c), ('t', 4)], ['GLU'])
                        if c0 == PT:
                            dve_tt(keepBs[:, l, c, :], TMP[c][:, c0:c0 + n], TMP[4][:, c0:c0 + n], MUL, [('t', c), ('t', 4)], ['keepBs'])
                        elif last:
                            dve_tt(keepB[:, l, c, :], TMP[c][:, PT - 30:PT], TMP[4][:, PT - 30:PT], MUL, [('t', c), ('t', 4)], ['keepB'])
                    else:
                        act(UC[:, e_ - 16, c0:c0 + n], ps[:, :n], AF.Copy, reads=[pk], writes=['UC'])
        dve_cp(UAh[:, l, :, :], UA[:, :, PT:PT + 3], UAK, ['UAh'], eng='pool')
        dve_cp(GLUh[:, l, :, :], GLU[:, :, PT:PT + 30], ['GLU'], ['GLUh'], eng='pool')
        BTs = lambda c8, ri, jm: slabT[:, (c8 * 2 + ri) * 512 + jm * 128:(c8 * 2 + ri) * 512 + (jm + 1) * 128]
        CQs = lambda j, ri: slabQ[:, (j * 2 + ri) * 128:(j * 2 + ri + 1) * 128]
        cAR = lambda j: COEF[:, l, 0, j:j + 1]
        cAI = lambda j: COEF[:, l, 1, j:j + 1]
        cNAI = lambda j: COEF[:, l, 2, j:j + 1]
        AR8l = COEF[:, l, 3, :]
        AI8l = COEF[:, l, 4, :]
        SP_ = ['SPr', 'SPi']
        dve_cp(SP2[:, 0, :], SC[:, l, :, :].rearrange("p a b -> p (a b)"), ['SC'], ['SPr', 'SPi', ('S2', 0)])

        def xcompute(j):
            c8, jm = j // 4, j % 4
            psr, pkr = nb()
            mm(psr[:, :PT], [(BTs(c8, 0, jm), UC[:, c8, 0:PT])], reads=[rkT, 'UC'], wkey=pkr)
            psi, pki = nb()
            mm(psi[:, :PT], [(BTs(c8, 1, jm), UC[:, c8, 0:PT])], reads=[rkT, 'UC'], wkey=pki)
            act(XR[:, :PT], psr[:, :PT], AF.Copy, reads=[pkr], writes=XRK)
            act(XI[:, :PT], psi[:, :PT], AF.Copy, reads=[pki], writes=XIK)

        def horner(j):
            C_ = ['COEF']
            for jj in range(1, LCH):
                cr, ci = XR[:, jj:PT:LCH], XI[:, jj:PT:LCH]
                pr, pi_ = XR[:, jj - 1:PT:LCH], XI[:, jj - 1:PT:LCH]
                kr, ki, kpr, kpi = ('XR', jj), ('XI', jj), ('XR', jj - 1), ('XI', jj - 1)
                dve_stt(cr, pr, cAR(j), cr, MUL, ADD, C_ + [kpr, kr], [kr])
                dve_stt(ci, pi_, cAR(j), ci, MUL, ADD, C_ + [kpi, ki], [ki])
                dve_stt(cr, pi_, cNAI(j), cr, MUL, ADD, C_ + [kpi, kr], [kr])
                dve_stt(ci, pr, cAI(j), ci, MUL, ADD, C_ + [kpr, ki], [ki])

        for c8 in range(4):
            for hf in range(2):
                slabW, rkW = get_slab()
                for jm2 in range(2):
                    j = 4 * c8 + 2 * hf + jm2
                    for ri in range(2):
                        ps, pk = nb()
                        bb = (jm2 * 2 + ri) * 8
                        mm(ps[:, :NCH], [(slabW[:, (bb + m) * 128:(bb + m + 1) * 128], UC[:, c8, m:PT:LCH]) for m in range(LCH)],
                           reads=[rkW, 'UC'], wkey=pk)
                        act(Z2[:, :, ri * 16 + j], ps[:, :NCH], AF.Copy, reads=[pk], writes=(ZRK if ri == 0 else ZIK))
        def chunk_gen():
            CA2, CB2 = CO2[:, l, 0, :], CO2[:, l, 1, :]
            for c in range(NCH):
                cur = SP2[:, c, :]
                kc = ('S2', c)
                fin = (c == NCH - 1)
                dve_tt(QT[:], cur, CA2, MUL, ['COEF', kc], ['QT'])
                dve_tt(QU[:, 0:16], SP2[:, c, 16:32], CB2[:, 0:16], MUL, ['COEF', kc], ['QU0'])
                dve_tt(QU[:, 16:32], SP2[:, c, 0:16], CB2[:, 16:32], MUL, ['COEF', kc], ['QU1'])
                dve_tt(QW[:], QT[:], Z2[:, c, :], ADD, ['QT'] + ZRK + ZIK, ['QW'])
                dve_tt(SP2[:, c + 1, :], QW[:], QU[:], ADD, ['QW', 'QU0', 'QU1'], [('S2', c + 1)] + (['SPr', 'SPi'] if fin else []))
                yield
        cg = chunk_gen()

        def cpump(n):
            for _ in range(n):
                try:
                    next(cg)
                except StopIteration:
                    return
        act(sca[:], lama[:, l, :], AF.Exp, reads=CONST, writes=['sca'], scale=-1.0)
        act(sca[:], sca[:], AF.Ln, reads=['sca'], writes=['sca'], bias=oneb[:, 0:1])
        dve_ts(sca[:], sca[:], -8.0, None, MUL, None, ['sca'], ['sca'])
        slabA, rkA = get_slab()
        for e_ in range(8):
            dA = lambda k, e_=e_: slabA[:, (e_ * 4 + k) * 128:(e_ * 4 + k + 1) * 128]
            si_ = e_ % 2
            T0, T1, T2, T3 = TMP[4 * si_], TMP[4 * si_ + 1], TMP[4 * si_ + 2], TMP[4 * si_ + 3]
            k0, k1_, k2_, k3_ = ('t', 4 * si_), ('t', 4 * si_ + 1), ('t', 4 * si_ + 2), ('t', 4 * si_ + 3)
            TBe = TB[2] if si_ == 0 else TB[0]
            kb_ = ('tb', 2) if si_ == 0 else ('tb', 0)
            for (c0, n) in ctiles:
                ps, pk = nb()
                if c0 == 0:
                    mm(ps[:, :n], [(dA(k), UA[:, e_, k:k + n]) for k in range(4)], reads=[rkA, ('scr', 16 + e_)], wkey=pk)
                else:
                    mm(ps[:, :n], [(dA(3), UA[:, e_, 3 + c0:3 + c0 + n]),
                                   (prodA[:, l, e_ * 128:(e_ + 1) * 128], sel4[:, :])], reads=[rkA, ('scr', 16 + e_), 'prodA'] + CONST, wkey=pk)
                act(TBe[:, c0:c0 + n], ps[:, :n], AF.Identity, reads=[pk] + CONST, writes=[kb_], bias=bca[:, l, e_:e_ + 1])
                act(T0[:, c0:c0 + n], ps[:, :n], AF.Identity, reads=[pk] + CONST, writes=[k0], bias=bca[:, l, e_:e_ + 1])
                ps2, pk2 = nb()
                mm(ps2[:, :n], [(wrg[:, l, e_, :], TBe[:, c0:c0 + n])], reads=[kb_] + CONST, wkey=pk2)
                ps3, pk3 = nb()
                mm(ps3[:, :n], [(wig[:, l, e_, :], TBe[:, c0:c0 + n])], reads=[kb_] + CONST, wkey=pk3)
                act(T1[:, c0:c0 + n], ps2[:, :n], AF.Sigmoid, reads=[pk2] + CONST, writes=[k1_], bias=brg[:, l, e_:e_ + 1])
                act(T2[:, c0:c0 + n], ps3[:, :n], AF.Sigmoid, reads=[pk3] + CONST, writes=[k2_], bias=big[:, l, e_:e_ + 1])
            act(T1[:, :NC], T1[:, :NC], AF.Exp, reads=[k1_, 'sca'], writes=[k1_], scale=sca[:, e_:e_ + 1])
            dve_tt(T2[:, :NC], T2[:, :NC], T0[:, :NC], MUL, [k2_, k0], [k2_])
            dve_tt(T3[:, :NC], T1[:, :NC], T1[:, :NC], MUL, [k1_], [k3_])
            act(T3[:, :NC], T3[:, :NC], AF.Sqrt, reads=[k3_] + CONST, writes=[k3_], scale=-1.0, bias=oneb[:, 0:1])
            dve_tt(T2[:, :NC], T2[:, :NC], T3[:, :NC], MUL, [k2_, k3_], [k2_])
            S.op('dve', lambda e, e_=e_, T0=T0, T1=T1, T2=T2: e.tensor_tensor_scan(T0[:, 0:PT], T1[:, 0:PT], T2[:, 0:PT], HST[:, l, e_:e_ + 1], MUL, ADD),
                 reads=[k1_, k2_, 'HST', k0], writes=[k0])
            dve_cp(HST[:, l, e_:e_ + 1], T0[:, PT - 1:PT], [k0], ['HST'])
            if p == 0:
                dve_tt(T0[:, PT:NC], T1[:, PT:NC], H0[:, l, e_, :], MUL, [k1_, k0] + CONST, [k0])
                dve_tt(T0[:, PT:NC], T0[:, PT:NC], T2[:, PT:NC], ADD, [k0, k2_], [k0])
                dve_cp(keepH[:, l, e_, :], T0[:, PT:NC], [k0], ['keepH'])
            dve_cp(scr[:, e_, :NC], T0[:, :NC], [k0], [('scr', e_)], eng='pool')
            cpump(NCH // 8)
        for c in range(4):
            slabB, rkB = get_slab()
            dB = lambda k: slabB[:, k * 128:(k + 1) * 128]
            for (c0, n) in ctiles:
                ps, pk = nb()
                if c0 == 0:
                    mm(ps[:, :n], [(dB(k), GLU[:, c, k:k + n]) for k in range(31)], reads=[rkB, 'GLU'], wkey=pk)
                else:
                    mm(ps[:, :n], [(dB(30), GLU[:, c, 30 + c0:30 + c0 + n])] +
                       [(prodB[:, l, kk, c * 128:(c + 1) * 128], sel8[:, :]) for kk in range(4)], reads=[rkB, 'GLU', 'prodB'] + CONST, wkey=pk)
                act(TMP[c][:, c0:c0 + n], ps[:, :n], AF.Identity, reads=[pk] + CONST, writes=[('t', c)], bias=bdb[:, l, c:c + 1])
                if c0 == 0:
                    act(TB[0][:, :n], ps[:, :n], AF.Identity, reads=[pk] + CONST, writes=[('tb', 0)], bias=bdb[:, l, c:c + 1])
                    act(TB[1][:, :n], ps[:, :n], AF.Square, reads=[pk] + CONST, writes=[('tb', 1)], bias=bdb[:, l, c:c + 1])
                    S.op('pe', lambda e, c=c, n=n: e.matmul(pst1[:, :n], onesb[:], TB[0][:, :n], start=(c == 0), stop=(c == 3)),
                         reads=[('tb', 0)] + CONST, writes=['pst1'] if c == 0 else [])
                    S.lastw['pst1'] = (S.echan['pe'], S.echan['pe'].cnt)
                    S.op('pe', lambda e, c=c, n=n: e.matmul(pst2[:, :n], onesb[:], TB[1][:, :n], start=(c == 0), stop=(c == 3)),
                         reads=[('tb', 1)] + CONST, writes=['pst2'] if c == 0 else [])
                    S.lastw['pst2'] = (S.echan['pe'], S.echan['pe'].cnt)
                else:
                    act(CBs[:, c, :], ps[:, :n], AF.Identity, reads=[pk] + CONST, writes=['CBs'], bias=bdb[:, l, c:c + 1])
                    act(CB2s[:, c, :], ps[:, :n], AF.Square, reads=[pk] + CONST, writes=['CBs'], bias=bdb[:, l, c:c + 1])
        for (c0, n) in ctiles:
            if c0 == 0:
                s1, k1, s2, k2 = pst1, 'pst1', pst2, 'pst2'
            else:
                s1, k1 = nb()
                mm(s1[:, :n], [(onesb[:], CBs[:, c, :]) for c in range(4)], reads=['CBs'] + CONST, wkey=k1)
                s2, k2 = nb()
                mm(s2[:, :n], [(onesb[:], CB2s[:, c, :]) for c in range(4)], reads=['CBs'] + CONST, wkey=k2)
            act(TMP[4][:, :n], s1[:, :n], AF.Copy, reads=[k1], writes=[('t', 4)], scale=1.0 / W_B)
            dve_tt(TMP[5][:, :n], TMP[4][:, :n], TMP[4][:, :n], MUL, [('t', 4)], [('t', 5)])
            dve_stt(TMP[5][:, :n], s2[:, :n], 1.0 / W_B, TMP[5][:, :n], MUL, SUB, [k2, ('t', 5)], [('t', 5)])
            act(TMP[5][:, :n], TMP[5][:, :n], AF.Sqrt, reads=[('t', 5)] + CONST, writes=[('t', 5)], bias=epsb[:, 0:1])
            S.op('dve', lambda e, n=n: e.reciprocal(TMP[5][:, :n], TMP[5][:, :n]), reads=[('t', 5)], writes=[('t', 5)])
            for c in range(4):
                dve_tt(TMP[6][:, :n], TMP[c][:, c0:c0 + n], TMP[4][:, :n], SUB, [('t', c), ('t', 4)], [('t', 6)])
                dve_tt(TMP[6][:, :n], TMP[6][:, :n], TMP[5][:, :n], MUL, [('t', 6), ('t', 5)], [('t', 6)])
                act(scr[:, 8 + c, c0:c0 + n], TMP[6][:, :n], AF.Silu, reads=[('t', 6)] + CONST, writes=[('scr', 8 + c)],
                    scale=lng[:, l, c:c + 1], bias=lnb[:, l, c:c + 1])
        cpump(NCH)
        dve_cp(SC[:, l, :, :].rearrange("p a b -> p (a b)"), SP2[:, NCH, :], SP_, ['SC'])
        SPb = S5S[:, :, :, :].rearrange("p a b c -> p (a b c)")[:, 0:2 * 16 * NCH].rearrange("p (j r c) -> p j r c", r=2, c=NCH)
        XSb = S5S[:, :, :, :].rearrange("p a b c -> p (a b c)")[:, 2048:2048 + 32 * NS].rearrange("p (j r c) -> p j r c", r=2, c=NS)
        act(SPb[:, :, 0, :], SP2[:, 0:NCH, 0:16].rearrange("p c j -> p j c"), AF.Copy, reads=SP_, writes=BQK)
        act(SPb[:, :, 1, :], SP2[:, 0:NCH, 16:32].rearrange("p c j -> p j c"), AF.Copy, reads=SP_, writes=BQK)
        for c8 in range(4):
            ps, pk = nb()
            for ph in range(2):
                slabC, rkC = get_slab()
                for pwi in range(4):
                    jp = 4 * ph + pwi
                    pairs = [(slabC[:, ((jm * 2 + ri) * 4 + pwi) * 128:((jm * 2 + ri) * 4 + pwi + 1) * 128], SPb[:, 4 * c8 + jm, ri, :])
                             for jm in range(4) for ri in range(2)]
                    pairs += [(slabC[:, 4096 + m * 128:4096 + (m + 1) * 128], UC[:, c8, (jp - m):PT:LCH]) for m in range(jp + 1)]
                    def f(e, pairs=pairs, jp=jp, ps=ps):
                        last = None
                        for i_, (a_, b_) in enumerate(pairs):
                            last = e.matmul(ps[:, jp:PT:LCH], a_, b_, start=(i_ == 0), stop=(i_ == len(pairs) - 1))
                        return last
                    S.op('pe', f, reads=[rkC, 'UC'] + BQK, writes=[pk] if jp == 0 else [])
                    S.lastw[pk] = (S.echan['pe'], S.echan['pe'].cnt)
            dve_stt(TMP[0][:, :PT], UC[:, c8, 0:PT], dss[:, l, c8:c8 + 1], ps[:, :PT], MUL, ADD, ['UC', pk] + CONST, [('t', 0)])
            act(YG[:, c8, 0:PT], TMP[0][:, :PT], AF.Gelu, reads=[('t', 0)], writes=['YG'])
        slabT, rkT = get_slab()
        slabQ, rkQ = get_slab(held=1)
        if p == 0:
            for c8 in range(4):
                for jm in range(4):
                    j = 4 * c8 + jm
                    psr, pkr = nb()
                    mm(psr[:, :NS], [(BTs(c8, 0, jm), UC[:, c8, PT:PT + NS])], reads=[rkT, 'UC'], wkey=pkr)
                    psi, pki = nb()
                    mm(psi[:, :NS], [(BTs(c8, 1, jm), UC[:, c8, PT:PT + NS])], reads=[rkT, 'UC'], wkey=pki)
                    xs_r, xs_i = XSs[:, l, 0, j, :], XSs[:, l, 1, j, :]
                    RS = CONST + ['COEF', 'XSs']
                    dve_stt(xs_r, S0[:, l, 0, j, :], cAR(j), psr[:, :NS], MUL, ADD, RS + [pkr], ['XSs'])
                    dve_stt(xs_r, S0[:, l, 1, j, :], cNAI(j), xs_r, MUL, ADD, RS, ['XSs'])
                    dve_stt(xs_i, S0[:, l, 1, j, :], cAR(j), psi[:, :NS], MUL, ADD, RS + [pki], ['XSs'])
                    dve_stt(xs_i, S0[:, l, 0, j, :], cAI(j), xs_i, MUL, ADD, RS, ['XSs'])
                    act(XSb[:, j, 0, :], xs_r, AF.Copy, reads=['XSs'], writes=['XSb'])
                    act(XSb[:, j, 1, :], xs_i, AF.Copy, reads=['XSs'], writes=['XSb'])
                ps, pk = nb()
                mm(ps[:, :NS], [(CQs(4 * c8 + jm, ri), XSb[:, 4 * c8 + jm, ri, :]) for jm in range(4) for ri in range(2)],
                   reads=[rkQ, 'XSb'], wkey=pk)
                dve_stt(TMP[0][:, :NS], UC[:, c8, PT:PT + NS], dss[:, l, c8:c8 + 1], ps[:, :NS], MUL, ADD, ['UC', pk] + CONST, [('t', 0)])
                act(YG[:, c8, PT:PT + NS], TMP[0][:, :NS], AF.Gelu, reads=[('t', 0)], writes=['YG'])
        slab, rk = get_slab()
        for c in range(4):
            for (c0, n) in ctiles:
                ps, pk = nb()
                mm(ps[:, :n], [(slab[:, k * 512 + c * 128:k * 512 + c * 128 + 128], YG[:, k, c0:c0 + n]) for k in range(4)], reads=[rk, 'YG'], wkey=pk)
                act(TMP[1][:, :n], ps[:, :n], AF.Sigmoid, reads=[pk] + CONST, writes=[('t', 1)], bias=bglu[:, l, c:c + 1])
                dve_tt(scr[:, 12 + c, c0:c0 + n], YG[:, c, c0:c0 + n], TMP[1][:, :n], MUL, ['YG', ('t', 1)], [('scr', 12 + c)])
        for d in range(8):
            slab, rk = get_slab()
            for (c0, n) in ctiles:
                specs = [(0, 3072, 8, 0, 0), (1024, 4096, 4, 8, 8), (2048, 4608, 4, 12, 16)]
                for bi_, (goff, poff, KC, sbase, gb) in enumerate(specs):
                    psg, pkg = nb()
                    mm(psg[:, :n], [(slab[:, goff + k * 128:goff + (k + 1) * 128], hbf[:, k, c0:c0 + n]) for k in range(8)], reads=[rk] + HK, wkey=pkg)
                    gt = TMP[0] if bi_ == 0 else TMP[2]
                    gk = ('t', 0) if bi_ == 0 else ('t', 2)
                    act(gt[:, :n], psg[:, :n], AF.Sigmoid, reads=[pkg] + CONST, writes=[gk], bias=bgate[:, l, gb + d:gb + d + 1])
                    psp, pkp = nb()
                    mm(psp[:, :n], [(slab[:, poff + k * 128:poff + (k + 1) * 128], scr[:, sbase + k, c0:c0 + n]) for k in range(KC)],
                       reads=[rk] + [('scr', sbase + k) for k in range(KC)], wkey=pkp)
                    if bi_ == 0:
                        dve_tt(TMP[1][:, :n], TMP[0][:, :n], psp[:, :n], MUL, [gk, pkp], [('t', 1)])
                    else:
                        dve_tt(TMP[3][:, :n], TMP[2][:, :n], psp[:, :n], MUL, [gk, pkp], [('t', 3)])
                        if bi_ == 1:
                            dve_tt(TMP[1][:, :n], TMP[1][:, :n], TMP[3][:, :n], ADD, [('t', 1), ('t', 3)], [('t', 1)])
                        else:
                            dve_tt(scr[:, 16 + d, c0:c0 + n], TMP[1][:, :n], TMP[3][:, :n], ADD, [('t', 1), ('t', 3)], [('scr', 16 + d)])
        for s_ in range(2):
            slab, rk = get_slab()
            for c4 in range(4):
                d = s_ * 4 + c4
                for (c0, n) in ctiles:
                    ps, pk = nb()
                    mm(ps[:, :n], [(slab[:, k * 512 + c4 * 128:k * 512 + c4 * 128 + 128], scr[:, 16 + k, c0:c0 + n]) for k in range(8)],
                       reads=[rk] + [('scr', 16 + k) for k in range(8)], wkey=pk)
                    dve_tt(xres[:, d, c0:c0 + n], xres[:, d, c0:c0 + n], ps[:, :n], ADD, [('x', d), pk], [('x', d)])
        rmsnorm(ctiles, gffn[:, l, :], 'h')
        for s_ in range(11):
            slab, rk = get_slab()
            for c4 in range(4):
                ci_ = s_ * 4 + c4
                for (c0, n) in ctiles:
                    ps, pk = nb()
                    mm(ps[:, :n], [(slab[:, k * 512 + c4 * 128:k * 512 + c4 * 128 + 128], hbf[:, k, c0:c0 + n]) for k in range(8)],
                       reads=[rk] + HK, wkey=pk)
                    if ci_ < 22:
                        act(scr[:, ci_, c0:c0 + n], ps[:, :n], AF.Silu, reads=[pk], writes=[('scr', ci_)])
                    else:
                        f_ = ci_ - 22
                        dve_tt(scr[:, f_, c0:c0 + n], scr[:, f_, c0:c0 + n], ps[:, :n], MUL, [('scr', f_), pk], [('scr', f_)])
        for d in range(8):
            slab, rk = get_slab()
            for (c0, n) in ctiles:
                ps, pk = nb()
                mm(ps[:, :n], [(slab[:, k * 128:(k + 1) * 128], scr[:, k, c0:c0 + n]) for k in range(22)],
                   reads=[rk] + [('scr', k) for k in range(22)], wkey=pk)
                dve_tt(xres[:, d, c0:c0 + n], xres[:, d, c0:c0 + n], ps[:, :n], ADD, [('x', d), pk], [('x', d)])
        def tr_out(src_fn, nblk, nrows, dst_ap):
            for g0 in range(0, nblk, 4):
                ps, pk = nb()
                gn = min(4, nblk - g0)
                def f(e, ps=ps, g0=g0, gn=gn):
                    last = None
                    for i in range(gn):
                        last = e.transpose(ps[0:nrows, i * 128:(i + 1) * 128], src_fn(g0 + i), identf[:])
                    return last
                S.op('pe', f, reads=KEEPK + CONST, writes=[pk])
                act(tokbuf[0:nrows, g0 * 128:(g0 + gn) * 128], ps[0:nrows, 0:gn * 128], AF.Copy, reads=[pk], writes=['tokbuf'])
            S.dma('sp', dst_ap, tokbuf[0:nrows, 0:nblk * 128], c_tok, reads=['tokbuf'])

        KEEPK = ['keepAs', 'keepH', 'keepBs', 'XSs', 'keepA', 'keepB', 'HST', 'SC']
        if p == 0:
            S.dma('sp', O['s_lru_conv'][l, :, 0:2, :], I['st_lru_conv'][l, :, 1:3, :], c_out)
            S.dma('sp', O['s_cfm_conv'][l, :, 0:29, :], I['st_cfm_conv'][l, :, 1:30, :], c_out)
            tr_out(lambda i: keepAs[:, l, i, :], 8, NS, O['s_lru_conv'][l, :, 2, :])
            tr_out(lambda i: keepH[:, l, i, :], 8, NS, O['s_lru_h'][l, :, :])
            tr_out(lambda i: keepBs[:, l, i, :], 4, NS, O['s_cfm_conv'][l, :, 29, :])
            for ri, nm in enumerate(['s_ssm_re', 's_ssm_im']):
                dstv = O[nm][l].rearrange("b g p -> b (g p)")
                for hf in range(2):
                    tr_out(lambda i, ri=ri, hf=hf: XSs[:, l, ri, hf * 8 + i, :], 8, NS, dstv[:, hf * 1024:(hf + 1) * 1024])
        if last:
            tr_out(lambda i: keepA[:, l, i, :], 8, 3, O['p_lru_conv'][l, :, :])
            tr_out(lambda i: keepB[:, l, i, :], 4, 30, O['p_cfm_conv'][l, :, :])
            tr_out(lambda i: HST[:, l, :], 1, 8, O['p_lru_h'][l].rearrange("(e p) -> e p", p=128))
            tr_out(lambda i: SC[:, l, 0, :], 1, 16, O['p_ssm_re'][l].rearrange("(j two) p -> j (two p)", two=2))
            tr_out(lambda i: SC[:, l, 1, :], 1, 16, O['p_ssm_im'][l].rearrange("(j two) p -> j (two p)", two=2))

    def final_out(p):
        ctiles = [(0, PT)] + ([(PT, NS)] if p == 0 else [])
        for (c0, n) in ctiles:
            ps, pk = nb()
            for k in range(8):
                tb = TB[k % 2]
                act(tb[:, :n], xres[:, k, c0:c0 + n], AF.Square, reads=[('x', k)], writes=[('tb', k % 2)])
                S.op('pe', lambda e, k=k, tb=tb, ps=ps, n=n: e.matmul(ps[:, :n], onesb[:], tb[:, :n], start=(k == 0), stop=(k == 7)),
                     reads=[('tb', k % 2)] + CONST, writes=[pk] if k == 0 else [])
                S.lastw[pk] = (S.echan['pe'], S.echan['pe'].cnt)
            r = TMP[7]
            act(r[:, :n], ps[:, :n], AF.Sqrt, reads=[pk] + CONST, writes=['t7'], scale=1.0 / D, bias=epsb[:, 0:1])
            S.op('dve', lambda e, r=r, n=n: e.reciprocal(r[:, :n], r[:, :n]), reads=['t7'], writes=['t7'])
            for k in range(8):
                dve_stt(xres[:, k, c0:c0 + n], xres[:, k, c0:c0 + n], gfin[:, k:k + 1], r[:, :n], MUL, MUL,
                        reads=[('x', k), 't7'] + CONST, writes=[('x', k)])
        XK = [('x', k) for k in range(8)]
        for tt in range(4):
            for half in range(2):
                ps, pk = nb()
                def f(e, ps=ps, half=half, tt=tt):
                    last = None
                    for kk in range(4):
                        k = half * 4 + kk
                        last = e.transpose(ps[:, kk * 128:(kk + 1) * 128], xres[:, k, tt * 128:(tt + 1) * 128], identf[:])
                    return last
                S.op('pe', f, reads=XK + CONST, writes=[pk])
                act(tokout[:, half * 512:(half + 1) * 512], ps[:, :], AF.Copy, reads=[pk], writes=['tokbuf'])
            t0 = p * PT + tt * 128
            S.dma('sp', O['y_p'][t0:t0 + 128, :], tokout[:], c_tok, reads=['tokbuf'])
        if p == 0:
            for half in range(2):
                ps, pk = nb()
                def f(e, ps=ps, half=half):
                    last = None
                    for kk in range(4):
                        k = half * 4 + kk
                        last = e.transpose(ps[0:NS, kk * 128:(kk + 1) * 128], xres[:, k, PT:PT + NS], identf[:])
                    return last
                S.op('pe', f, reads=XK + CONST, writes=[pk])
                act(tokout[0:NS, half * 512:(half + 1) * 512], ps[0:NS, :], AF.Copy, reads=[pk], writes=['tokbuf'])
            S.dma('sp', O['y_s'][:, :], tokout[0:NS, :], c_tok, reads=['tokbuf'])

    emit_const_loads()

    def tr_in(dst_fn, src_ap, ncols):
        S.dma('sp', tokbuf[0:NS, 0:ncols], src_ap, c_in, writes=['tokbuf'])
        nblk = ncols // 128
        for g0 in range(0, nblk, 4):
            ps, pk = nb()
            gn = min(4, nblk - g0)
            def f(e, ps=ps, g0=g0, gn=gn):
                last = None
                for i in range(gn):
                    last = e.transpose(ps[:, i * NS:(i + 1) * NS], tokbuf[0:NS, (g0 + i) * 128:(g0 + i + 1) * 128], identf[0:NS, 0:NS])
                return last
            S.op('pe', f, reads=['tokbuf'] + CONST, writes=[pk])
            for i in range(gn):
                act(dst_fn(g0 + i), ps[:, i * NS:(i + 1) * NS], AF.Copy, reads=[pk], writes=['H0S0'])

    for l in range(DEPTH):
        tr_in(lambda i, l=l: H0[:, l, i, :], I['st_lru_h'][l], 1024)
        for ri, nm in enumerate(['st_ssm_re', 'st_ssm_im']):
            srcv = I[nm][l].rearrange("b g p -> b (g p)")
            for hf in range(2):
                tr_in(lambda i, l=l, ri=ri, hf=hf: S0[:, l, ri, hf * 8 + i, :], srcv[:, hf * 1024:(hf + 1) * 1024], 1024)
    CONST.append('H0S0')
    cgen['g'] = convert_gen(0)
    tables(0)
    cgen['g'] = convert_gen(1)
    tables(1)
    S.barrier_keys(['BT'], ['ZR', 'ZI'])
    S.barrier_keys(STGK + ['stg'], [('t', i_) for i_ in range(8)] + [('scr', k_) for k_ in range(24)])
    S.barrier_keys(['s5in', 's5p', 'CQ', 'BT', 'pbf'] + BQK, [('x', k_) for k_ in range(8)])
    for p in range(NPASS):
        x_load(p)
        for l in range(DEPTH):
            layer(p, l)
        final_out(p)
    S.finish([c_out, c_tok])
    return nc


_CACHE = {}


def kernel(**inputs):
    n = 8
    consts = {
        'identf': np.eye(128, dtype=np.float32),
        'onesf': np.ones((128, 128), dtype=np.float32),
        'sel4f': np.repeat(np.eye(NS, dtype=np.float32), 4, axis=0),
        'sel8f': np.repeat(np.eye(NS, dtype=np.float32), 8, axis=0),
    }
    in_maps = []
    for i in range(n):
        m = {}
        m['xp'] = np.ascontiguousarray(inputs['x_prompt'][i])
        m['xs'] = np.ascontiguousarray(inputs['x_sample'][NS * i:NS * (i + 1), 0, :])
        m['st_lru_conv'] = np.ascontiguousarray(inputs['state_lru_conv'][:, NS * i:NS * (i + 1)])
        m['st_lru_h'] = np.ascontiguousarray(inputs['state_lru_h'][:, NS * i:NS * (i + 1)])
        m['st_cfm_conv'] = np.ascontiguousarray(inputs['state_cfm_conv'][:, NS * i:NS * (i + 1)])
        m['st_ssm_re'] = np.ascontiguousarray(inputs['state_ssm_re'][:, NS * i:NS * (i + 1)])
        m['st_ssm_im'] = np.ascontiguousarray(inputs['state_ssm_im'][:, NS * i:NS * (i + 1)])
        for k in PARAMS:
            m[k] = np.ascontiguousarray(np.asarray(inputs[k], dtype=np.float32))
        m.update(consts)
        in_maps.append(m)
    shapes = {k: v.shape for k, v in in_maps[0].items()}
    if 'nc' not in _CACHE:
        _CACHE['nc'] = build_program(shapes)
    nc = _CACHE['nc']
    res = run_bass_kernel_spmd(nc, in_maps, core_ids=list(range(n)))
    R = res.results
    cat = lambda name, ax: np.concatenate([np.asarray(r[name]) for r in R], axis=ax)
    y_prompt = np.stack([np.asarray(r['y_p']) for r in R], axis=0)
    y_sample = cat('y_s', 0)[:, None, :]
    outs = [y_prompt, y_sample]
    for nm in ['p_lru_conv', 'p_lru_h', 'p_cfm_conv', 'p_ssm_re', 'p_ssm_im']:
        outs.append(np.stack([np.asarray(r[nm]) for r in R], axis=1))
    for nm in ['s_lru_conv', 's_lru_h', 's_cfm_conv', 's_ssm_re', 's_ssm_im']:
        outs.append(cat(nm, 1))
    return tuple(np.ascontiguousarray(o.astype(np.float32)) for o in outs)
```
